# Optimizing a Trainium2 kernel written in Bass

```python
import jax
import jax.numpy as jnp
from jax import lax
import numpy as np

D_MODEL = 1024
BATCH = 16
SEQ = 2048
DEPTH = 2

CHUNK = 64
MIX_WIDTH = D_MODEL
GROUP_WIDTH = MIX_WIDTH // 4
HEAD_DIM = GROUP_WIDTH // 4
CONV_WIDTH = 4
ROPE_THETA = 10000.0
EPS = 1e-6
RET_HEADS = 4
RET_DK = HEAD_DIM
RET_DV = HEAD_DIM
LRU_WIDTH = GROUP_WIDTH
LRU_BLOCKS = 4
LRU_BLOCK = LRU_WIDTH // LRU_BLOCKS
LRU_C = 8.0
MLA_HEADS = 4
MLA_NOPE = HEAD_DIM
MLA_ROPE = HEAD_DIM // 2
MLA_V = HEAD_DIM
MLA_Q_LORA = 3 * GROUP_WIDTH // 4
MLA_KV_LORA = GROUP_WIDTH // 2
Q_BLOCK = 128
GDN_HEADS = 4
GDN_DK = HEAD_DIM
GDN_DV = HEAD_DIM
GDN_QKV = 2 * GDN_HEADS * GDN_DK + GDN_HEADS * GDN_DV
N_MEM = 256
XA_HEADS = 4
XA_HEAD_DIM = D_MODEL // XA_HEADS
D_FF = 4 * D_MODEL
IN_SIZES = (
    RET_HEADS * RET_DK, RET_HEADS * RET_DK, RET_HEADS * RET_DV, RET_HEADS * RET_DV,
    LRU_WIDTH, LRU_WIDTH,
    MLA_Q_LORA, MLA_KV_LORA, MLA_ROPE,
    GDN_QKV, GDN_HEADS * GDN_DV, GDN_HEADS, GDN_HEADS,
)
IN_COLS = sum(IN_SIZES)

kernel_name = "hymba_style_chunk_causal_hybrid_trunk"

F32 = jnp.float32


def split_cols(z, sizes):
    points = []
    acc = 0
    for s in sizes[:-1]:
        acc += s
        points.append(acc)
    return jnp.split(z, points, axis=-1)


def rmsnorm(x, g):
    xf = x.astype(F32)
    y = xf * lax.rsqrt(jnp.mean(xf * xf, axis=-1, keepdims=True) + EPS)
    return (y * g.astype(F32)).astype(x.dtype)


def l2norm(t):
    return t * lax.rsqrt(jnp.sum(t * t, axis=-1, keepdims=True) + EPS)


def rope(x, pos):
    half = x.shape[-1] // 2
    inv_freq = jnp.power(ROPE_THETA, -jnp.arange(half, dtype=F32) / half)
    ang = pos.astype(F32)[..., None] * inv_freq
    cos = jnp.cos(ang)[:, :, None, :]
    sin = jnp.sin(ang)[:, :, None, :]
    xf = x.astype(F32)
    x1, x2 = xf[..., :half], xf[..., half:]
    return jnp.concatenate([x1 * cos - x2 * sin, x1 * sin + x2 * cos], axis=-1).astype(x.dtype)


def causal_conv(x, w):
    k, c = w.shape
    return lax.conv_general_dilated(
        x, w[:, None, :].astype(x.dtype), window_strides=(1,), padding=[(k - 1, 0)],
        dimension_numbers=("NWC", "WIO", "NWC"), feature_group_count=c)


def retention(q, k, v, gate, pos, gn_gain):
    b, s, _ = q.shape
    n = s // CHUNK
    h = RET_HEADS
    q = rope(q.reshape(b, s, h, RET_DK), pos).astype(F32) * (RET_DK ** -0.5)
    k = rope(k.reshape(b, s, h, RET_DK), pos).astype(F32)
    v = v.reshape(b, s, h, RET_DV).astype(F32)
    log_gamma = jnp.log1p(-jnp.exp2(-5.0 - jnp.arange(h, dtype=F32)))
    idx = jnp.arange(CHUNK, dtype=F32)
    intra_decay = jnp.exp(jnp.abs(idx[:, None] - idx[None, :])[None] * log_gamma[:, None, None])
    xi = jnp.exp((idx[:, None] + 1.0) * log_gamma)
    zeta = jnp.exp((CHUNK - 1.0 - idx)[:, None] * log_gamma)
    chunk_decay = jnp.exp(CHUNK * log_gamma)
    qc = q.reshape(b, n, CHUNK, h, RET_DK)
    kc = k.reshape(b, n, CHUNK, h, RET_DK)
    vc = v.reshape(b, n, CHUNK, h, RET_DV)
    scores = jnp.einsum("bnihd,bnjhd->bnhij", qc, kc) * intra_decay
    intra = jnp.einsum("bnhij,bnjhe->bnihe", scores, vc)

    def step(state, inp):
        q_c, k_c, v_c = inp
        cross = jnp.einsum("bihd,bhde->bihe", q_c, state) * xi[None, :, :, None]
        state = state * chunk_decay[None, :, None, None] + jnp.einsum(
            "bjhd,bjhe->bhde", k_c * zeta[None, :, :, None], v_c)
        return state, cross

    state0 = jnp.zeros((b, h, RET_DK, RET_DV), F32)
    to_chunk_major = lambda t: t.transpose(1, 0, 2, 3, 4)
    _, cross = lax.scan(step, state0, (to_chunk_major(qc), to_chunk_major(kc), to_chunk_major(vc)))
    o = (intra + to_chunk_major(cross)).reshape(b, s, h, RET_DV)
    mu = jnp.mean(o, axis=-1, keepdims=True)
    var = jnp.mean(jnp.square(o - mu), axis=-1, keepdims=True)
    o = (o - mu) * lax.rsqrt(var + EPS) * gn_gain.astype(F32)
    return (o.reshape(b, s, h * RET_DV) * jax.nn.silu(gate.astype(F32))).astype(gate.dtype)


def rg_lru_branch(xb, gb, conv_w, conv_b, wa, ba, wx, bx, lam):
    b, s, w = xb.shape
    xc = (causal_conv(xb, conv_w) + conv_b).astype(F32)
    xr = xc.reshape(b, s, LRU_BLOCKS, LRU_BLOCK)
    r = jax.nn.sigmoid(jnp.einsum("bsnc,ncd->bsnd", xr, wa.astype(F32)).reshape(b, s, w) + ba)
    i = jax.nn.sigmoid(jnp.einsum("bsnc,ncd->bsnd", xr, wx.astype(F32)).reshape(b, s, w) + bx)
    log_a = -LRU_C * r * jax.nn.softplus(-lam.astype(F32))
    a = jnp.exp(log_a)
    u = jnp.sqrt(-jnp.expm1(2.0 * log_a)) * (i * xc)

    def combine(left, right):
        a1, b1 = left
        a2, b2 = right
        return a1 * a2, a2 * b1 + b2

    _, hs = lax.associative_scan(combine, (a, u), axis=1)
    return (hs * jax.nn.gelu(gb.astype(F32))).astype(xb.dtype)


def mla(c_q, c_kv, k_rope_raw, pos, q_norm, w_uq, kv_norm, w_ukv):
    b, s, _ = c_q.shape
    h = MLA_HEADS
    q = (rmsnorm(c_q, q_norm) @ w_uq).reshape(b, s, h, MLA_NOPE + MLA_ROPE)
    q_nope = q[..., :MLA_NOPE]
    q_rope = rope(q[..., MLA_NOPE:], pos)
    kv = (rmsnorm(c_kv, kv_norm) @ w_ukv).reshape(b, s, h, MLA_NOPE + MLA_V)
    k_nope, v = kv[..., :MLA_NOPE], kv[..., MLA_NOPE:]
    k_rope = rope(k_rope_raw[:, :, None, :], pos)[:, :, 0, :]
    scale = (MLA_NOPE + MLA_ROPE) ** -0.5
    nq = s // Q_BLOCK
    qn_blocks = q_nope.reshape(b, nq, Q_BLOCK, h, MLA_NOPE).transpose(1, 0, 2, 3, 4)
    qr_blocks = q_rope.reshape(b, nq, Q_BLOCK, h, MLA_ROPE).transpose(1, 0, 2, 3, 4)
    key_chunk = jnp.arange(s) // CHUNK

    def attend(args):
        qn, qr, blk = args
        sc = (jnp.einsum("bqhd,bkhd->bhqk", qn, k_nope)
              + jnp.einsum("bqhr,bkr->bhqk", qr, k_rope)).astype(F32) * scale
        q_chunk = (blk * Q_BLOCK + jnp.arange(Q_BLOCK)) // CHUNK
        mask = key_chunk[None, :] <= q_chunk[:, None]
        p = jax.nn.softmax(jnp.where(mask, sc, -jnp.inf), axis=-1)
        return jnp.einsum("bhqk,bkhe->bqhe", p.astype(v.dtype), v)

    o = lax.map(attend, (qn_blocks, qr_blocks, jnp.arange(nq)))
    return o.transpose(1, 0, 2, 3, 4).reshape(b, s, h * MLA_V)


def gated_deltanet(qkv, gate, b_raw, a_raw, conv_w, a_log, dt_bias, norm_g):
    b, s, _ = qkv.shape
    h = GDN_HEADS
    n = s // CHUNK
    c = CHUNK
    qkv = jax.nn.silu(causal_conv(qkv, conv_w).astype(F32))
    q, k, v = jnp.split(qkv, [h * GDN_DK, 2 * h * GDN_DK], axis=-1)
    heads = lambda t, d: t.reshape(b, n, c, h, d).transpose(0, 3, 1, 2, 4)
    q = l2norm(heads(q, GDN_DK)) * (GDN_DK ** -0.5)
    k = l2norm(heads(k, GDN_DK))
    v = heads(v, GDN_DV)
    beta = jax.nn.sigmoid(b_raw.astype(F32)).reshape(b, n, c, h).transpose(0, 3, 1, 2)
    g = -jnp.exp(a_log.astype(F32)) * jax.nn.softplus(a_raw.astype(F32) + dt_bias.astype(F32))
    gc = jnp.cumsum(g.reshape(b, n, c, h).transpose(0, 3, 1, 2), axis=-1)
    idx = jnp.arange(c)
    causal = idx[:, None] >= idx[None, :]
    strict = idx[:, None] > idx[None, :]
    decay = jnp.exp(jnp.where(causal, gc[..., :, None] - gc[..., None, :], -jnp.inf))
    kb = k * beta[..., None]
    vb = v * beta[..., None]
    lower = jnp.where(strict, jnp.einsum("bhnid,bhnjd->bhnij", kb, k) * decay, 0.0)
    rhs = jnp.concatenate([vb, kb * jnp.exp(gc)[..., None]], axis=-1)
    sol = lax.linalg.triangular_solve(lower + jnp.eye(c, dtype=F32), rhs,
                                      left_side=True, lower=True, unit_diagonal=True)
    u, w = sol[..., :GDN_DV], sol[..., GDN_DV:]
    attn = jnp.where(causal, jnp.einsum("bhnid,bhnjd->bhnij", q, k) * decay, 0.0)
    g_last = gc[..., -1]
    q_dec = q * jnp.exp(gc)[..., None]
    k_dec = k * jnp.exp(g_last[..., None] - gc)[..., None]

    def step(state, inp):
        u_c, w_c, q_c, k_c, attn_c, gl = inp
        v_new = u_c - jnp.einsum("bhcd,bhde->bhce", w_c, state)
        o_c = jnp.einsum("bhcd,bhde->bhce", q_c, state) + jnp.einsum("bhij,bhje->bhie", attn_c, v_new)
        state = state * jnp.exp(gl)[..., None, None] + jnp.einsum("bhcd,bhce->bhde", k_c, v_new)
        return state, o_c

    cm = lambda t: t.transpose(2, 0, 1, 3, 4)
    state0 = jnp.zeros((b, h, GDN_DK, GDN_DV), F32)
    _, o = lax.scan(step, state0, (cm(u), cm(w), cm(q_dec), cm(k_dec), cm(attn), g_last.transpose(2, 0, 1)))
    o = o.transpose(1, 0, 3, 2, 4).reshape(b, s, h, GDN_DV)
    o = rmsnorm(o, norm_g) * jax.nn.silu(gate.astype(F32).reshape(b, s, h, GDN_DV))
    return o.reshape(b, s, h * GDN_DV).astype(gate.dtype)


def memory_xattn(hq, mem_n, wq, wkv, wo):
    b, s, d = hq.shape
    m = mem_n.shape[1]
    q = (hq @ wq).reshape(b, s, XA_HEADS, XA_HEAD_DIM)
    kv = (mem_n @ wkv).reshape(b, m, 2, XA_HEADS, XA_HEAD_DIM)
    k, v = kv[:, :, 0], kv[:, :, 1]
    sc = jnp.einsum("bqhd,bmhd->bhqm", q, k).astype(F32) * (XA_HEAD_DIM ** -0.5)
    p = jax.nn.softmax(sc, axis=-1)
    o = jnp.einsum("bhqm,bmhd->bqhd", p.astype(v.dtype), v).reshape(b, s, d)
    return o @ wo


def setup_inputs(seed: int = 0) -> dict:
    key = jax.random.key(seed)
    ks = iter(jax.random.split(key, 48))
    L = DEPTH

    def nrm(shape, scale):
        return jax.random.normal(next(ks), shape, F32) * scale

    def gain(shape):
        return 1.0 + nrm(shape, 0.05)

    x = nrm((BATCH, SEQ, D_MODEL), 1.0)
    mem = nrm((BATCH, N_MEM, D_MODEL), 1.0)
    start = jax.random.randint(next(ks), (BATCH, 1), 0, 64, dtype=jnp.int32) * CHUNK
    positions = (start + jnp.arange(SEQ, dtype=jnp.int32)[None, :]).astype(jnp.int32)
    u = jax.random.uniform(next(ks), (L, LRU_WIDTH), F32, minval=0.9, maxval=0.999)
    sg = u ** (1.0 / LRU_C)
    lru_lambda = jnp.log(sg) - jnp.log1p(-sg)
    gdn_a_log = jnp.log(jax.random.uniform(next(ks), (L, GDN_HEADS), F32, minval=1.0, maxval=16.0))
    dt = jnp.exp(jax.random.uniform(next(ks), (L, GDN_HEADS), F32,
                                    minval=float(np.log(1e-3)), maxval=float(np.log(1e-1))))
    gdn_dt_bias = dt + jnp.log(-jnp.expm1(-dt))
    return {
        "x": x,
        "mem": mem,
        "positions": positions,
        "norm_mix": gain((L, D_MODEL)),
        "w_in": nrm((L, D_MODEL, IN_COLS), D_MODEL ** -0.5),
        "ret_gn": gain((L, RET_HEADS, RET_DV)),
        "lru_conv_w": nrm((L, CONV_WIDTH, LRU_WIDTH), CONV_WIDTH ** -0.5),
        "lru_conv_b": nrm((L, LRU_WIDTH), 0.02),
        "lru_wa": nrm((L, LRU_BLOCKS, LRU_BLOCK, LRU_BLOCK), LRU_BLOCK ** -0.5),
        "lru_ba": nrm((L, LRU_WIDTH), 0.02),
        "lru_wx": nrm((L, LRU_BLOCKS, LRU_BLOCK, LRU_BLOCK), LRU_BLOCK ** -0.5),
        "lru_bx": nrm((L, LRU_WIDTH), 0.02),
        "lru_lambda": lru_lambda,
        "mla_q_norm": gain((L, MLA_Q_LORA)),
        "mla_w_uq": nrm((L, MLA_Q_LORA, MLA_HEADS * (MLA_NOPE + MLA_ROPE)), MLA_Q_LORA ** -0.5),
        "mla_kv_norm": gain((L, MLA_KV_LORA)),
        "mla_w_ukv": nrm((L, MLA_KV_LORA, MLA_HEADS * (MLA_NOPE + MLA_V)), MLA_KV_LORA ** -0.5),
        "gdn_conv_w": nrm((L, CONV_WIDTH, GDN_QKV), CONV_WIDTH ** -0.5),
        "gdn_a_log": gdn_a_log,
        "gdn_dt_bias": gdn_dt_bias,
        "gdn_norm": gain((L, GDN_DV)),
        "w_out": nrm((L, MIX_WIDTH, D_MODEL), MIX_WIDTH ** -0.5),
        "norm_xattn": gain((L, D_MODEL)),
        "norm_mem": gain((L, D_MODEL)),
        "xa_wq": nrm((L, D_MODEL, D_MODEL), D_MODEL ** -0.5),
        "xa_wkv": nrm((L, D_MODEL, 2 * D_MODEL), D_MODEL ** -0.5),
        "xa_wo": nrm((L, D_MODEL, D_MODEL), D_MODEL ** -0.5),
        "norm_mlp": gain((L, D_MODEL)),
        "mlp_w1": nrm((L, D_MODEL, D_FF), D_MODEL ** -0.5),
        "mlp_w2": nrm((L, D_FF, D_MODEL), D_FF ** -0.5),
        "final_norm": gain((D_MODEL,)),
    }


def reference(x, mem, positions, norm_mix, w_in, ret_gn, lru_conv_w, lru_conv_b, lru_wa, lru_ba,
              lru_wx, lru_bx, lru_lambda, mla_q_norm, mla_w_uq, mla_kv_norm, mla_w_ukv,
              gdn_conv_w, gdn_a_log, gdn_dt_bias, gdn_norm, w_out, norm_xattn, norm_mem,
              xa_wq, xa_wkv, xa_wo, norm_mlp, mlp_w1, mlp_w2, final_norm):
    for l in range(DEPTH):
        h = rmsnorm(x, norm_mix[l])
        z = h @ w_in[l]
        (rq, rk, rv, rg, lx, lg, cq, ckv, kr, gqkv, gg, gb, ga) = split_cols(z, IN_SIZES)
        y_a = retention(rq, rk, rv, rg, positions, ret_gn[l])
        y_b = rg_lru_branch(lx, lg, lru_conv_w[l], lru_conv_b[l], lru_wa[l], lru_ba[l],
                            lru_wx[l], lru_bx[l], lru_lambda[l])
        y_c = mla(cq, ckv, kr, positions, mla_q_norm[l], mla_w_uq[l], mla_kv_norm[l], mla_w_ukv[l])
        y_d = gated_deltanet(gqkv, gg, gb, ga, gdn_conv_w[l], gdn_a_log[l], gdn_dt_bias[l], gdn_norm[l])
        x = x + jnp.concatenate([y_a, y_b, y_c, y_d], axis=-1) @ w_out[l]
        x = x + memory_xattn(rmsnorm(x, norm_xattn[l]), rmsnorm(mem, norm_mem[l]),
                             xa_wq[l], xa_wkv[l], xa_wo[l])
        h = rmsnorm(x, norm_mlp[l])
        x = x + jnp.square(jax.nn.relu(h @ mlp_w1[l])) @ mlp_w2[l]
    return rmsnorm(x, final_norm)
```

```python
import math
from contextlib import ExitStack
import numpy as np
import concourse.bass as bass
import concourse.mybir as mybir
from concourse.bass_utils import run_bass_kernel_spmd

F32 = mybir.dt.float32
BF16 = mybir.dt.bfloat16
I32 = mybir.dt.int32
AF = mybir.ActivationFunctionType
ALU = mybir.AluOpType

NCORES = 8
DEPTH = 2
DM = 1024
S = 2048
BL = 2
T = BL * S
NMEM = 256
TM_ = BL * NMEM
INC = 2920
EPS = 1e-6
NEG = -1.0e30


class Tok:
    __slots__ = ("sem", "key", "val")

    def __init__(self, sem, key, val):
        self.sem = sem
        self.key = key
        self.val = val


class View:
    __slots__ = ("tile", "ap")

    def __init__(self, tile, ap):
        self.tile = tile
        self.ap = ap


class Tile:
    def __init__(self, ctx, t):
        self.t = t
        self.w = {}
        self.wdma = True
        self.rd = {}
        ctx.tiles.append(self)

    def __getitem__(self, idx):
        return View(self, self.t[idx])

    def v(self, ap):
        return View(self, ap)


class Sub:
    def __init__(self, tile, ap):
        self.tile = tile
        self.t = ap

    def __getitem__(self, idx):
        return View(self.tile, self.t[idx])


class Eng:
    def __init__(self, ctx, name, e):
        self.ctx = ctx
        self.name = name
        self.e = e
        self.sem = None
        self.key = None
        self.cnt = 0
        self.seen = {}
        self.total = 0
        self.rec = []

    def wait(self, tok):
        if tok is None:
            return
        if tok.key == self.key and self.name == "tensor":
            return
        if self.seen.get(tok.key, 0) >= tok.val:
            return
        sem, val = tok.sem, tok.val
        self.rec.append(lambda e: e.wait_ge(sem, val))
        self.seen[tok.key] = tok.val

    def __getattr__(self, opname):
        def f(**kw):
            return self.ctx.emit(self, opname, kw)
        return f


OUT_KEYS = ("out", "accum_out", "ap")


class Ctx:
    NDMA = 32

    def __init__(self, nc, stack):
        self.nc = nc
        self.stack = stack
        self.tiles = []
        self.epoch = 0
        self.engs = {}
        for name in ["tensor", "vector", "scalar", "gpsimd", "sync"]:
            self.engs[name] = Eng(self, name, getattr(nc, name))
        self.pe = self.engs["tensor"]
        self.dve = self.engs["vector"]
        self.act = self.engs["scalar"]
        self.pool = self.engs["gpsimd"]
        self.sp = self.engs["sync"]
        self.dsem = [stack.enter_context(nc.semaphore("dma%d" % i)) for i in range(self.NDMA)]
        self.dcnt = [0] * self.NDMA
        self.dnext = 0
        self.dnext_by = {"hw": 0, "sw": 0}
        self._new_epoch_sems()

    def _new_epoch_sems(self):
        for name, E in self.engs.items():
            if E.sem is not None and name != "tensor":
                continue
            E.sem = self.stack.enter_context(self.nc.semaphore("e%d_%s" % (self.epoch, name)))
            E.key = (self.epoch, name)
            E.total += E.cnt
            E.cnt = 0
        self.epoch += 1

    def tile(self, t):
        return Tile(self, t)

    def emit(self, E, opname, kw):
        outs = []
        ins = []
        kw2 = {}
        for k, v in kw.items():
            if isinstance(v, View):
                (outs if k in OUT_KEYS else ins).append(v.tile)
                kw2[k] = v.ap
            else:
                kw2[k] = v
        is_dma = opname == "dma_start"
        for t in ins:
            for tok in t.w.values():
                E.wait(tok)
        for t in outs:
            if t.rd:
                for tok in t.rd.values():
                    E.wait(tok)
                if t in ins:
                    pass
            if not (is_dma and t.wdma and not t.rd):
                for tok in t.w.values():
                    E.wait(tok)
        if is_dma:
            half = self.NDMA // 2
            kind = "sw" if E.name == "gpsimd" else "hw"
            i = (self.dnext_by[kind] % half) + (half if kind == "sw" else 0)
            self.dnext_by[kind] += 1
            self.dnext += 1
            if self.dcnt[i] > 0:
                E.wait(Tok(self.dsem[i], ("d", i), 16 * self.dcnt[i]))
            self.dcnt[i] += 1
            tok = Tok(self.dsem[i], ("d", i), 16 * self.dcnt[i])
            dsem = tok.sem
            E.rec.append(lambda e: e.dma_start(**kw2).then_inc(dsem, 16))
        else:
            E.cnt += 1
            tok = Tok(E.sem, E.key, E.cnt)
            esem = E.sem
            E.rec.append(lambda e: getattr(e, opname)(**kw2).then_inc(esem, 1))
        for t in ins:
            if t not in outs:
                t.rd[tok.key] = tok
        for t in outs:
            if is_dma and t.wdma and not t.rd and t.w:
                t.w[tok.key] = tok
            else:
                t.w = {tok.key: tok}
                t.wdma = is_dma
            t.rd = {}

    def barrier(self):
        toks = [Tok(E.sem, E.key, E.cnt) for E in self.engs.values() if E.cnt > 0]
        for i in range(self.NDMA):
            if self.dcnt[i] > 0:
                toks.append(Tok(self.dsem[i], ("d", i), 16 * self.dcnt[i]))
        for E in self.engs.values():
            for tok in toks:
                E.wait(tok)
        for t in self.tiles:
            t.w = {}
            t.wdma = True
            t.rd = {}
        self._new_epoch_sems()

    def finish(self):
        self.barrier()
        with self.nc.Block() as block:
            for name, E in self.engs.items():
                def run(e, E=E):
                    for f in E.rec:
                        f(e)
                getattr(block, name)(run)


class Arena:
    def __init__(self, c, t, nwords):
        self.c = c
        self.t = t
        self.n = nwords
        self.off = 0

    def reset(self, to=0):
        self.off = to

    def alloc(self, P, shape, dtype):
        shape = tuple(shape)
        n = int(np.prod(shape))
        words = n if dtype in (F32, I32) else (n + 1) // 2
        words = (words + 1) // 2 * 2
        ap = self.t[0:P, self.off:self.off + words]
        self.off += words
        assert self.off <= self.n, ("arena overflow", self.off, self.n)
        if dtype != F32:
            ap = ap.bitcast(dtype)
        ap = ap[:, 0:n]
        if len(shape) == 2:
            ap = ap.rearrange("p (a b) -> p a b", a=shape[0])
        elif len(shape) == 3:
            ap = ap.rearrange("p (a b c) -> p a b c", a=shape[0], b=shape[1])
        return self.c.tile(ap)


class DGrid:
    def __init__(self, c, ap, row_chunks, colb):
        self.ap = ap
        self.rc = row_chunks
        self.colb = colb
        ncb = ap.shape[1] // colb
        self.tiles = [[c.tile(ap) for _ in range(ncb)] for _ in row_chunks]

    def find(self, r):
        for i, (s, n) in enumerate(self.rc):
            if s <= r < s + n:
                return i
        raise KeyError(r)

    def v(self, r0, nr, cb, c0=None, ncol=None):
        i = self.find(r0)
        assert r0 + nr <= self.rc[i][0] + self.rc[i][1]
        if c0 is None:
            c0, ncol = cb * self.colb, self.colb
        return self.tiles[i][cb].v(self.ap[r0:r0 + nr, c0:c0 + ncol])

    def vb(self, r0, P, cb):
        i = self.find(r0)
        ap = self.ap[r0:r0 + 1, cb * self.colb:(cb + 1) * self.colb].partition_broadcast(P)
        return self.tiles[i][cb].v(ap)


ZCH = ([(i * 128, 128) for i in range(12)] +
       [(1536, 128), (1664, 64), (1728, 128), (1856, 32)] +
       [(1888 + i * 128, 128) for i in range(6)] +
       [(2656, 128), (2784, 128), (2912, 8)])
XCH = [(i * 128, 128) for i in range(8)]

NCOL = 151
C_RGN2, C_GCW2, C_GNORM2, C_ALOG2, C_DTB2 = 120, 122, 146, 147, 149
C_NMIX, C_NXA, C_NMLP, C_NMEM, C_NFIN = 0, 8, 16, 24, 32
C_QN, C_KVN, C_RGN = 40, 42, 43
C_LCW, C_LCB, C_LBA, C_LBX, C_LLAM = 47, 55, 57, 59, 61
C_GCW, C_GNORM, C_ALOG, C_DTB = 63, 111, 112, 116

K_ID, K_ROTA, K_ROTC, K_ONES, K_B64 = 0, 128, 256, 384, 512
K_INVA, K_INVC = 640, 641
K_XI, K_ZETA, K_DMASK = 642, 642 + 256, 642 + 512
K_NMGE, K_NMGT, K_NMLT, K_DIAG, K_OFFL = 1410, 1474, 1538, 1602, 1666
K_G64 = 1730
K_XI2, K_ZETA2, K_DMASK2, K_ID2 = 1734, 1734 + 128, 1734 + 256, 1734 + 384
NCST = 1734 + 448


def host_constants():
    c = np.zeros((128, NCST), np.float32)
    c[:, K_ID:K_ID + 128] = np.eye(128, dtype=np.float32)
    ra = np.zeros((128, 128), np.float32)
    rc = np.zeros((128, 128), np.float32)
    for m in range(128):
        if (m % 64) < 32:
            ra[m + 32, m] = -1.0
        else:
            ra[m - 32, m] = 1.0
        if (m % 32) < 16:
            rc[m + 16, m] = -1.0
        else:
            rc[m - 16, m] = 1.0
    c[:, K_ROTA:K_ROTA + 128] = ra
    c[:, K_ROTC:K_ROTC + 128] = rc
    c[:, K_ONES:K_ONES + 128] = 1.0
    b64 = np.zeros((128, 128), np.float32)
    b64[:64, :64] = 1.0
    b64[64:, 64:] = 1.0
    c[:, K_B64:K_B64 + 128] = b64
    p = np.arange(128)
    c[:, K_INVA] = np.power(np.float32(10000.0), -(p % 32).astype(np.float32) / np.float32(32)).astype(np.float32)
    c[:, K_INVC] = np.power(np.float32(10000.0), -(p % 16).astype(np.float32) / np.float32(16)).astype(np.float32)
    idx = np.arange(64, dtype=np.float64)
    for h in range(4):
        lg = math.log1p(-2.0 ** (-5.0 - h))
        c[:, K_XI + 64 * h:K_XI + 64 * h + 64] = (0.125 * np.exp((idx + 1.0) * lg))[None, :]
        c[:, K_ZETA + 64 * h:K_ZETA + 64 * h + 64] = np.exp((63.0 - idx) * lg)[None, :]
        c[:64, K_DMASK + 64 * h:K_DMASK + 64 * h + 64] = np.exp(np.abs(idx[:, None] - idx[None, :]) * lg)
        c[:, K_G64 + h] = math.exp(64.0 * lg)
    pp = np.arange(64)[:, None]
    ff = np.arange(64)[None, :]
    for r0 in (0, 64):
        c[r0:r0 + 64, K_NMGE:K_NMGE + 64] = np.where(ff >= pp, 0.0, NEG)
        c[r0:r0 + 64, K_NMGT:K_NMGT + 64] = np.where(ff > pp, 0.0, NEG)
        c[r0:r0 + 64, K_NMLT:K_NMLT + 64] = np.where(ff < pp, 0.0, NEG)
        c[r0:r0 + 64, K_DIAG:K_DIAG + 64] = ((pp // 32) == (ff // 32)).astype(np.float32)
        c[r0:r0 + 64, K_OFFL:K_OFFL + 64] = ((pp >= 32) & (ff < 32)).astype(np.float32)
        c[r0:r0 + 64, K_ID2:K_ID2 + 64] = np.eye(64, dtype=np.float32)
    for hp in range(2):
        for hl in range(2):
            h = 2 * hp + hl
            lg = math.log1p(-2.0 ** (-5.0 - h))
            rows = slice(64 * hl, 64 * hl + 64)
            c[rows, K_XI2 + 64 * hp:K_XI2 + 64 * hp + 64] = (0.125 * np.exp((idx + 1.0) * lg))[None, :]
            c[rows, K_ZETA2 + 64 * hp:K_ZETA2 + 64 * hp + 64] = np.exp((63.0 - idx) * lg)[None, :]
            c[rows, K_DMASK2 + 64 * hp:K_DMASK2 + 64 * hp + 64] = np.exp(np.abs(idx[:, None] - idx[None, :]) * lg)
    return c


def host_cols(inp):
    out = np.zeros((DEPTH, 128, NCOL), np.float32)
    for l in range(DEPTH):
        o = out[l]

        def put8(c0, v):
            o[:, c0:c0 + 8] = v.reshape(8, 128).T
        put8(C_NMIX, inp["norm_mix"][l])
        put8(C_NXA, inp["norm_xattn"][l])
        put8(C_NMLP, inp["norm_mlp"][l])
        put8(C_NMEM, inp["norm_mem"][l])
        put8(C_NFIN, inp["final_norm"])
        o[:, C_QN] = inp["mla_q_norm"][l][:128]
        o[:64, C_QN + 1] = inp["mla_q_norm"][l][128:]
        o[:, C_KVN] = inp["mla_kv_norm"][l]
        for h in range(4):
            o[:64, C_RGN + h] = inp["ret_gn"][l][h]
        for ch in range(2):
            sl = slice(ch * 128, ch * 128 + 128)
            for k in range(4):
                o[:, C_LCW + ch * 4 + k] = inp["lru_conv_w"][l][k, sl]
            o[:, C_LCB + ch] = inp["lru_conv_b"][l][sl]
            o[:, C_LBA + ch] = inp["lru_ba"][l][sl]
            o[:, C_LBX + ch] = inp["lru_bx"][l][sl]
            o[:, C_LLAM + ch] = inp["lru_lambda"][l][sl]
        for part in range(3):
            for h in range(4):
                for k in range(4):
                    o[:64, C_GCW + (part * 4 + h) * 4 + k] = inp["gdn_conv_w"][l][k, part * 256 + h * 64: part * 256 + h * 64 + 64]
        o[:64, C_GNORM] = inp["gdn_norm"][l]
        o[:, C_GNORM2] = np.tile(inp["gdn_norm"][l], 2)
        for hp in range(2):
            o[:, C_RGN2 + hp] = inp["ret_gn"][l][2 * hp:2 * hp + 2].reshape(128)
            for part in range(3):
                for k in range(4):
                    o[:, C_GCW2 + (part * 2 + hp) * 4 + k] = inp["gdn_conv_w"][l][k, part * 256 + hp * 128: part * 256 + hp * 128 + 128]
            o[:, C_ALOG2 + hp] = np.repeat(inp["gdn_a_log"][l][2 * hp:2 * hp + 2], 64)
            o[:, C_DTB2 + hp] = np.repeat(inp["gdn_dt_bias"][l][2 * hp:2 * hp + 2], 64)
        for h in range(4):
            o[:, C_ALOG + h] = inp["gdn_a_log"][l][h]
            o[:, C_DTB + h] = inp["gdn_dt_bias"][l][h]
    return out


AW = 50600
TWO_PI_S = 6.2831793


class K:
    def __init__(self, nc, st, debug=()):
        self.nc = nc
        self.debug = set(debug)
        c = self.c = Ctx(nc, st)
        self.ar = Arena(c, st.enter_context(nc.sbuf_tensor("arena", [128, AW], F32)), AW)
        self.psl = [c.tile(st.enter_context(nc.psum_tensor("ps%d" % i, [128, 512], F32))) for i in range(8)]
        self.psi = 0
        self.CST = c.tile(st.enter_context(nc.sbuf_tensor("sb_cst", [128, NCST], F32)))
        self.COLS = c.tile(st.enter_context(nc.sbuf_tensor("sb_cols", [128, DEPTH, NCOL], F32)))
        self.ONESB = c.tile(st.enter_context(nc.sbuf_tensor("onesb", [128, 128], BF16)))
        ein = lambda name, shape, dt=F32: nc.dram_tensor(name, list(shape), dt, kind="ExternalInput").ap()
        self.d_in = {}
        for name, shape, dt in [
            ("xT", (DM, T), F32), ("memT", (DM, TM_), F32), ("pos", (1, T), I32),
            ("cst", (128, NCST), F32), ("cols", (DEPTH, 128, NCOL), F32),
            ("w_in", (DEPTH, DM, INC), F32), ("lru_wa", (DEPTH, 4, 64, 64), F32), ("lru_wx", (DEPTH, 4, 64, 64), F32),
            ("mla_w_uq", (DEPTH, 192, 384), F32), ("mla_w_ukv", (DEPTH, 128, 512), F32),
            ("w_out", (DEPTH, DM, DM), F32), ("xa_wq", (DEPTH, DM, DM), F32), ("xa_wkv", (DEPTH, DM, 2 * DM), F32),
            ("xa_wo", (DEPTH, DM, DM), F32), ("mlp_w1", (DEPTH, DM, 4 * DM), F32), ("mlp_w2", (DEPTH, 4 * DM, DM), F32),
        ]:
            self.d_in[name] = c.tile(ein(name, shape, dt))
        self.outT = DGrid(c, nc.dram_tensor("outT", [DM, T], F32, kind="ExternalOutput").ap(), XCH, 512)

        def scratch(name, shape, dt=F32):
            kind = "ExternalOutput" if name in self.debug else "Internal"
            return nc.dram_tensor(name, list(shape), dt, kind=kind).ap()
        self.XT = DGrid(c, scratch("XS", (DM, T)), XCH, 512)
        self.XIN = DGrid(c, self.d_in["xT"].t, XCH, 512)
        self.ZT = DGrid(c, scratch("ZT", (INC, T)), ZCH, 512)
        self.YT = DGrid(c, scratch("YT", (DM, T), BF16), XCH, 512)
        self.TAB = {n: DGrid(c, scratch(n, (128, T)), [(0, 128)], 512) for n in ("COSA", "SINA", "COSC", "SINC")}

    def ps(self):
        t = self.psl[self.psi % 8]
        self.psi += 1
        return t

    def col(self, l, j, P=128):
        return self.COLS[0:P, l, j:j + 1]

    def phase_setup(self):
        c, ar = self.c, self.ar
        c.sp.dma_start(out=self.CST[:, :], in_=self.d_in["cst"][:, :])
        for l in range(DEPTH):
            c.sp.dma_start(out=self.COLS[:, l, :], in_=self.d_in["cols"][l, :, :])
        c.dve.tensor_copy(out=self.ONESB[:, :], in_=self.CST[:, K_ONES:K_ONES + 128])
        ar.reset()
        POSI = ar.alloc(128, (T,), I32)
        POSF = ar.alloc(128, (T,), F32)
        U = ar.alloc(128, (T,), F32)
        UI = ar.alloc(128, (T,), I32)
        UF = ar.alloc(128, (T,), F32)
        R = ar.alloc(128, (T,), F32)
        posd = self.d_in["pos"]
        c.sp.dma_start(out=POSI[:, :], in_=posd.v(posd.t.partition_broadcast(128)))
        c.dve.tensor_copy(out=POSF[:, :], in_=POSI[:, :])
        for kcol, cn, sn in ((K_INVA, "COSA", "SINA"), (K_INVC, "COSC", "SINC")):
            for name, shift in ((sn, 0.0), (cn, 0.25)):
                c.dve.tensor_scalar(out=U[:, :], in0=POSF[:, :], scalar1=self.CST[:, kcol:kcol + 1],
                                    scalar2=1.0 / (2.0 * math.pi), op0=ALU.mult, op1=ALU.mult)
                if shift:
                    c.dve.tensor_scalar(out=U[:, :], in0=U[:, :], scalar1=shift, scalar2=None, op0=ALU.add)
                c.dve.tensor_copy(out=UI[:, :], in_=U[:, :])
                c.dve.tensor_copy(out=UF[:, :], in_=UI[:, :])
                c.dve.tensor_tensor(out=R[:, :], in0=U[:, :], in1=UF[:, :], op=ALU.subtract)
                c.act.activation(out=R[:, :], in_=R[:, :], func=AF.Sin, scale=TWO_PI_S)
                for tb in range(T // 512):
                    c.pool.dma_start(out=self.TAB[name].v(0, 128, tb), in_=R[:, tb * 512:(tb + 1) * 512])

    def load_w_bf16(self, W, src_tile, src_ap_fn, nk, ncols, stg):
        c = self.c
        for k in range(nk):
            s = stg[k % len(stg)]
            c.sp.dma_start(out=s[:, 0:ncols], in_=src_tile.v(src_ap_fn(k)))
            if k % 2:
                c.act.activation(out=W[:, k, :], in_=s[:, 0:ncols], func=AF.Copy)
            else:
                c.dve.tensor_copy(out=W[:, k, :], in_=s[:, 0:ncols])

    def rms_stats(self, x, nk, n, sq, rs, kparts=None):
        c = self.c
        c.act.activation(out=sq[:, :, :], in_=x[:, :, :], func=AF.Square)
        p = self.ps()
        for k in range(nk):
            c.pe.matmul(out=p[:, 0:n], lhsT=self.ONESB[:, :], rhs=sq[:, k, :], start=(k == 0), stop=(k == nk - 1))
        c.act.activation(out=rs[:, 0:n], in_=p[:, 0:n], func=AF.Ln, scale=1.0 / (128 * nk), bias=EPS)
        c.act.activation(out=rs[:, 0:n], in_=rs[:, 0:n], func=AF.Exp, scale=-0.5)

    def phase_proj(self, l, XSRC):
        c, ar = self.c, self.ar
        c.barrier()
        ar.reset()
        CG = [(0, 1024), (1024, 1888), (1888, INC)]
        WG = [ar.alloc(128, (8, c1 - c0), BF16) for c0, c1 in CG]
        stg = [ar.alloc(128, (1032,), F32) for _ in range(4)]
        win = self.d_in["w_in"]
        X32 = [ar.alloc(128, (8, 512), F32) for _ in range(2)]
        for k in range(8):
            c.sp.dma_start(out=X32[0][:, k, :], in_=XSRC.v(k * 128, 128, 0))
        wi = [0]

        def load_group(gi, act_only):
            c0, c1 = CG[gi]
            for k in range(8):
                s_ = stg[wi[0] % 4]
                c.sp.dma_start(out=s_[:, 0:c1 - c0], in_=win.v(win.t[l, k * 128:(k + 1) * 128, c0:c1]))
                if act_only or wi[0] % 2:
                    c.act.activation(out=WG[gi][:, k, :], in_=s_[:, 0:c1 - c0], func=AF.Copy)
                else:
                    c.dve.tensor_copy(out=WG[gi][:, k, :], in_=s_[:, 0:c1 - c0])
                wi[0] += 1

        load_group(0, False)

        def wsl(k, s0, n):
            for gi, (c0, c1) in enumerate(CG):
                if c0 <= s0 < c1:
                    return WG[gi][:, k, s0 - c0:s0 - c0 + n]
        SQ = ar.alloc(128, (8, 512), BF16)
        XG = [ar.alloc(128, (8, 512), BF16) for _ in range(2)]
        RS = [ar.alloc(128, (512,), F32) for _ in range(2)]
        ZS = [ar.alloc(128, (512,), F32) for _ in range(4)]
        for tb in range(T // 512):
            x = X32[tb % 2]
            if tb > 0:
                for k in range(8):
                    c.sp.dma_start(out=x[:, k, :], in_=XSRC.v(k * 128, 128, tb))
            rs = RS[tb % 2]
            self.rms_stats(x, 8, 512, SQ, rs)
            xg = XG[tb % 2]
            for k in range(8):
                c.dve.tensor_scalar(out=xg[:, k, :], in0=x[:, k, :], scalar1=self.col(l, C_NMIX + k), scalar2=None,
                                    op0=ALU.mult)
            if tb == 0:
                load_group(1, True)
                load_group(2, True)
            for ci, (s0, n) in enumerate(ZCH):
                p = self.ps()
                for k in range(8):
                    c.pe.matmul(out=p[0:n, :], lhsT=wsl(k, s0, n), rhs=xg[:, k, :], start=(k == 0), stop=(k == 7))
                z = ZS[ci % 4]
                c.dve.tensor_tensor(out=z[0:n, :], in0=p[0:n, :], in1=rs[0:n, :], op=ALU.mult)
                c.pool.dma_start(out=self.ZT.v(s0, n, tb), in_=z[0:n, :])

    def load_seq_rows(self, dst, grid, r0, P, b, eng=None):
        eng = eng or self.c.sp
        for j in range(4):
            eng.dma_start(out=dst[0:P, j * 512:(j + 1) * 512], in_=grid.v(r0, P, b * 4 + j))

    def store_seq_rows(self, grid, r0, P, b, src):
        for j in range(4):
            self.c.pool.dma_start(out=grid.v(r0, P, b * 4 + j), in_=src[0:P, j * 512:(j + 1) * 512])

    def conv4(self, out, x, l, colbase, P, bias_col=None):
        c = self.c
        w = lambda k: self.col(l, colbase + k, P)
        if bias_col is None:
            c.dve.tensor_scalar(out=out[0:P, :], in0=x[0:P, :], scalar1=w(3), scalar2=None, op0=ALU.mult)
        else:
            c.dve.tensor_scalar(out=out[0:P, :], in0=x[0:P, :], scalar1=w(3), scalar2=self.col(l, bias_col, P),
                                op0=ALU.mult, op1=ALU.add)
        for sh in (1, 2, 3):
            c.dve.scalar_tensor_tensor(out=out[0:P, sh:S], in0=x[0:P, 0:S - sh], scalar=w(3 - sh), in1=out[0:P, sh:S],
                                       op0=ALU.mult, op1=ALU.add)

    def phase_lru(self, l):
        c, ar = self.c, self.ar
        c.barrier()
        ar.reset()
        WA = ar.alloc(128, (128,), F32)
        WX = ar.alloc(128, (128,), F32)
        SP_ = ar.alloc(128, (8,), F32)
        XB = ar.alloc(128, (S,), F32)
        GB = ar.alloc(128, (S,), F32)
        XC = ar.alloc(128, (S,), F32)
        RG = ar.alloc(128, (S,), F32)
        IG = ar.alloc(128, (S,), F32)
        A_ = ar.alloc(128, (S,), F32)
        U_ = ar.alloc(128, (S,), F32)
        H_ = ar.alloc(128, (S,), F32)
        YB = ar.alloc(128, (S,), BF16)
        for ch in range(2):
            for W, nm in ((WA, "lru_wa"), (WX, "lru_wx")):
                c.pool.memset(ap=W[:, :], constant=0.0)
                src = self.d_in[nm]
                for hb in range(2):
                    c.sp.dma_start(out=W[64 * hb:64 * hb + 64, 64 * hb:64 * hb + 64], in_=src[l, 2 * ch + hb, :, :])
            lam = self.col(l, C_LLAM + ch)
            c.dve.tensor_scalar(out=SP_[:, 0:1], in0=lam, scalar1=-1.0, scalar2=None, op0=ALU.mult)
            c.dve.tensor_tensor(out=SP_[:, 1:2], in0=SP_[:, 0:1], in1=lam, op=ALU.max)
            c.act.activation(out=SP_[:, 2:3], in_=SP_[:, 1:2], func=AF.Exp, scale=-1.0)
            c.act.activation(out=SP_[:, 2:3], in_=SP_[:, 2:3], func=AF.Ln, bias=1.0)
            c.dve.tensor_scalar(out=SP_[:, 3:4], in0=SP_[:, 0:1], scalar1=0.0, scalar2=None, op0=ALU.max)
            c.dve.tensor_tensor(out=SP_[:, 3:4], in0=SP_[:, 3:4], in1=SP_[:, 2:3], op=ALU.add)
            c.dve.tensor_scalar(out=SP_[:, 4:5], in0=SP_[:, 3:4], scalar1=-8.0, scalar2=None, op0=ALU.mult)
            c.dve.tensor_scalar(out=SP_[:, 5:6], in0=SP_[:, 3:4], scalar1=-16.0, scalar2=None, op0=ALU.mult)
            for b in range(BL):
                self.load_seq_rows(XB, self.ZT, 1024 + ch * 128, 128, b)
                self.load_seq_rows(GB, self.ZT, 1280 + ch * 128, 128, b)
                self.conv4(XC, XB, l, C_LCW + ch * 4, 128, bias_col=C_LCB + ch)
                for j in range(4):
                    sl = slice(j * 512, (j + 1) * 512)
                    p = self.ps()
                    c.pe.matmul(out=p[:, :], lhsT=WA[:, :], rhs=XC[:, sl], start=True, stop=True)
                    c.act.activation(out=RG[:, sl], in_=p[:, :], func=AF.Sigmoid, bias=self.col(l, C_LBA + ch))
                    p = self.ps()
                    c.pe.matmul(out=p[:, :], lhsT=WX[:, :], rhs=XC[:, sl], start=True, stop=True)
                    c.act.activation(out=IG[:, sl], in_=p[:, :], func=AF.Sigmoid, bias=self.col(l, C_LBX + ch))
                c.act.activation(out=A_[:, :], in_=RG[:, :], func=AF.Exp, scale=SP_[:, 4:5])
                c.act.activation(out=U_[:, :], in_=RG[:, :], func=AF.Exp, scale=SP_[:, 5:6])
                c.dve.tensor_scalar(out=U_[:, :], in0=U_[:, :], scalar1=-1.0, scalar2=1.0, op0=ALU.mult, op1=ALU.add)
                c.dve.tensor_scalar(out=U_[:, :], in0=U_[:, :], scalar1=0.0, scalar2=None, op0=ALU.max)
                c.act.activation(out=U_[:, :], in_=U_[:, :], func=AF.Sqrt)
                c.dve.tensor_tensor(out=IG[:, :], in0=IG[:, :], in1=XC[:, :], op=ALU.mult)
                c.dve.tensor_tensor(out=U_[:, :], in0=U_[:, :], in1=IG[:, :], op=ALU.mult)
                c.dve.tensor_tensor_scan(out=H_[:, :], data0=A_[:, :], data1=U_[:, :], initial=0.0, op0=ALU.mult, op1=ALU.add)
                c.pool.tensor_tensor(out=RG[:, :], in0=GB[:, :], in1=GB[:, :], op=ALU.mult)
                c.pool.tensor_scalar(out=RG[:, :], in0=RG[:, :], scalar1=0.044715, scalar2=1.0, op0=ALU.mult, op1=ALU.add)
                c.pool.tensor_tensor(out=RG[:, :], in0=RG[:, :], in1=GB[:, :], op=ALU.mult)
                c.act.activation(out=RG[:, :], in_=RG[:, :], func=AF.Tanh, scale=0.7978845608028654)
                c.pool.tensor_scalar(out=RG[:, :], in0=RG[:, :], scalar1=0.5, scalar2=0.5, op0=ALU.mult, op1=ALU.add)
                c.pool.tensor_tensor(out=RG[:, :], in0=RG[:, :], in1=GB[:, :], op=ALU.mult)
                c.dve.tensor_tensor(out=YB[:, :], in0=H_[:, :], in1=RG[:, :], op=ALU.mult)
                self.store_seq_rows(self.YT, 256 + ch * 128, 128, b, YB)

    def lru_gen(self, l):
        c, ar = self.c, self.ar
        WA = ar.alloc(128, (128,), F32)
        WX = ar.alloc(128, (128,), F32)
        SP_ = ar.alloc(128, (8,), F32)
        XB, GB, XC, RG, IG, A_, U_, H_ = [ar.alloc(128, (S,), F32) for _ in range(8)]
        YB = ar.alloc(128, (S,), BF16)
        pl = self.psl[7]
        for ch in range(2):
            for W, nm in ((WA, "lru_wa"), (WX, "lru_wx")):
                c.pool.memset(ap=W[:, :], constant=0.0)
                src = self.d_in[nm]
                for hb in range(2):
                    c.sp.dma_start(out=W[64 * hb:64 * hb + 64, 64 * hb:64 * hb + 64], in_=src[l, 2 * ch + hb, :, :])
            yield
            lam = self.col(l, C_LLAM + ch)
            c.dve.tensor_scalar(out=SP_[:, 0:1], in0=lam, scalar1=-1.0, scalar2=None, op0=ALU.mult)
            c.dve.tensor_tensor(out=SP_[:, 1:2], in0=SP_[:, 0:1], in1=lam, op=ALU.max)
            c.act.activation(out=SP_[:, 2:3], in_=SP_[:, 1:2], func=AF.Exp, scale=-1.0)
            c.act.activation(out=SP_[:, 2:3], in_=SP_[:, 2:3], func=AF.Ln, bias=1.0)
            c.dve.tensor_scalar(out=SP_[:, 3:4], in0=SP_[:, 0:1], scalar1=0.0, scalar2=None, op0=ALU.max)
            c.dve.tensor_tensor(out=SP_[:, 3:4], in0=SP_[:, 3:4], in1=SP_[:, 2:3], op=ALU.add)
            c.dve.tensor_scalar(out=SP_[:, 4:5], in0=SP_[:, 3:4], scalar1=-8.0, scalar2=None, op0=ALU.mult)
            c.dve.tensor_scalar(out=SP_[:, 5:6], in0=SP_[:, 3:4], scalar1=-16.0, scalar2=None, op0=ALU.mult)
            yield
            for b in range(BL):
                self.load_seq_rows(XB, self.ZT, 1024 + ch * 128, 128, b)
                self.load_seq_rows(GB, self.ZT, 1280 + ch * 128, 128, b)
                yield
                w = lambda k: self.col(l, C_LCW + ch * 4 + k)
                c.dve.tensor_scalar(out=XC[:, :], in0=XB[:, :], scalar1=w(3), scalar2=self.col(l, C_LCB + ch), op0=ALU.mult, op1=ALU.add)
                yield
                for sh in (1, 2, 3):
                    c.dve.scalar_tensor_tensor(out=XC[:, sh:S], in0=XB[:, 0:S - sh], scalar=w(3 - sh), in1=XC[:, sh:S], op0=ALU.mult, op1=ALU.add)
                    yield
                for j in range(4):
                    sl = slice(j * 512, (j + 1) * 512)
                    c.pe.matmul(out=pl[:, :], lhsT=WA[:, :], rhs=XC[:, sl], start=True, stop=True)
                    c.act.activation(out=RG[:, sl], in_=pl[:, :], func=AF.Sigmoid, bias=self.col(l, C_LBA + ch))
                    c.pe.matmul(out=pl[:, :], lhsT=WX[:, :], rhs=XC[:, sl], start=True, stop=True)
                    c.act.activation(out=IG[:, sl], in_=pl[:, :], func=AF.Sigmoid, bias=self.col(l, C_LBX + ch))
                yield
                c.act.activation(out=A_[:, :], in_=RG[:, :], func=AF.Exp, scale=SP_[:, 4:5])
                yield
                c.act.activation(out=U_[:, :], in_=RG[:, :], func=AF.Exp, scale=SP_[:, 5:6])
                yield
                c.dve.tensor_scalar(out=U_[:, :], in0=U_[:, :], scalar1=-1.0, scalar2=1.0, op0=ALU.mult, op1=ALU.add)
                yield
                c.dve.tensor_scalar(out=U_[:, :], in0=U_[:, :], scalar1=0.0, scalar2=None, op0=ALU.max)
                yield
                c.act.activation(out=U_[:, :], in_=U_[:, :], func=AF.Sqrt)
                yield
                c.pool.tensor_tensor(out=IG[:, :], in0=IG[:, :], in1=XC[:, :], op=ALU.mult)
                yield
                c.dve.tensor_tensor(out=U_[:, :], in0=U_[:, :], in1=IG[:, :], op=ALU.mult)
                yield
                c.dve.tensor_tensor_scan(out=H_[:, :], data0=A_[:, :], data1=U_[:, :], initial=0.0, op0=ALU.mult, op1=ALU.add)
                yield
                c.pool.tensor_tensor(out=RG[:, :], in0=GB[:, :], in1=GB[:, :], op=ALU.mult)
                yield
                c.pool.tensor_scalar(out=RG[:, :], in0=RG[:, :], scalar1=0.044715, scalar2=1.0, op0=ALU.mult, op1=ALU.add)
                yield
                c.pool.tensor_tensor(out=RG[:, :], in0=RG[:, :], in1=GB[:, :], op=ALU.mult)
                yield
                c.act.activation(out=RG[:, :], in_=RG[:, :], func=AF.Tanh, scale=0.7978845608028654)
                yield
                c.pool.tensor_scalar(out=RG[:, :], in0=RG[:, :], scalar1=0.5, scalar2=0.5, op0=ALU.mult, op1=ALU.add)
                yield
                c.pool.tensor_tensor(out=RG[:, :], in0=RG[:, :], in1=GB[:, :], op=ALU.mult)
                yield
                c.dve.tensor_tensor(out=YB[:, :], in0=H_[:, :], in1=RG[:, :], op=ALU.mult)
                yield
                self.store_seq_rows(self.YT, 256 + ch * 128, 128, b, YB)
                yield

    def rope64(self, dst, src, COS, SIN, T2, rot_k, P):
        c = self.c
        for j in range(4):
            sl = slice(j * 512, (j + 1) * 512)
            p = self.ps()
            c.pe.matmul(out=p[0:P, :], lhsT=self.CST[0:P, rot_k:rot_k + P], rhs=src[0:P, sl], start=True, stop=True)
            c.dve.tensor_tensor(out=T2[0:P, sl], in0=p[0:P, :], in1=SIN[0:P, sl], op=ALU.mult)
        c.pool.tensor_tensor(out=dst[0:P, :], in0=src[0:P, :], in1=COS[0:P, :], op=ALU.mult)
        c.pool.tensor_tensor(out=dst[0:P, :], in0=dst[0:P, :], in1=T2[0:P, :], op=ALU.add)

    def tok_major(self, dst, src, evac_scalar=None):
        c = self.c
        for g in range(4):
            p = self.ps()
            for s_ in range(8):
                n = g * 8 + s_
                c.pe.transpose(out=p[0:64, s_ * 64:s_ * 64 + 64], in_=src[0:64, n * 64:n * 64 + 64],
                               identity=self.CST[0:64, K_ID:K_ID + 64])
            pv = p.v(p.t[0:64, :].rearrange("p (s e) -> p s e", s=8))
            if evac_scalar is None:
                c.act.activation(out=dst[:, g * 8:g * 8 + 8, :], in_=pv, func=AF.Copy)
            else:
                c.dve.tensor_tensor(out=dst[:, g * 8:g * 8 + 8, :], in0=pv, in1=evac_scalar(g), op=ALU.mult)

    def phase_ret(self, l):
        c, ar = self.c, self.ar
        c.barrier()
        ar.reset()
        f32 = lambda: ar.alloc(64, (S,), F32)
        Q, Kt, V, G, COS, SIN, T1, T2, KZ, KV, G64, SS = [f32() for _ in range(12)]
        QH = ar.alloc(64, (S,), BF16)
        KH = ar.alloc(64, (S,), BF16)
        QX = ar.alloc(64, (S,), BF16)
        KZT = ar.alloc(64, (32, 64), BF16)
        VT = ar.alloc(64, (32, 64), BF16)
        SB = ar.alloc(64, (33, 64), BF16)
        SC = [ar.alloc(64, (512,), BF16) for _ in range(2)]
        OS = ar.alloc(64, (512,), F32)
        CEN = ar.alloc(64, (512,), F32)
        SQ_ = ar.alloc(64, (512,), F32)
        RSD = ar.alloc(64, (512,), F32)
        SG = ar.alloc(64, (512,), F32)
        YO = [ar.alloc(64, (512,), BF16) for _ in range(2)]
        ones64 = self.CST[0:64, K_B64:K_B64 + 64]
        v3 = lambda t_: t_.v(t_.t.rearrange("p (n i) -> p n i", i=64))
        for b in range(BL):
            self.load_seq_rows(COS, self.TAB["COSA"], 0, 64, b)
            self.load_seq_rows(SIN, self.TAB["SINA"], 0, 64, b)
            for h in range(4):
                self.load_seq_rows(Q, self.ZT, 0 + h * 64, 64, b)
                self.load_seq_rows(Kt, self.ZT, 256 + h * 64, 64, b)
                self.load_seq_rows(V, self.ZT, 512 + h * 64, 64, b)
                self.load_seq_rows(G, self.ZT, 768 + h * 64, 64, b)
                xi = self.CST.v(self.CST.t[0:64, K_XI + 64 * h:K_XI + 64 * h + 64].unsqueeze(1).to_broadcast([64, 32, 64]))
                ze = self.CST.v(self.CST.t[0:64, K_ZETA + 64 * h:K_ZETA + 64 * h + 64].unsqueeze(1).to_broadcast([64, 32, 64]))
                dm = self.CST.v(self.CST.t[0:64, K_DMASK + 64 * h:K_DMASK + 64 * h + 64].unsqueeze(1).to_broadcast([64, 8, 64]))
                self.rope64(T1, Q, COS, SIN, T2, K_ROTA, 64)
                c.dve.tensor_scalar(out=QH[:, :], in0=T1[:, :], scalar1=0.125, scalar2=None, op0=ALU.mult)
                c.pool.tensor_tensor(out=v3(QX), in0=v3(T1), in1=xi, op=ALU.mult)
                self.rope64(T1, Kt, COS, SIN, T2, K_ROTA, 64)
                c.dve.tensor_copy(out=KH[:, :], in_=T1[:, :])
                c.pool.tensor_tensor(out=v3(KZ), in0=v3(T1), in1=ze, op=ALU.mult)
                self.tok_major(KZT, KZ)
                self.tok_major(VT, V)
                kv3 = KV.t.rearrange("p (e n) -> p e n", e=64)
                for g in range(4):
                    p = self.ps()
                    for s_ in range(8):
                        n = g * 8 + s_
                        c.pe.matmul(out=p[0:64, s_ * 64:s_ * 64 + 64], lhsT=KZT[:, n, :], rhs=VT[:, n, :], start=True, stop=True)
                    c.dve.tensor_copy(out=KV.v(kv3[:, :, g * 8:g * 8 + 8]),
                                      in_=p.v(p.t[0:64, :].rearrange("p (s e) -> p e s", s=8)))
                g64 = math.exp(64.0 * math.log1p(-2.0 ** (-5.0 - h)))
                c.pool.memset(ap=G64[:, :], constant=g64)
                c.pool.memset(ap=G64.v(G64.t.rearrange("p (e n) -> p e n", e=64)[:, :, 0:1]), constant=0.0)
                c.dve.tensor_tensor_scan(out=SS[:, :], data0=G64[:, :], data1=KV[:, :], initial=0.0, op0=ALU.mult, op1=ALU.add)
                c.pool.memset(ap=SB[:, 0, :], constant=0.0)
                c.pool.tensor_copy(out=SB[:, 1:33, :], in_=SS.v(SS.t.rearrange("p (e n) -> p n e", e=64)))
                for g in range(4):
                    sl = slice(g * 512, (g + 1) * 512)
                    pS = self.ps()
                    for s_ in range(8):
                        n = g * 8 + s_
                        cs = slice(n * 64, n * 64 + 64)
                        c.pe.matmul(out=pS[0:64, s_ * 64:s_ * 64 + 64], lhsT=KH[:, cs], rhs=QH[:, cs], start=True, stop=True)
                    sc = SC[g % 2]
                    c.dve.tensor_tensor(out=sc.v(sc.t.rearrange("p (s i) -> p s i", s=8)),
                                        in0=pS.v(pS.t[0:64, :].rearrange("p (s i) -> p s i", s=8)), in1=dm, op=ALU.mult)
                    pO = self.ps()
                    for s_ in range(8):
                        n = g * 8 + s_
                        cs = slice(n * 64, n * 64 + 64)
                        c.pe.matmul(out=pO[0:64, s_ * 64:s_ * 64 + 64], lhsT=VT[:, n, :], rhs=sc[:, s_ * 64:s_ * 64 + 64],
                                    start=True, stop=False)
                        c.pe.matmul(out=pO[0:64, s_ * 64:s_ * 64 + 64], lhsT=SB[:, n, :], rhs=QX[:, cs], start=False, stop=True)
                    c.act.activation(out=OS[:, :], in_=pO[0:64, :], func=AF.Copy)
                    p2 = self.ps()
                    c.pe.matmul(out=p2[0:64, :], lhsT=ones64, rhs=OS[:, :], start=True, stop=True)
                    c.dve.scalar_tensor_tensor(out=CEN[:, :], in0=p2[0:64, :], scalar=-1.0 / 64, in1=OS[:, :], op0=ALU.mult, op1=ALU.add)
                    c.act.activation(out=SQ_[:, :], in_=CEN[:, :], func=AF.Square)
                    p3 = self.ps()
                    c.pe.matmul(out=p3[0:64, :], lhsT=ones64, rhs=SQ_[:, :], start=True, stop=True)
                    c.act.activation(out=RSD[:, :], in_=p3[0:64, :], func=AF.Ln, scale=1.0 / 64, bias=EPS)
                    c.act.activation(out=RSD[:, :], in_=RSD[:, :], func=AF.Exp, scale=-0.5)
                    c.act.activation(out=SG[:, :], in_=G[:, sl], func=AF.Silu)
                    c.dve.tensor_tensor(out=CEN[:, :], in0=CEN[:, :], in1=RSD[:, :], op=ALU.mult)
                    yo = YO[g % 2]
                    c.dve.scalar_tensor_tensor(out=yo[:, :], in0=CEN[:, :], scalar=self.col(l, C_RGN + h, 64), in1=SG[:, :],
                                               op0=ALU.mult, op1=ALU.mult)
                    c.pool.dma_start(out=self.YT.v(h * 64, 64, b * 4 + g), in_=yo[:, :])

    def phase_mla(self, l):
        c, ar = self.c, self.ar
        c.barrier()
        ar.reset()
        SCALE = 96.0 ** -0.5
        WUQ = ar.alloc(128, (2, 384), BF16)
        WUKV = ar.alloc(128, (512,), BF16)
        STG = ar.alloc(128, (512,), F32)
        wq = self.d_in["mla_w_uq"]
        c.sp.dma_start(out=STG[:, 0:384], in_=wq[l, 0:128, :])
        c.dve.tensor_copy(out=WUQ[:, 0, :], in_=STG[:, 0:384])
        c.sp.dma_start(out=STG[0:64, 0:384], in_=wq[l, 128:192, :])
        c.dve.tensor_copy(out=WUQ[0:64, 1, :], in_=STG[0:64, 0:384])
        c.sp.dma_start(out=STG[:, :], in_=self.d_in["mla_w_ukv"][l, :, :])
        c.dve.tensor_copy(out=WUKV[:, :], in_=STG[:, :])
        QA = ar.alloc(96, (4, S), BF16)
        KA = ar.alloc(96, (4, S), BF16)
        VTK = ar.alloc(128, (16, 4 * 65), BF16)
        vtk4 = lambda t0, t1: VTK.v(VTK.t[:, t0:t1, :].rearrange("p t (h x) -> p t h x", h=4))
        CQ = ar.alloc(128, (2, 512), F32)
        CKV = ar.alloc(128, (512,), F32)
        KRR = ar.alloc(96, (512,), F32)
        COS = ar.alloc(96, (512,), F32)
        SIN = ar.alloc(96, (512,), F32)
        SQ = ar.alloc(128, (2, 512), BF16)
        SQK = ar.alloc(128, (512,), BF16)
        RSQ = ar.alloc(128, (512,), F32)
        RSK = ar.alloc(128, (512,), F32)
        RC = ar.alloc(128, (4,), F32)
        CQG = ar.alloc(128, (2, 512), BF16)
        CKVG = ar.alloc(128, (512,), BF16)
        QRF = ar.alloc(96, (512,), F32)
        T1 = ar.alloc(96, (512,), F32)
        T2 = ar.alloc(96, (512,), F32)
        PT = [ar.alloc(128, (512,), BF16) for _ in range(4)]
        RROW = ar.alloc(65, (512,), F32)
        RSM = ar.alloc(64, (512,), F32)
        c.pool.memset(ap=VTK[:, :, :], constant=1.0)
        YC = [ar.alloc(64, (512,), BF16) for _ in range(2)]
        rotc = self.CST[64:96, K_ROTC + 64:K_ROTC + 96]
        ones_f = self.CST[:, K_ONES:K_ONES + 64]
        lg = self.lru_gen(l)
        R = slice(64, 96)

        def rope32(dst_views, src):
            p2 = self.ps()
            c.pe.matmul(out=p2[R, :], lhsT=rotc, rhs=src[R, :], start=True, stop=True)
            c.dve.tensor_tensor(out=T2[R, :], in0=p2[R, :], in1=SIN[R, :], op=ALU.mult)
            c.pool.tensor_tensor(out=T1[R, :], in0=src[R, :], in1=COS[R, :], op=ALU.mult)
            for i, dv in enumerate(dst_views):
                (c.dve if i % 2 == 0 else c.pool).tensor_tensor(out=dv, in0=T1[R, :], in1=T2[R, :], op=ALU.add)

        pti = 0
        for b in range(BL):
            for j in range(4):
                tb = b * 4 + j
                tk = slice(j * 512, (j + 1) * 512)
                c.sp.dma_start(out=CQ[:, 0, :], in_=self.ZT.v(1536, 128, tb))
                c.sp.dma_start(out=CQ[0:64, 1, :], in_=self.ZT.v(1664, 64, tb))
                c.sp.dma_start(out=CKV[:, :], in_=self.ZT.v(1728, 128, tb))
                c.sp.dma_start(out=KRR[R, :], in_=self.ZT.v(1856, 32, tb))
                c.sp.dma_start(out=COS[R, :], in_=self.TAB["COSC"].v(64, 32, tb))
                c.sp.dma_start(out=SIN[R, :], in_=self.TAB["SINC"].v(64, 32, tb))
                c.act.activation(out=SQ[:, 0, :], in_=CQ[:, 0, :], func=AF.Square)
                c.act.activation(out=SQ[0:64, 1, :], in_=CQ[0:64, 1, :], func=AF.Square)
                p = self.ps()
                c.pe.matmul(out=p[:, :], lhsT=self.ONESB[:, :], rhs=SQ[:, 0, :], start=True, stop=False)
                c.pe.matmul(out=p[:, :], lhsT=self.ONESB[0:64, :], rhs=SQ[0:64, 1, :], start=False, stop=True)
                c.act.activation(out=RSQ[:, :], in_=p[:, :], func=AF.Ln, scale=1.0 / 192, bias=EPS)
                c.act.activation(out=RSQ[:, :], in_=RSQ[:, :], func=AF.Exp, scale=-0.5)
                c.dve.tensor_scalar(out=CQG[:, 0, :], in0=CQ[:, 0, :], scalar1=self.col(l, C_QN), scalar2=None, op0=ALU.mult)
                c.dve.tensor_scalar(out=CQG[0:64, 1, :], in0=CQ[0:64, 1, :], scalar1=self.col(l, C_QN + 1, 64), scalar2=None, op0=ALU.mult)
                for h in range(4):
                    p = self.ps()
                    c.pe.matmul(out=p[0:64, :], lhsT=WUQ[:, 0, h * 96:h * 96 + 64], rhs=CQG[:, 0, :], start=True, stop=False)
                    c.pe.matmul(out=p[0:64, :], lhsT=WUQ[0:64, 1, h * 96:h * 96 + 64], rhs=CQG[0:64, 1, :], start=False, stop=True)
                    c.dve.scalar_tensor_tensor(out=QA[0:64, h, tk], in0=p[0:64, :], scalar=SCALE, in1=RSQ[0:64, :], op0=ALU.mult, op1=ALU.mult)
                    p = self.ps()
                    c.pe.matmul(out=p[R, :], lhsT=WUQ[:, 0, h * 96 + 64:h * 96 + 96], rhs=CQG[:, 0, :], start=True, stop=False)
                    c.pe.matmul(out=p[R, :], lhsT=WUQ[0:64, 1, h * 96 + 64:h * 96 + 96], rhs=CQG[0:64, 1, :], start=False, stop=True)
                    c.dve.scalar_tensor_tensor(out=QRF[R, :], in0=p[R, :], scalar=SCALE, in1=RSQ[R, :], op0=ALU.mult, op1=ALU.mult)
                    rope32([QA[R, h, tk]], QRF)
                c.act.activation(out=SQK[:, :], in_=CKV[:, :], func=AF.Square)
                p = self.ps()
                c.pe.matmul(out=p[:, :], lhsT=self.ONESB[:, :], rhs=SQK[:, :], start=True, stop=True)
                c.act.activation(out=RSK[:, :], in_=p[:, :], func=AF.Ln, scale=1.0 / 128, bias=EPS)
                c.act.activation(out=RSK[:, :], in_=RSK[:, :], func=AF.Exp, scale=-0.5)
                c.dve.tensor_scalar(out=CKVG[:, :], in0=CKV[:, :], scalar1=self.col(l, C_KVN), scalar2=None, op0=ALU.mult)
                for h in range(4):
                    p = self.ps()
                    c.pe.matmul(out=p[0:64, :], lhsT=WUKV[:, h * 128:h * 128 + 64], rhs=CKVG[:, :], start=True, stop=True)
                    c.dve.tensor_tensor(out=KA[0:64, h, tk], in0=p[0:64, :], in1=RSK[0:64, :], op=ALU.mult)
                pc = self.ps()
                for sub in range(4):
                    c.pe.matmul(out=pc[:, sub:sub + 1], lhsT=SQK[:, sub * 128:sub * 128 + 128], rhs=self.ONESB[:, 0:1], start=True, stop=True)
                c.act.activation(out=RC[:, :], in_=pc[:, 0:4], func=AF.Ln, scale=1.0 / 128, bias=EPS)
                c.act.activation(out=RC[:, :], in_=RC[:, :], func=AF.Exp, scale=-0.5)
                wv = WUKV.v(WUKV.t.rearrange("p (h x) -> p h x", h=4)[:, :, 64:128])
                for sub in range(4):
                    p = self.ps()
                    c.pe.matmul(out=p.v(p.t[:, 0:256].rearrange("p (h e) -> p h e", h=4)), lhsT=CKVG[:, sub * 128:sub * 128 + 128], rhs=wv, start=True, stop=True)
                    c.dve.tensor_scalar(out=VTK.v(VTK.t[:, j * 4 + sub, :].rearrange("p (h x) -> p h x", h=4)[:, :, 0:64]),
                                        in0=p.v(p.t[:, 0:256].rearrange("p (h e) -> p h e", h=4)), scalar1=RC[:, sub:sub + 1], scalar2=None, op0=ALU.mult)
                rope32([KA[R, h, tk] for h in range(4)], KRR)
            items = [(h, qb, kt) for h in range(4) for qb in range(4) for kt in range(4 * qb + 4)]

            def score(idx):
                nonlocal pti
                h, qb, kt = items[idx]
                jd = kt - 4 * qb
                q0 = 128 * jd if jd > 0 else 0
                n = 512 - q0
                ks = slice(kt * 128, kt * 128 + 128)
                qs = slice(qb * 512 + q0, qb * 512 + 512)
                pT = self.psl[3 + pti % 4]
                pt = PT[pti % 4]
                pti += 1
                c.pe.matmul(out=pT[:, 0:n], lhsT=KA[:, h, ks], rhs=QA[:, h, qs], start=True, stop=True)
                c.act.activation(out=pt[:, 0:n], in_=pT[:, 0:n], func=AF.Exp)
                if jd >= 0:
                    c.pool.memset(ap=pt[64:128, 0:64], constant=0.0)
                return pt, q0, n

            LA = 2
            pend = [score(i) for i in range(LA)]
            it = 0
            for idx, (h, qb, kt) in enumerate(items):
                pt, q0, n = pend.pop(0)
                if idx + LA < len(items):
                    pend.append(score(idx + LA))
                if idx % 3 == 0:
                    next(lg, None)
                nkt = 4 * qb + 4
                pO = self.psl[it % 2]
                c.pe.matmul(out=pO[0:65, q0:512], lhsT=VTK[:, kt, h * 65:h * 65 + 65], rhs=pt[:, 0:n], start=(kt == 0), stop=(kt == nkt - 1))
                if kt == nkt - 1:
                    c.act.activation(out=RROW[64:65, :], in_=pO[64:65, :], func=AF.Ln)
                    c.act.activation(out=RROW[64:65, :], in_=RROW[64:65, :], func=AF.Exp, scale=-1.0)
                    pM = self.psl[2]
                    c.pe.matmul(out=pM[0:64, :], lhsT=self.CST[64:65, K_ONES:K_ONES + 64], rhs=RROW[64:65, :], start=True, stop=True)
                    c.act.activation(out=RSM[:, :], in_=pM[0:64, :], func=AF.Copy)
                    yc = YC[it % 2]
                    c.dve.tensor_tensor(out=yc[:, :], in0=pO[0:64, :], in1=RSM[:, :], op=ALU.mult)
                    c.pool.dma_start(out=self.YT.v(512 + h * 64, 64, b * 4 + qb), in_=yc[:, :])
                    it += 1
        for _ in lg:
            pass

    def phase_gdn(self, l):
        c, ar = self.c, self.ar
        c.barrier()
        ar.reset()
        KBG = [ar.alloc(64, (S,), BF16) for _ in range(4)]
        QD = [ar.alloc(64, (S,), BF16) for _ in range(4)]
        TMt = [ar.alloc(64, (32, 64), BF16) for _ in range(4)]
        ATT = [ar.alloc(64, (32, 64), BF16) for _ in range(4)]
        KD = [ar.alloc(64, (32, 64), BF16) for _ in range(4)]
        VB = ar.alloc(64, (4, 32, 64), BF16)
        EGL = ar.alloc(64, (4, 32), F32)
        X, TT, Kf, Qf, Vf, BETB, GCB = [ar.alloc(64, (S,), F32) for _ in range(7)]
        Kb = ar.alloc(64, (S,), BF16)
        Qb = ar.alloc(64, (S,), BF16)
        CS = ar.alloc(64, (S,), BF16)
        GCc = ar.alloc(64, (32,), F32)
        BTc = ar.alloc(64, (32,), F32)
        SCOL = ar.alloc(64, (4,), F32)
        g3 = lambda: ar.alloc(64, (8, 64), F32)
        D1, DX, EUi, EUs, EL, U_, L_, Pa, Qa, R1, R2, Lof = [g3() for _ in range(12)]
        TG, Pb, Qb2, Rd, Rd2, Ysb = DX, U_, L_, R1, R2, D1
        St = ar.alloc(64, (4, 64), F32)
        Sb = ar.alloc(64, (4, 64), BF16)
        RH = ar.alloc(64, (4, 64), BF16)
        VNb = ar.alloc(64, (4, 64), BF16)
        GT = Sub(X, X.t.rearrange("p (h t) -> p h t", h=4))
        OS, SQ_, RSD, SG = [Sub(TT, TT.t[:, i * 512:(i + 1) * 512]) for i in range(4)]
        YO = [ar.alloc(64, (512,), BF16) for _ in range(2)]
        ones64 = self.CST[0:64, K_B64:K_B64 + 64]
        id64 = self.CST[0:64, K_ID:K_ID + 64]
        cbc = lambda k0: self.CST.v(self.CST.t[0:64, k0:k0 + 64].unsqueeze(1).to_broadcast([64, 8, 64]))
        v3 = lambda t_, sl: t_.v(t_.t[:, sl].rearrange("p (n i) -> p n i", i=64))
        p3 = lambda p: p.v(p.t[0:64, :].rearrange("p (s i) -> p s i", s=8))
        c.pool.memset(ap=CS[:, :], constant=1.0)
        c.pool.memset(ap=CS.v(CS.t.rearrange("p (n i) -> p n i", i=64)[:, :, 0:1]), constant=0.0)

        def mm8(pt, lhs_fn, rhs_fn):
            for s_ in range(8):
                c.pe.matmul(out=pt[0:64, s_ * 64:s_ * 64 + 64], lhsT=lhs_fn(s_), rhs=rhs_fn(s_), start=True, stop=True)

        for b in range(BL):
            for h in range(4):
                for j in range(4):
                    c.sp.dma_start(out=BETB[:, j * 512:(j + 1) * 512], in_=self.ZT.vb(2912 + h, 64, b * 4 + j))
                    c.sp.dma_start(out=GCB[:, j * 512:(j + 1) * 512], in_=self.ZT.vb(2916 + h, 64, b * 4 + j))
                c.act.activation(out=BETB[:, :], in_=BETB[:, :], func=AF.Sigmoid)
                c.dve.tensor_scalar(out=X[:, :], in0=GCB[:, :], scalar1=self.col(l, C_DTB + h, 64), scalar2=None, op0=ALU.add)
                c.dve.tensor_scalar(out=TT[:, :], in0=X[:, :], scalar1=-1.0, scalar2=None, op0=ALU.mult)
                c.dve.tensor_tensor(out=TT[:, :], in0=TT[:, :], in1=X[:, :], op=ALU.max)
                c.act.activation(out=TT[:, :], in_=TT[:, :], func=AF.Exp, scale=-1.0)
                c.act.activation(out=TT[:, :], in_=TT[:, :], func=AF.Ln, bias=1.0)
                c.dve.tensor_scalar(out=X[:, :], in0=X[:, :], scalar1=0.0, scalar2=None, op0=ALU.max)
                c.dve.tensor_tensor(out=X[:, :], in0=X[:, :], in1=TT[:, :], op=ALU.add)
                c.act.activation(out=SCOL[:, 0:1], in_=self.col(l, C_ALOG + h, 64), func=AF.Exp)
                c.dve.tensor_scalar(out=SCOL[:, 1:2], in0=SCOL[:, 0:1], scalar1=-1.0, scalar2=None, op0=ALU.mult)
                c.dve.tensor_scalar(out=X[:, :], in0=X[:, :], scalar1=SCOL[:, 1:2], scalar2=None, op0=ALU.mult)
                c.dve.tensor_tensor_scan(out=GCB[:, :], data0=CS[:, :], data1=X[:, :], initial=0.0, op0=ALU.mult, op1=ALU.add)
                for part, dst in ((0, Qf), (1, Kf), (2, Vf)):
                    self.load_seq_rows(X, self.ZT, 1888 + part * 256 + h * 64, 64, b)
                    self.conv4(TT, X, l, C_GCW + (part * 4 + h) * 4, 64)
                    c.act.activation(out=dst[:, :], in_=TT[:, :], func=AF.Silu)
                    if part < 2:
                        c.act.activation(out=TT[:, :], in_=dst[:, :], func=AF.Square)
                        for j in range(4):
                            sl = slice(j * 512, (j + 1) * 512)
                            p = self.ps()
                            c.pe.matmul(out=p[0:64, :], lhsT=ones64, rhs=TT[:, sl], start=True, stop=True)
                            c.act.activation(out=X[:, sl], in_=p[0:64, :], func=AF.Ln, bias=EPS)
                        c.act.activation(out=X[:, :], in_=X[:, :], func=AF.Exp, scale=-0.5)
                        if part == 0:
                            c.dve.scalar_tensor_tensor(out=dst[:, :], in0=dst[:, :], scalar=0.125, in1=X[:, :], op0=ALU.mult, op1=ALU.mult)
                        else:
                            c.dve.tensor_tensor(out=dst[:, :], in0=dst[:, :], in1=X[:, :], op=ALU.mult)
                c.pool.tensor_copy(out=Kb[:, :], in_=Kf[:, :])
                c.pool.tensor_copy(out=Qb[:, :], in_=Qf[:, :])
                c.act.activation(out=TT[:, :], in_=GCB[:, :], func=AF.Exp)
                c.pool.tensor_tensor(out=X[:, :], in0=BETB[:, :], in1=TT[:, :], op=ALU.mult)
                c.dve.tensor_tensor(out=KBG[h][:, :], in0=Kf[:, :], in1=X[:, :], op=ALU.mult)
                c.dve.tensor_tensor(out=QD[h][:, :], in0=Qf[:, :], in1=TT[:, :], op=ALU.mult)
                c.pool.tensor_copy(out=EGL[:, h, :], in_=TT.v(TT.t.rearrange("p (n i) -> p n i", i=64)[:, :, 63]))
                c.pool.tensor_copy(out=X[0:32, :], in_=GCB[0:32, :])
                c.pool.tensor_copy(out=X[32:64, :], in_=BETB[32:64, :])
                for g in range(4):
                    p = self.ps()
                    for s_ in range(8):
                        n = g * 8 + s_
                        c.pe.transpose(out=p[0:64, s_ * 64:s_ * 64 + 64], in_=X[0:64, n * 64:n * 64 + 64], identity=id64)
                    pv = p.t[0:64, :].rearrange("p (s i) -> p s i", s=8)
                    c.dve.tensor_copy(out=GCc[:, g * 8:g * 8 + 8], in_=p.v(pv[:, :, 0]))
                    c.dve.tensor_copy(out=BTc[:, g * 8:g * 8 + 8], in_=p.v(pv[:, :, 32]))
                for g in range(4):
                    tk = slice(g * 512, (g + 1) * 512)
                    ns = slice(g * 8, g * 8 + 8)
                    cs = lambda s_: slice((g * 8 + s_) * 64, (g * 8 + s_) * 64 + 64)
                    gcc_b = GCc.v(GCc.t[:, ns].unsqueeze(2).to_broadcast([64, 8, 64]))
                    btc_b = BTc.v(BTc.t[:, ns].unsqueeze(2).to_broadcast([64, 8, 64]))
                    pG = self.ps()
                    mm8(pG, lambda s_: Kb[:, cs(s_)], lambda s_: Kb[:, cs(s_)])
                    pQ = self.ps()
                    mm8(pQ, lambda s_: Kb[:, cs(s_)], lambda s_: Qb[:, cs(s_)])
                    c.dve.tensor_tensor(out=D1[:, :, :], in0=v3(GCB, tk), in1=gcc_b, op=ALU.subtract)
                    c.pool.tensor_tensor(out=DX[:, :, :], in0=D1[:, :, :], in1=cbc(K_NMGE), op=ALU.add)
                    c.act.activation(out=EUi[:, :, :], in_=DX[:, :, :], func=AF.Exp)
                    c.pool.tensor_tensor(out=DX[:, :, :], in0=D1[:, :, :], in1=cbc(K_NMGT), op=ALU.add)
                    c.act.activation(out=EUs[:, :, :], in_=DX[:, :, :], func=AF.Exp)
                    c.dve.scalar_tensor_tensor(out=DX[:, :, :], in0=D1[:, :, :], scalar=-1.0, in1=cbc(K_NMLT), op0=ALU.mult, op1=ALU.add)
                    c.act.activation(out=EL[:, :, :], in_=DX[:, :, :], func=AF.Exp)
                    c.dve.tensor_tensor(out=ATT[h][:, ns, :], in0=p3(pQ), in1=EUi[:, :, :], op=ALU.mult)
                    c.pool.tensor_tensor(out=TG[:, :, :], in0=v3(BETB, tk), in1=EUs[:, :, :], op=ALU.mult)
                    c.dve.tensor_tensor(out=U_[:, :, :], in0=p3(pG), in1=TG[:, :, :], op=ALU.mult)
                    c.pool.tensor_tensor(out=TG[:, :, :], in0=EL[:, :, :], in1=btc_b, op=ALU.mult)
                    c.dve.tensor_tensor(out=L_[:, :, :], in0=p3(pG), in1=TG[:, :, :], op=ALU.mult)
                    pK = self.ps()
                    for s_ in range(8):
                        c.pe.transpose(out=pK[0:64, s_ * 64:s_ * 64 + 64], in_=Kf[0:64, cs(s_)], identity=id64)
                    c.dve.tensor_tensor(out=KD[h][:, ns, :], in0=p3(pK),
                                        in1=EUi.v(EUi.t[:, :, 63:64].to_broadcast([64, 8, 64])), op=ALU.mult)
                    pV = self.ps()
                    for s_ in range(8):
                        c.pe.transpose(out=pV[0:64, s_ * 64:s_ * 64 + 64], in_=Vf[0:64, cs(s_)], identity=id64)
                    c.dve.tensor_tensor(out=VB[:, h, ns, :], in0=p3(pV), in1=btc_b, op=ALU.mult)
                    c.dve.scalar_tensor_tensor(out=R1[:, :, :], in0=U_[:, :, :], scalar=-1.0, in1=cbc(K_ID), op0=ALU.mult, op1=ALU.add)
                    c.dve.scalar_tensor_tensor(out=R2[:, :, :], in0=L_[:, :, :], scalar=-1.0, in1=cbc(K_ID), op0=ALU.mult, op1=ALU.add)
                    c.pool.tensor_tensor(out=Lof[:, :, :], in0=L_[:, :, :], in1=cbc(K_OFFL), op=ALU.mult)
                    P_, Q_ = U_, L_
                    for lev in range(4):
                        Pn, Qn = (Pa, Qa) if lev % 2 == 0 else (Pb, Qb2)
                        pP = self.ps()
                        mm8(pP, lambda s_: Q_[:, s_, :], lambda s_: P_[:, s_, :])
                        pQ2 = self.ps()
                        mm8(pQ2, lambda s_: P_[:, s_, :], lambda s_: Q_[:, s_, :])
                        c.act.activation(out=Pn[:, :, :], in_=p3(pP), func=AF.Copy)
                        c.dve.tensor_copy(out=Qn[:, :, :], in_=p3(pQ2))
                        pR = self.ps()
                        mm8(pR, lambda s_: Qn[:, s_, :], lambda s_: R1[:, s_, :])
                        pR2 = self.ps()
                        mm8(pR2, lambda s_: Pn[:, s_, :], lambda s_: R2[:, s_, :])
                        c.dve.tensor_tensor(out=R1[:, :, :], in0=p3(pR), in1=R1[:, :, :], op=ALU.add)
                        c.dve.tensor_tensor(out=R2[:, :, :], in0=p3(pR2), in1=R2[:, :, :], op=ALU.add)
                        P_, Q_ = Pn, Qn
                    c.pool.tensor_tensor(out=Rd[:, :, :], in0=R1[:, :, :], in1=cbc(K_DIAG), op=ALU.mult)
                    c.pool.tensor_tensor(out=Rd2[:, :, :], in0=R2[:, :, :], in1=cbc(K_DIAG), op=ALU.mult)
                    pY = self.ps()
                    mm8(pY, lambda s_: Lof[:, s_, :], lambda s_: Rd[:, s_, :])
                    c.act.activation(out=Ysb[:, :, :], in_=p3(pY), func=AF.Copy)
                    pX = self.ps()
                    mm8(pX, lambda s_: Rd2[:, s_, :], lambda s_: Ysb[:, s_, :])
                    c.dve.tensor_tensor(out=TMt[h][:, ns, :], in0=Rd[:, :, :], in1=p3(pX), op=ALU.subtract)
            c.pool.memset(ap=St[:, :, :], constant=0.0)
            c.pool.memset(ap=Sb[:, :, :], constant=0.0)
            pO = self.psl[0:4]
            pP1, pVN, pSN, p7 = self.psl[4], self.psl[5], self.psl[6], self.psl[7]
            p4 = lambda p: p.v(p.t[0:64, 0:256].rearrange("p (h e) -> p h e", h=4))
            for n in range(32):
                s_, g = n % 8, n // 8
                cs = slice(n * 64, n * 64 + 64)
                if s_ == 0:
                    for h in range(4):
                        c.sp.dma_start(out=GT[:, h, :], in_=self.ZT.v(2656 + h * 64, 64, b * 4 + g))
                for h in range(4):
                    c.pe.matmul(out=pP1[0:64, h * 64:h * 64 + 64], lhsT=KBG[h][:, cs], rhs=Sb[:, h, :], start=True, stop=True)
                c.dve.tensor_tensor(out=RH[:, :, :], in0=VB[:, :, n, :], in1=p4(pP1), op=ALU.subtract)
                for h in range(4):
                    c.pe.matmul(out=pVN[0:64, h * 64:h * 64 + 64], lhsT=TMt[h][:, n, :], rhs=RH[:, h, :], start=True, stop=True)
                c.act.activation(out=VNb[:, :, :], in_=p4(pVN), func=AF.Copy)
                for h in range(4):
                    c.pe.matmul(out=pO[h][0:64, s_ * 64:s_ * 64 + 64], lhsT=Sb[:, h, :], rhs=QD[h][:, cs], start=True, stop=False)
                    c.pe.matmul(out=pO[h][0:64, s_ * 64:s_ * 64 + 64], lhsT=VNb[:, h, :], rhs=ATT[h][:, n, :], start=False, stop=True)
                for h in range(4):
                    c.pe.matmul(out=pSN[0:64, h * 64:h * 64 + 64], lhsT=KD[h][:, n, :], rhs=VNb[:, h, :], start=True, stop=True)
                for h in range(4):
                    c.dve.scalar_tensor_tensor(out=St[:, h, :], in0=St[:, h, :], scalar=EGL[:, h, n:n + 1],
                                               in1=pSN[0:64, h * 64:h * 64 + 64], op0=ALU.mult, op1=ALU.add)
                c.pool.tensor_copy(out=Sb[:, :, :], in_=St[:, :, :])
                if s_ == 7:
                    for h in range(4):
                        c.act.activation(out=OS[:, :], in_=pO[h][0:64, :], func=AF.Copy)
                        c.act.activation(out=SQ_[:, :], in_=OS[:, :], func=AF.Square)
                        c.pe.matmul(out=p7[0:64, :], lhsT=ones64, rhs=SQ_[:, :], start=True, stop=True)
                        c.act.activation(out=RSD[:, :], in_=p7[0:64, :], func=AF.Ln, scale=1.0 / 64, bias=EPS)
                        c.act.activation(out=RSD[:, :], in_=RSD[:, :], func=AF.Exp, scale=-0.5)
                        c.act.activation(out=SG[:, :], in_=GT[:, h, :], func=AF.Silu)
                        c.dve.tensor_tensor(out=OS[:, :], in0=OS[:, :], in1=RSD[:, :], op=ALU.mult)
                        yo = YO[h % 2]
                        c.dve.scalar_tensor_tensor(out=yo[:, :], in0=OS[:, :], scalar=self.col(l, C_GNORM, 64), in1=SG[:, :],
                                                   op0=ALU.mult, op1=ALU.mult)
                        c.pool.dma_start(out=self.YT.v(768 + h * 64, 64, b * 4 + g), in_=yo[:, :])

    def phase_wout(self, l, XSRC):
        c, ar = self.c, self.ar
        c.barrier()
        ar.reset()
        W = ar.alloc(128, (8, DM), BF16)
        stg = [ar.alloc(128, (DM,), F32) for _ in range(2)]
        wo = self.d_in["w_out"]
        self.load_w_bf16(W, wo, lambda k: wo.t[l, k * 128:(k + 1) * 128, :], 8, DM, stg)
        Y = [ar.alloc(128, (8, 512), BF16) for _ in range(2)]
        X32 = [ar.alloc(128, (8, 512), F32) for _ in range(2)]
        XO = [ar.alloc(128, (512,), F32) for _ in range(4)]
        for tb in range(T // 512):
            y, x = Y[tb % 2], X32[tb % 2]
            for k in range(8):
                c.sp.dma_start(out=y[:, k, :], in_=self.YT.v(k * 128, 128, tb))
                c.sp.dma_start(out=x[:, k, :], in_=XSRC.v(k * 128, 128, tb))
            for cc in range(8):
                p = self.ps()
                for k in range(8):
                    c.pe.matmul(out=p[:, :], lhsT=W[:, k, cc * 128:cc * 128 + 128], rhs=y[:, k, :], start=(k == 0), stop=(k == 7))
                xo = XO[cc % 4]
                c.dve.tensor_tensor(out=xo[:, :], in0=p[:, :], in1=x[:, cc, :], op=ALU.add)
                c.pool.dma_start(out=self.XT.v(cc * 128, 128, tb), in_=xo[:, :])

    def phase_xattn(self, l, XSRC):
        c, ar = self.c, self.ar
        c.barrier()
        ar.reset()
        KT = ar.alloc(128, (8, TM_), BF16)
        VT = ar.alloc(128, (4, DM), BF16)
        WM = ar.alloc(128, (8, DM), BF16)
        WQ = ar.alloc(128, (8, DM), BF16)
        WO = ar.alloc(128, (8, DM), BF16)
        stgB = [ar.alloc(128, (DM,), F32) for _ in range(2)]
        mark = ar.off
        WKV = ar.alloc(128, (8, 2 * DM), BF16)
        stg = [ar.alloc(128, (2 * DM,), F32) for _ in range(2)]
        wkv = self.d_in["xa_wkv"]
        M32 = ar.alloc(128, (8, TM_), F32)
        SQ = ar.alloc(128, (8, TM_), BF16)
        MG = ar.alloc(128, (8, TM_), BF16)
        RS = ar.alloc(128, (TM_,), F32)
        RC = ar.alloc(128, (4,), F32)
        mt_ = self.d_in["memT"]
        for k in range(8):
            c.sp.dma_start(out=M32[:, k, :], in_=mt_[k * 128:(k + 1) * 128, :])
        self.load_w_bf16(WKV, wkv, lambda k: wkv.t[l, k * 128:(k + 1) * 128, :], 8, 2 * DM, stg)
        wm, wq, wo = self.d_in["w_out"], self.d_in["xa_wq"], self.d_in["xa_wo"]
        self.load_w_bf16(WM, wm, lambda k: wm.t[l, k * 128:(k + 1) * 128, :], 8, DM, stgB)
        self.load_w_bf16(WQ, wq, lambda k: wq.t[l, k * 128:(k + 1) * 128, :], 8, DM, stgB)
        self.load_w_bf16(WO, wo, lambda k: wo.t[l, k * 128:(k + 1) * 128, :], 8, DM, stgB)
        self.rms_stats(M32, 8, TM_, SQ, RS)
        for k in range(8):
            c.dve.tensor_scalar(out=MG[:, k, :], in0=M32[:, k, :], scalar1=self.col(l, C_NMEM + k), scalar2=None, op0=ALU.mult)
        for cc in range(8):
            p = self.ps()
            for k in range(8):
                c.pe.matmul(out=p[:, :], lhsT=WKV[:, k, cc * 128:cc * 128 + 128], rhs=MG[:, k, :], start=(k == 0), stop=(k == 7))
            c.dve.tensor_tensor(out=KT[:, cc, :], in0=p[:, :], in1=RS[:, :], op=ALU.mult)
        pc = self.ps()
        for mt in range(4):
            for k in range(8):
                c.pe.matmul(out=pc[:, mt:mt + 1], lhsT=SQ[:, k, mt * 128:mt * 128 + 128], rhs=self.ONESB[:, 0:1], start=(k == 0), stop=(k == 7))
        c.act.activation(out=RC[:, :], in_=pc[:, 0:4], func=AF.Ln, scale=1.0 / DM, bias=EPS)
        c.act.activation(out=RC[:, :], in_=RC[:, :], func=AF.Exp, scale=-0.5)
        for mt in range(4):
            for hf in range(2):
                p = self.ps()
                for k in range(8):
                    c.pe.matmul(out=p[:, :], lhsT=MG[:, k, mt * 128:mt * 128 + 128], rhs=WKV[:, k, DM + hf * 512:DM + hf * 512 + 512],
                                start=(k == 0), stop=(k == 7))
                c.dve.tensor_scalar(out=VT[:, mt, hf * 512:hf * 512 + 512], in0=p[:, :], scalar1=RC[:, mt:mt + 1], scalar2=None, op0=ALU.mult)
        c.barrier()
        ar.reset(mark)
        YM = ar.alloc(128, (8, 512), BF16)
        X32 = [ar.alloc(128, (8, 512), F32) for _ in range(2)]
        SQ = ar.alloc(128, (8, 512), BF16)
        XG = ar.alloc(128, (8, 512), BF16)
        RS = ar.alloc(128, (512,), F32)
        QT = ar.alloc(128, (8, 512), BF16)
        PT = [ar.alloc(128, (512,), BF16) for _ in range(4)]
        RSM = ar.alloc(128, (512,), F32)
        OT = ar.alloc(128, (8, 512), BF16)
        XO = [ar.alloc(128, (512,), F32) for _ in range(4)]
        for tb in range(T // 512):
            b = tb // 4
            x = X32[tb % 2]
            for k in range(8):
                c.sp.dma_start(out=YM[:, k, :], in_=self.YT.v(k * 128, 128, tb))
                c.sp.dma_start(out=x[:, k, :], in_=XSRC.v(k * 128, 128, tb))
            for cc in range(8):
                p = self.ps()
                for k in range(8):
                    c.pe.matmul(out=p[:, :], lhsT=WM[:, k, cc * 128:cc * 128 + 128], rhs=YM[:, k, :], start=(k == 0), stop=(k == 7))
                c.dve.tensor_tensor(out=x[:, cc, :], in0=p[:, :], in1=x[:, cc, :], op=ALU.add)
            self.rms_stats(x, 8, 512, SQ, RS)
            for k in range(8):
                c.dve.tensor_scalar(out=XG[:, k, :], in0=x[:, k, :], scalar1=self.col(l, C_NXA + k), scalar2=None, op0=ALU.mult)
            for cc in range(8):
                p = self.ps()
                for k in range(8):
                    c.pe.matmul(out=p[:, :], lhsT=WQ[:, k, cc * 128:cc * 128 + 128], rhs=XG[:, k, :], start=(k == 0), stop=(k == 7))
                c.dve.scalar_tensor_tensor(out=QT[:, cc, :], in0=p[:, :], scalar=1.0 / 16.0, in1=RS[:, :], op0=ALU.mult, op1=ALU.mult)
            for h in range(4):
                pts = []
                for m in range(2):
                    pS = self.ps()
                    for c2 in range(2):
                        c.pe.matmul(out=pS[:, :], lhsT=KT[:, 2 * h + c2, b * NMEM + m * 128:b * NMEM + m * 128 + 128],
                                    rhs=QT[:, 2 * h + c2, :], start=(c2 == 0), stop=(c2 == 1))
                    pt = PT[(2 * h + m) % 4]
                    c.act.activation(out=pt[:, :], in_=pS[:, :], func=AF.Exp)
                    pts.append(pt)
                pM = self.ps()
                for m in range(2):
                    c.pe.matmul(out=pM[:, :], lhsT=self.ONESB[:, :], rhs=pts[m][:, :], start=(m == 0), stop=(m == 1))
                c.act.activation(out=RSM[:, :], in_=pM[:, :], func=AF.Ln)
                c.act.activation(out=RSM[:, :], in_=RSM[:, :], func=AF.Exp, scale=-1.0)
                for c2 in range(2):
                    pO = self.ps()
                    for m in range(2):
                        c.pe.matmul(out=pO[:, :], lhsT=VT[:, b * 2 + m, (2 * h + c2) * 128:(2 * h + c2) * 128 + 128], rhs=pts[m][:, :],
                                    start=(m == 0), stop=(m == 1))
                    c.dve.tensor_tensor(out=OT[:, 2 * h + c2, :], in0=pO[:, :], in1=RSM[:, :], op=ALU.mult)
            for cc in range(8):
                p = self.ps()
                for k in range(8):
                    c.pe.matmul(out=p[:, :], lhsT=WO[:, k, cc * 128:cc * 128 + 128], rhs=OT[:, k, :], start=(k == 0), stop=(k == 7))
                xo = XO[cc % 4]
                c.dve.tensor_tensor(out=xo[:, :], in0=p[:, :], in1=x[:, cc, :], op=ALU.add)
                c.pool.dma_start(out=self.XT.v(cc * 128, 128, tb), in_=xo[:, :])

    def phase_mlp(self, l, final):
        c, ar = self.c, self.ar
        c.barrier()
        ar.reset()
        DF = 4 * DM
        W1 = ar.alloc(128, (8, DF), BF16)
        W2 = ar.alloc(128, (32, DM), BF16)
        stg = [ar.alloc(128, (1024,), F32) for _ in range(3)]
        w1, w2 = self.d_in["mlp_w1"], self.d_in["mlp_w2"]
        i = 0
        for k in range(8):
            for hf in range(4):
                s = stg[i % 3]
                c.sp.dma_start(out=s[:, :], in_=w1[l, k * 128:(k + 1) * 128, hf * 1024:(hf + 1) * 1024])
                if i % 2:
                    c.act.activation(out=W1[:, k, hf * 1024:(hf + 1) * 1024], in_=s[:, :], func=AF.Copy)
                else:
                    c.dve.tensor_copy(out=W1[:, k, hf * 1024:(hf + 1) * 1024], in_=s[:, :])
                i += 1
        NB = 256
        X32 = [ar.alloc(128, (8, NB), F32) for _ in range(2)]
        for k in range(8):
            c.sp.dma_start(out=X32[0][:, k, :], in_=self.XT.v(k * 128, 128, 0, c0=0, ncol=NB))
        SQ = ar.alloc(128, (8, NB), BF16)
        XG = ar.alloc(128, (8, NB), BF16)
        RS = ar.alloc(128, (NB,), F32)
        RS2 = ar.alloc(128, (NB,), F32)
        HR = [ar.alloc(128, (NB,), BF16) for _ in range(2)]
        HID = ar.alloc(128, (32, NB), BF16)
        TO = [ar.alloc(128, (NB,), F32) for _ in range(2)]
        XN = ar.alloc(128, (8, NB), F32)
        OUTS = [ar.alloc(128, (NB,), F32) for _ in range(2)]
        for tb in range(T // NB):
            x = X32[tb % 2]
            dv = lambda grid, k: grid.v(k * 128, 128, tb // 2, c0=tb * NB, ncol=NB)
            if tb > 0:
                for k in range(8):
                    c.sp.dma_start(out=x[:, k, :], in_=dv(self.XT, k))
            self.rms_stats(x, 8, NB, SQ, RS)
            c.pool.tensor_tensor(out=RS2[:, :], in0=RS[:, :], in1=RS[:, :], op=ALU.mult)
            for k in range(8):
                c.dve.tensor_scalar(out=XG[:, k, :], in0=x[:, k, :], scalar1=self.col(l, C_NMLP + k), scalar2=None, op0=ALU.mult)
            if tb == 0:
                for f in range(32):
                    s = stg[i % 3]
                    c.sp.dma_start(out=s[:, 0:DM], in_=w2[l, f * 128:(f + 1) * 128, :])
                    c.dve.tensor_copy(out=W2[:, f, :], in_=s[:, 0:DM])
                    i += 1
            for f in range(32):
                p = self.ps()
                for k in range(8):
                    c.pe.matmul(out=p[:, 0:NB], lhsT=W1[:, k, f * 128:f * 128 + 128], rhs=XG[:, k, :], start=(k == 0), stop=(k == 7))
                hr = HR[f % 2]
                c.act.activation(out=hr[:, :], in_=p[:, 0:NB], func=AF.Relu)
                c.pool.tensor_tensor(out=HID[:, f, :], in0=hr[:, :], in1=hr[:, :], op=ALU.mult)
            for cc in range(8):
                p = self.ps()
                for f in range(32):
                    c.pe.matmul(out=p[:, 0:NB], lhsT=W2[:, f, cc * 128:cc * 128 + 128], rhs=HID[:, f, :], start=(f == 0), stop=(f == 31))
                to = TO[cc % 2]
                c.dve.tensor_tensor(out=to[:, :], in0=p[:, 0:NB], in1=RS2[:, :], op=ALU.mult)
                c.dve.tensor_tensor(out=XN[:, cc, :], in0=to[:, :], in1=x[:, cc, :], op=ALU.add)
                if not final:
                    c.pool.dma_start(out=dv(self.XT, cc), in_=XN[:, cc, :])
            if final:
                self.rms_stats(XN, 8, NB, SQ, RS)
                for k in range(8):
                    o = OUTS[k % 2]
                    c.dve.scalar_tensor_tensor(out=o[:, :], in0=XN[:, k, :], scalar=self.col(l, C_NFIN + k), in1=RS[:, :],
                                               op0=ALU.mult, op1=ALU.mult)
                    c.pool.dma_start(out=dv(self.outT, k), in_=o[:, :])

    def tok_major2(self, dst, src, evac=None):
        c = self.c
        for g in range(4):
            p = self.ps()
            for hl in range(2):
                pb = 64 * hl
                for s_ in range(8):
                    n = g * 8 + s_
                    c.pe.matmul(out=p[pb:pb + 64, s_ * 64:s_ * 64 + 64], lhsT=src[pb:pb + 64, n * 64:n * 64 + 64],
                                rhs=self.CST[pb:pb + 64, K_ID + pb:K_ID + pb + 64], start=True, stop=True)
            pv = p.v(p.t[:, :].rearrange("p (s e) -> p s e", s=8))
            if evac is None:
                c.act.activation(out=dst[:, g * 8:g * 8 + 8, :], in_=pv, func=AF.Copy)
            else:
                c.dve.tensor_tensor(out=dst[:, g * 8:g * 8 + 8, :], in0=pv, in1=evac(g), op=ALU.mult)

    def phase_ret2(self, l):
        c, ar = self.c, self.ar
        c.barrier()
        ar.reset()
        f32 = lambda: ar.alloc(128, (S,), F32)
        Q, Kt, V, G, COS, SIN, T1, T2, KZ, KV, G64, SS = [f32() for _ in range(12)]
        QH = ar.alloc(128, (S,), BF16)
        KH = ar.alloc(128, (S,), BF16)
        QX = ar.alloc(128, (S,), BF16)
        KZT = ar.alloc(128, (32, 64), BF16)
        VT = ar.alloc(128, (32, 64), BF16)
        SB = ar.alloc(128, (33, 64), BF16)
        SC = [ar.alloc(128, (512,), BF16) for _ in range(2)]
        OS = ar.alloc(128, (512,), F32)
        CEN = ar.alloc(128, (512,), F32)
        SQ_ = ar.alloc(128, (512,), F32)
        RSD = ar.alloc(128, (512,), F32)
        SG = ar.alloc(128, (512,), F32)
        YO = [ar.alloc(128, (512,), BF16) for _ in range(2)]
        b64 = self.CST[:, K_B64:K_B64 + 128]
        v3 = lambda t_: t_.v(t_.t.rearrange("p (n i) -> p n i", i=64))
        bc = lambda k0, n: self.CST.v(self.CST.t[:, k0:k0 + 64].unsqueeze(1).to_broadcast([128, n, 64]))
        for b in range(BL):
            self.load_seq_rows(COS, self.TAB["COSA"], 0, 128, b)
            self.load_seq_rows(SIN, self.TAB["SINA"], 0, 128, b)
            for hp in range(2):
                self.load_seq_rows(Q, self.ZT, 0 + hp * 128, 128, b)
                self.load_seq_rows(Kt, self.ZT, 256 + hp * 128, 128, b)
                self.load_seq_rows(V, self.ZT, 512 + hp * 128, 128, b)
                self.load_seq_rows(G, self.ZT, 768 + hp * 128, 128, b)
                self.rope64(T1, Q, COS, SIN, T2, K_ROTA, 128)
                c.dve.tensor_scalar(out=QH[:, :], in0=T1[:, :], scalar1=0.125, scalar2=None, op0=ALU.mult)
                c.pool.tensor_tensor(out=v3(QX), in0=v3(T1), in1=bc(K_XI2 + 64 * hp, 32), op=ALU.mult)
                self.rope64(T1, Kt, COS, SIN, T2, K_ROTA, 128)
                c.dve.tensor_copy(out=KH[:, :], in_=T1[:, :])
                c.pool.tensor_tensor(out=v3(KZ), in0=v3(T1), in1=bc(K_ZETA2 + 64 * hp, 32), op=ALU.mult)
                self.tok_major2(KZT, KZ)
                self.tok_major2(VT, V)
                kv3 = KV.t.rearrange("p (e n) -> p e n", e=64)
                for g in range(4):
                    p = self.ps()
                    for hl in range(2):
                        pb = 64 * hl
                        for s_ in range(8):
                            n = g * 8 + s_
                            c.pe.matmul(out=p[pb:pb + 64, s_ * 64:s_ * 64 + 64], lhsT=KZT[pb:pb + 64, n, :], rhs=VT[pb:pb + 64, n, :],
                                        start=True, stop=True)
                    c.dve.tensor_copy(out=KV.v(kv3[:, :, g * 8:g * 8 + 8]),
                                      in_=p.v(p.t[:, :].rearrange("p (s e) -> p e s", s=8)))
                for hl in range(2):
                    g64 = math.exp(64.0 * math.log1p(-2.0 ** (-5.0 - (2 * hp + hl))))
                    c.pool.memset(ap=G64[64 * hl:64 * hl + 64, :], constant=g64)
                c.pool.memset(ap=G64.v(G64.t.rearrange("p (e n) -> p e n", e=64)[:, :, 0:1]), constant=0.0)
                c.dve.tensor_tensor_scan(out=SS[:, :], data0=G64[:, :], data1=KV[:, :], initial=0.0, op0=ALU.mult, op1=ALU.add)
                c.pool.memset(ap=SB[:, 0, :], constant=0.0)
                c.pool.tensor_copy(out=SB[:, 1:33, :], in_=SS.v(SS.t.rearrange("p (e n) -> p n e", e=64)))
                for g in range(4):
                    sl = slice(g * 512, (g + 1) * 512)
                    pS = self.ps()
                    for hl in range(2):
                        pb = 64 * hl
                        for s_ in range(8):
                            cs = slice((g * 8 + s_) * 64, (g * 8 + s_) * 64 + 64)
                            c.pe.matmul(out=pS[pb:pb + 64, s_ * 64:s_ * 64 + 64], lhsT=KH[pb:pb + 64, cs], rhs=QH[pb:pb + 64, cs],
                                        start=True, stop=True)
                    sc = SC[g % 2]
                    c.dve.tensor_tensor(out=sc.v(sc.t.rearrange("p (s i) -> p s i", s=8)),
                                        in0=pS.v(pS.t[:, :].rearrange("p (s i) -> p s i", s=8)), in1=bc(K_DMASK2 + 64 * hp, 8), op=ALU.mult)
                    pO = self.ps()
                    for hl in range(2):
                        pb = 64 * hl
                        for s_ in range(8):
                            n = g * 8 + s_
                            cs = slice(n * 64, n * 64 + 64)
                            c.pe.matmul(out=pO[pb:pb + 64, s_ * 64:s_ * 64 + 64], lhsT=VT[pb:pb + 64, n, :],
                                        rhs=sc[pb:pb + 64, s_ * 64:s_ * 64 + 64], start=True, stop=False)
                            c.pe.matmul(out=pO[pb:pb + 64, s_ * 64:s_ * 64 + 64], lhsT=SB[pb:pb + 64, n, :], rhs=QX[pb:pb + 64, cs],
                                        start=False, stop=True)
                    c.act.activation(out=OS[:, :], in_=pO[:, :], func=AF.Copy)
                    p2 = self.ps()
                    c.pe.matmul(out=p2[:, :], lhsT=b64, rhs=OS[:, :], start=True, stop=True)
                    c.dve.scalar_tensor_tensor(out=CEN[:, :], in0=p2[:, :], scalar=-1.0 / 64, in1=OS[:, :], op0=ALU.mult, op1=ALU.add)
                    c.act.activation(out=SQ_[:, :], in_=CEN[:, :], func=AF.Square)
                    p3 = self.ps()
                    c.pe.matmul(out=p3[:, :], lhsT=b64, rhs=SQ_[:, :], start=True, stop=True)
                    c.act.activation(out=RSD[:, :], in_=p3[:, :], func=AF.Ln, scale=1.0 / 64, bias=EPS)
                    c.act.activation(out=RSD[:, :], in_=RSD[:, :], func=AF.Exp, scale=-0.5)
                    c.act.activation(out=SG[:, :], in_=G[:, sl], func=AF.Silu)
                    c.dve.tensor_tensor(out=CEN[:, :], in0=CEN[:, :], in1=RSD[:, :], op=ALU.mult)
                    yo = YO[g % 2]
                    c.dve.scalar_tensor_tensor(out=yo[:, :], in0=CEN[:, :], scalar=self.col(l, C_RGN2 + hp), in1=SG[:, :],
                                               op0=ALU.mult, op1=ALU.mult)
                    c.pool.dma_start(out=self.YT.v(hp * 128, 128, b * 4 + g), in_=yo[:, :])

    def phase_gdn2(self, l):
        c, ar = self.c, self.ar
        c.barrier()
        ar.reset()
        NPR = 2 * BL
        KBG = [ar.alloc(128, (S,), BF16) for _ in range(NPR)]
        QD = [ar.alloc(128, (S,), BF16) for _ in range(NPR)]
        TMt = [ar.alloc(128, (32, 64), BF16) for _ in range(NPR)]
        ATT = [ar.alloc(128, (32, 64), BF16) for _ in range(NPR)]
        KD = [ar.alloc(128, (32, 64), BF16) for _ in range(NPR)]
        VB = ar.alloc(128, (NPR, 32, 64), BF16)
        EGL = ar.alloc(128, (NPR, 32), F32)
        X, TT, Kf, Qf, Vf, BETB, GCB = [ar.alloc(128, (S,), F32) for _ in range(7)]
        Kb = ar.alloc(128, (S,), BF16)
        Qb = ar.alloc(128, (S,), BF16)
        CS = ar.alloc(128, (S,), BF16)
        GCc = ar.alloc(128, (32,), F32)
        BTc = ar.alloc(128, (32,), F32)
        SCOL = ar.alloc(128, (4,), F32)
        g3 = lambda: ar.alloc(128, (8, 64), F32)
        D1, DX, EUi, EUs, EL, U_, L_, Pa, Qa, R1, R2, Lof = [g3() for _ in range(12)]
        TG, Pb, Qb2, Rd, Rd2, Ysb = DX, U_, L_, R1, R2, D1
        St = ar.alloc(128, (NPR, 64), F32)
        Sb = ar.alloc(128, (NPR, 64), BF16)
        RH = ar.alloc(128, (NPR, 64), BF16)
        VNb = ar.alloc(128, (NPR, 64), BF16)
        GT = Sub(X, X.t.rearrange("p (h t) -> p h t", h=NPR))
        OS, SQ_, RSD, SG = [Sub(TT, TT.t[:, i * 512:(i + 1) * 512]) for i in range(4)]
        YO = [ar.alloc(128, (512,), BF16) for _ in range(2)]
        b64 = self.CST[:, K_B64:K_B64 + 128]
        idb = lambda pb: self.CST[pb:pb + 64, K_ID + pb:K_ID + pb + 64]
        cbc = lambda k0: self.CST.v(self.CST.t[:, k0:k0 + 64].unsqueeze(1).to_broadcast([128, 8, 64]))
        v3 = lambda t_, sl: t_.v(t_.t[:, sl].rearrange("p (n i) -> p n i", i=64))
        p3 = lambda p: p.v(p.t[:, :].rearrange("p (s i) -> p s i", s=8))
        c.pool.memset(ap=CS[:, :], constant=1.0)
        c.pool.memset(ap=CS.v(CS.t.rearrange("p (n i) -> p n i", i=64)[:, :, 0:1]), constant=0.0)

        def mm8(pt, lhs_fn, rhs_fn):
            for hl in range(2):
                pb = 64 * hl
                for s_ in range(8):
                    c.pe.matmul(out=pt[pb:pb + 64, s_ * 64:s_ * 64 + 64], lhsT=lhs_fn(pb, s_), rhs=rhs_fn(pb, s_), start=True, stop=True)

        def tr8(pt, src, cs):
            for hl in range(2):
                pb = 64 * hl
                for s_ in range(8):
                    c.pe.matmul(out=pt[pb:pb + 64, s_ * 64:s_ * 64 + 64], lhsT=src[pb:pb + 64, cs(s_)], rhs=idb(pb), start=True, stop=True)

        for b in range(BL):
            for hp in range(2):
                ip = b * 2 + hp
                for hl in range(2):
                    for j in range(4):
                        c.sp.dma_start(out=BETB[64 * hl:64 * hl + 64, j * 512:(j + 1) * 512], in_=self.ZT.vb(2912 + 2 * hp + hl, 64, b * 4 + j))
                        c.sp.dma_start(out=GCB[64 * hl:64 * hl + 64, j * 512:(j + 1) * 512], in_=self.ZT.vb(2916 + 2 * hp + hl, 64, b * 4 + j))
                self.load_seq_rows(X, self.ZT, 1888 + 0 * 256 + hp * 128, 128, b)
                self.load_seq_rows(Vf, self.ZT, 1888 + 1 * 256 + hp * 128, 128, b)
                c.act.activation(out=BETB[:, :], in_=BETB[:, :], func=AF.Sigmoid)
                c.dve.tensor_scalar(out=Qf[:, :], in0=GCB[:, :], scalar1=self.col(l, C_DTB2 + hp), scalar2=None, op0=ALU.add)
                c.dve.tensor_scalar(out=Kf[:, :], in0=Qf[:, :], scalar1=-1.0, scalar2=None, op0=ALU.mult)
                c.dve.tensor_tensor(out=Kf[:, :], in0=Kf[:, :], in1=Qf[:, :], op=ALU.max)
                c.act.activation(out=Kf[:, :], in_=Kf[:, :], func=AF.Exp, scale=-1.0)
                c.act.activation(out=Kf[:, :], in_=Kf[:, :], func=AF.Ln, bias=1.0)
                c.dve.tensor_scalar(out=Qf[:, :], in0=Qf[:, :], scalar1=0.0, scalar2=None, op0=ALU.max)
                c.dve.tensor_tensor(out=Qf[:, :], in0=Qf[:, :], in1=Kf[:, :], op=ALU.add)
                c.act.activation(out=SCOL[:, 0:1], in_=self.col(l, C_ALOG2 + hp), func=AF.Exp)
                c.dve.tensor_scalar(out=SCOL[:, 1:2], in0=SCOL[:, 0:1], scalar1=-1.0, scalar2=None, op0=ALU.mult)
                c.dve.tensor_scalar(out=Qf[:, :], in0=Qf[:, :], scalar1=SCOL[:, 1:2], scalar2=None, op0=ALU.mult)
                c.dve.tensor_tensor_scan(out=GCB[:, :], data0=CS[:, :], data1=Qf[:, :], initial=0.0, op0=ALU.mult, op1=ALU.add)
                for part, dst, raw in ((0, Qf, X), (1, Kf, Vf), (2, Vf, X)):
                    if part == 2:
                        self.load_seq_rows(X, self.ZT, 1888 + 2 * 256 + hp * 128, 128, b)
                    self.conv4(TT, raw, l, C_GCW2 + (part * 2 + hp) * 4, 128)
                    c.act.activation(out=dst[:, :], in_=TT[:, :], func=AF.Silu)
                    if part < 2:
                        c.act.activation(out=TT[:, :], in_=dst[:, :], func=AF.Square)
                        for j in range(4):
                            sl = slice(j * 512, (j + 1) * 512)
                            p = self.ps()
                            c.pe.matmul(out=p[:, :], lhsT=b64, rhs=TT[:, sl], start=True, stop=True)
                            c.act.activation(out=X[:, sl], in_=p[:, :], func=AF.Ln, bias=EPS)
                        c.act.activation(out=X[:, :], in_=X[:, :], func=AF.Exp, scale=-0.5)
                        if part == 0:
                            c.dve.scalar_tensor_tensor(out=dst[:, :], in0=dst[:, :], scalar=0.125, in1=X[:, :], op0=ALU.mult, op1=ALU.mult)
                        else:
                            c.dve.tensor_tensor(out=dst[:, :], in0=dst[:, :], in1=X[:, :], op=ALU.mult)
                c.act.activation(out=Kb[:, :], in_=Kf[:, :], func=AF.Copy)
                c.act.activation(out=Qb[:, :], in_=Qf[:, :], func=AF.Copy)
                c.act.activation(out=TT[:, :], in_=GCB[:, :], func=AF.Exp)
                c.pool.tensor_tensor(out=X[:, :], in0=BETB[:, :], in1=TT[:, :], op=ALU.mult)
                c.dve.tensor_tensor(out=KBG[ip][:, :], in0=Kf[:, :], in1=X[:, :], op=ALU.mult)
                c.dve.tensor_tensor(out=QD[ip][:, :], in0=Qf[:, :], in1=TT[:, :], op=ALU.mult)
                c.pool.tensor_copy(out=EGL[:, ip, :], in_=TT.v(TT.t.rearrange("p (n i) -> p n i", i=64)[:, :, 63]))
                for g in range(4):
                    cs = lambda s_: slice((g * 8 + s_) * 64, (g * 8 + s_) * 64 + 64)
                    p = self.ps()
                    tr8(p, GCB, cs)
                    c.dve.tensor_copy(out=GCc[:, g * 8:g * 8 + 8], in_=p.v(p.t[:, :].rearrange("p (s i) -> p s i", s=8)[:, :, 0]))
                    p = self.ps()
                    tr8(p, BETB, cs)
                    c.dve.tensor_copy(out=BTc[:, g * 8:g * 8 + 8], in_=p.v(p.t[:, :].rearrange("p (s i) -> p s i", s=8)[:, :, 0]))
                for g in range(4):
                    tk = slice(g * 512, (g + 1) * 512)
                    ns = slice(g * 8, g * 8 + 8)
                    cs = lambda s_: slice((g * 8 + s_) * 64, (g * 8 + s_) * 64 + 64)
                    gcc_b = GCc.v(GCc.t[:, ns].unsqueeze(2).to_broadcast([128, 8, 64]))
                    btc_b = BTc.v(BTc.t[:, ns].unsqueeze(2).to_broadcast([128, 8, 64]))
                    pG = self.ps()
                    mm8(pG, lambda pb, s_: Kb[pb:pb + 64, cs(s_)], lambda pb, s_: Kb[pb:pb + 64, cs(s_)])
                    pQ = self.ps()
                    mm8(pQ, lambda pb, s_: Kb[pb:pb + 64, cs(s_)], lambda pb, s_: Qb[pb:pb + 64, cs(s_)])
                    c.dve.tensor_tensor(out=D1[:, :, :], in0=v3(GCB, tk), in1=gcc_b, op=ALU.subtract)
                    c.pool.tensor_tensor(out=DX[:, :, :], in0=D1[:, :, :], in1=cbc(K_NMGE), op=ALU.add)
                    c.act.activation(out=EUi[:, :, :], in_=DX[:, :, :], func=AF.Exp)
                    c.pool.tensor_tensor(out=DX[:, :, :], in0=D1[:, :, :], in1=cbc(K_NMGT), op=ALU.add)
                    c.act.activation(out=EUs[:, :, :], in_=DX[:, :, :], func=AF.Exp)
                    c.dve.scalar_tensor_tensor(out=DX[:, :, :], in0=D1[:, :, :], scalar=-1.0, in1=cbc(K_NMLT), op0=ALU.mult, op1=ALU.add)
                    c.act.activation(out=EL[:, :, :], in_=DX[:, :, :], func=AF.Exp)
                    c.dve.tensor_tensor(out=ATT[ip][:, ns, :], in0=p3(pQ), in1=EUi[:, :, :], op=ALU.mult)
                    c.pool.tensor_tensor(out=TG[:, :, :], in0=v3(BETB, tk), in1=EUs[:, :, :], op=ALU.mult)
                    c.dve.tensor_tensor(out=U_[:, :, :], in0=p3(pG), in1=TG[:, :, :], op=ALU.mult)
                    c.pool.tensor_tensor(out=TG[:, :, :], in0=EL[:, :, :], in1=btc_b, op=ALU.mult)
                    c.dve.tensor_tensor(out=L_[:, :, :], in0=p3(pG), in1=TG[:, :, :], op=ALU.mult)
                    pK = self.ps()
                    tr8(pK, Kf, cs)
                    c.dve.tensor_tensor(out=KD[ip][:, ns, :], in0=p3(pK),
                                        in1=EUi.v(EUi.t[:, :, 63:64].to_broadcast([128, 8, 64])), op=ALU.mult)
                    pV = self.ps()
                    tr8(pV, Vf, cs)
                    c.dve.tensor_tensor(out=VB[:, ip, ns, :], in0=p3(pV), in1=btc_b, op=ALU.mult)
                    c.dve.scalar_tensor_tensor(out=R1[:, :, :], in0=U_[:, :, :], scalar=-1.0, in1=cbc(K_ID2), op0=ALU.mult, op1=ALU.add)
                    c.pool.tensor_tensor(out=Lof[:, :, :], in0=L_[:, :, :], in1=cbc(K_OFFL), op=ALU.mult)
                    P_, Q_ = U_, L_
                    for lev in range(4):
                        Pn, Qn = (Pa, Qa) if lev % 2 == 0 else (Pb, Qb2)
                        pP = self.ps()
                        mm8(pP, lambda pb, s_: Q_[pb:pb + 64, s_, :], lambda pb, s_: P_[pb:pb + 64, s_, :])
                        pQ2 = self.ps()
                        mm8(pQ2, lambda pb, s_: P_[pb:pb + 64, s_, :], lambda pb, s_: Q_[pb:pb + 64, s_, :])
                        c.act.activation(out=Pn[:, :, :], in_=p3(pP), func=AF.Copy)
                        c.dve.tensor_copy(out=Qn[:, :, :], in_=p3(pQ2))
                        pR = self.ps()
                        mm8(pR, lambda pb, s_: Qn[pb:pb + 64, s_, :], lambda pb, s_: R1[pb:pb + 64, s_, :])
                        c.dve.tensor_tensor(out=R1[:, :, :], in0=p3(pR), in1=R1[:, :, :], op=ALU.add)
                        P_, Q_ = Pn, Qn
                    c.pool.tensor_tensor(out=Rd[:, :, :], in0=R1[:, :, :], in1=cbc(K_DIAG), op=ALU.mult)
                    pT2 = self.ps()
                    mm8(pT2, lambda pb, s_: Rd[pb:pb + 64, s_, :], lambda pb, s_: idb(pb))
                    c.act.activation(out=Rd2[:, :, :], in_=p3(pT2), func=AF.Copy)
                    pY = self.ps()
                    mm8(pY, lambda pb, s_: Lof[pb:pb + 64, s_, :], lambda pb, s_: Rd[pb:pb + 64, s_, :])
                    c.act.activation(out=Ysb[:, :, :], in_=p3(pY), func=AF.Copy)
                    pX = self.ps()
                    mm8(pX, lambda pb, s_: Rd2[pb:pb + 64, s_, :], lambda pb, s_: Ysb[pb:pb + 64, s_, :])
                    c.dve.tensor_tensor(out=TMt[ip][:, ns, :], in0=Rd[:, :, :], in1=p3(pX), op=ALU.subtract)
        c.pool.memset(ap=St[:, :, :], constant=0.0)
        c.pool.memset(ap=Sb[:, :, :], constant=0.0)
        pO = self.psl[0:4]
        pP1, pVN, pSN, p7 = self.psl[4], self.psl[5], self.psl[6], self.psl[7]
        p4 = lambda p: p.v(p.t[:, 0:64 * NPR].rearrange("p (h e) -> p h e", h=NPR))
        for n in range(32):
            s_, g = n % 8, n // 8
            cs = slice(n * 64, n * 64 + 64)
            if s_ == 0:
                for ip in range(NPR):
                    c.sp.dma_start(out=GT[:, ip, :], in_=self.ZT.v(2656 + (ip % 2) * 128, 128, (ip // 2) * 4 + g))
            for ip in range(NPR):
                for pb in (0, 64):
                    c.pe.matmul(out=pP1[pb:pb + 64, ip * 64:ip * 64 + 64], lhsT=KBG[ip][pb:pb + 64, cs], rhs=Sb[pb:pb + 64, ip, :], start=True, stop=True)
            c.dve.tensor_tensor(out=RH[:, :, :], in0=VB[:, :, n, :], in1=p4(pP1), op=ALU.subtract)
            for ip in range(NPR):
                for pb in (0, 64):
                    c.pe.matmul(out=pVN[pb:pb + 64, ip * 64:ip * 64 + 64], lhsT=TMt[ip][pb:pb + 64, n, :], rhs=RH[pb:pb + 64, ip, :], start=True, stop=True)
            c.act.activation(out=VNb[:, :, :], in_=p4(pVN), func=AF.Copy)
            for ip in range(NPR):
                for pb in (0, 64):
                    c.pe.matmul(out=pO[ip][pb:pb + 64, s_ * 64:s_ * 64 + 64], lhsT=Sb[pb:pb + 64, ip, :], rhs=QD[ip][pb:pb + 64, cs], start=True, stop=False)
                    c.pe.matmul(out=pO[ip][pb:pb + 64, s_ * 64:s_ * 64 + 64], lhsT=VNb[pb:pb + 64, ip, :], rhs=ATT[ip][pb:pb + 64, n, :], start=False, stop=True)
            for ip in range(NPR):
                for pb in (0, 64):
                    c.pe.matmul(out=pSN[pb:pb + 64, ip * 64:ip * 64 + 64], lhsT=KD[ip][pb:pb + 64, n, :], rhs=VNb[pb:pb + 64, ip, :], start=True, stop=True)
            for ip in range(NPR):
                c.dve.scalar_tensor_tensor(out=St[:, ip, :], in0=St[:, ip, :], scalar=EGL[:, ip, n:n + 1],
                                           in1=pSN[:, ip * 64:ip * 64 + 64], op0=ALU.mult, op1=ALU.add)
            c.pool.tensor_copy(out=Sb[:, :, :], in_=St[:, :, :])
            if s_ == 7:
                for ip in range(NPR):
                    c.act.activation(out=OS[:, :], in_=pO[ip][:, :], func=AF.Copy)
                    c.act.activation(out=SQ_[:, :], in_=OS[:, :], func=AF.Square)
                    c.pe.matmul(out=p7[:, :], lhsT=b64, rhs=SQ_[:, :], start=True, stop=True)
                    c.act.activation(out=RSD[:, :], in_=p7[:, :], func=AF.Ln, scale=1.0 / 64, bias=EPS)
                    c.act.activation(out=RSD[:, :], in_=RSD[:, :], func=AF.Exp, scale=-0.5)
                    c.act.activation(out=SG[:, :], in_=GT[:, ip, :], func=AF.Silu)
                    c.dve.tensor_tensor(out=OS[:, :], in0=OS[:, :], in1=RSD[:, :], op=ALU.mult)
                    yo = YO[ip % 2]
                    c.dve.scalar_tensor_tensor(out=yo[:, :], in0=OS[:, :], scalar=self.col(l, C_GNORM2), in1=SG[:, :],
                                               op0=ALU.mult, op1=ALU.mult)
                    c.pool.dma_start(out=self.YT.v(768 + (ip % 2) * 128, 128, (ip // 2) * 4 + g), in_=yo[:, :])


def build(debug=(), upto=None):
    nc = bass.Bass("TRN2", target_bir_lowering=False)
    with ExitStack() as st:
        k = K(nc, st, debug)
        k.phase_setup()
        for l in range(DEPTH):
            k.phase_proj(l, k.XIN if l == 0 else k.XT)
            if upto == "proj":
                break
            k.phase_ret2(l)
            if upto == "ret":
                break
            k.phase_mla(l)
            if upto == "mla":
                break
            k.phase_gdn2(l)
            if upto == "gdn":
                break
            k.phase_xattn(l, k.XIN if l == 0 else k.XT)
            if upto == "xattn":
                break
            k.phase_mlp(l, final=(l == DEPTH - 1))
            if upto == "mlp":
                break
        k.c.finish()
        print("instr counts:", {n: E.total + E.cnt for n, E in k.c.engs.items()}, "dmas", k.c.dnext, "epochs", k.c.epoch)
    return nc


WEIGHT_KEYS = ["w_in", "lru_wa", "lru_wx", "mla_w_uq", "mla_w_ukv", "w_out", "xa_wq", "xa_wkv", "xa_wo", "mlp_w1", "mlp_w2"]


def prep_inputs(inp, cores=range(NCORES)):
    cst = host_constants()
    cols = host_cols(inp)
    shared = {k: np.ascontiguousarray(inp[k], dtype=np.float32) for k in WEIGHT_KEYS}
    shared["cst"] = cst
    shared["cols"] = cols
    maps = []
    for ci in cores:
        m = dict(shared)
        xs = np.asarray(inp["x"][ci * BL:(ci + 1) * BL], dtype=np.float32).reshape(T, DM)
        m["xT"] = np.ascontiguousarray(xs.T)
        ms = np.asarray(inp["mem"][ci * BL:(ci + 1) * BL], dtype=np.float32).reshape(TM_, DM)
        m["memT"] = np.ascontiguousarray(ms.T)
        m["pos"] = np.ascontiguousarray(np.asarray(inp["positions"][ci * BL:(ci + 1) * BL], dtype=np.int32).reshape(1, T))
        maps.append(m)
    return maps


def kernel(**inputs):
    nc = build()
    maps = prep_inputs(inputs)
    res = run_bass_kernel_spmd(nc, maps, core_ids=list(range(NCORES)))
    outs = [np.asarray(r["outT"]).T.reshape(BL, S, DM) for r in res.results]
    return np.ascontiguousarray(np.concatenate(outs, axis=0).astype(np.float32))
```

```python
import math
from contextlib import ExitStack
import numpy as np
import concourse.bass as bass
import concourse.mybir as mybir
from concourse.bass_utils import run_bass_kernel_spmd

F32 = mybir.dt.float32
BF16 = mybir.dt.bfloat16
I32 = mybir.dt.int32
AF = mybir.ActivationFunctionType
ALU = mybir.AluOpType

NCORES = 8
DEPTH = 2
DM = 1024
S = 2048
BL = 2
T = BL * S
NMEM = 256
TM_ = BL * NMEM
INC = 2920
EPS = 1e-6
NEG = -1.0e30


class Tok:
    __slots__ = ("sem", "key", "val")

    def __init__(self, sem, key, val):
        self.sem = sem
        self.key = key
        self.val = val


class View:
    __slots__ = ("tile", "ap")

    def __init__(self, tile, ap):
        self.tile = tile
        self.ap = ap


class Tile:
    def __init__(self, ctx, t):
        self.t = t
        self.w = {}
        self.wdma = True
        self.rd = {}
        ctx.tiles.append(self)

    def __getitem__(self, idx):
        return View(self, self.t[idx])

    def v(self, ap):
        return View(self, ap)


class Sub:
    def __init__(self, tile, ap):
        self.tile = tile
        self.t = ap

    def __getitem__(self, idx):
        return View(self.tile, self.t[idx])


class Eng:
    def __init__(self, ctx, name, e):
        self.ctx = ctx
        self.name = name
        self.e = e
        self.sem = None
        self.key = None
        self.cnt = 0
        self.seen = {}
        self.total = 0
        self.rec = []

    def wait(self, tok):
        if tok is None:
            return
        if tok.key == self.key and self.name == "tensor":
            return
        if self.seen.get(tok.key, 0) >= tok.val:
            return
        sem, val = tok.sem, tok.val
        self.rec.append(lambda e: e.wait_ge(sem, val))
        self.seen[tok.key] = tok.val

    def __getattr__(self, opname):
        def f(**kw):
            return self.ctx.emit(self, opname, kw)
        return f


OUT_KEYS = ("out", "accum_out", "ap")


class Ctx:
    NDMA = 32

    def __init__(self, nc, stack):
        self.nc = nc
        self.stack = stack
        self.tiles = []
        self.epoch = 0
        self.engs = {}
        for name in ["tensor", "vector", "scalar", "gpsimd", "sync"]:
            self.engs[name] = Eng(self, name, getattr(nc, name))
        self.pe = self.engs["tensor"]
        self.dve = self.engs["vector"]
        self.act = self.engs["scalar"]
        self.pool = self.engs["gpsimd"]
        self.sp = self.engs["sync"]
        self.dsem = [stack.enter_context(nc.semaphore("dma%d" % i)) for i in range(self.NDMA)]
        self.dcnt = [0] * self.NDMA
        self.dnext = 0
        self.dnext_by = {"hw": 0, "sw": 0}
        self._new_epoch_sems()

    def _new_epoch_sems(self):
        for name, E in self.engs.items():
            if E.sem is not None and name != "tensor":
                continue
            E.sem = self.stack.enter_context(self.nc.semaphore("e%d_%s" % (self.epoch, name)))
            E.key = (self.epoch, name)
            E.total += E.cnt
            E.cnt = 0
        self.epoch += 1

    def tile(self, t):
        return Tile(self, t)

    def emit(self, E, opname, kw):
        outs = []
        ins = []
        kw2 = {}
        for k, v in kw.items():
            if isinstance(v, View):
                (outs if k in OUT_KEYS else ins).append(v.tile)
                kw2[k] = v.ap
            else:
                kw2[k] = v
        is_dma = opname == "dma_start"
        for t in ins:
            for tok in t.w.values():
                E.wait(tok)
        for t in outs:
            if t.rd:
                for tok in t.rd.values():
                    E.wait(tok)
                if t in ins:
                    pass
            if not (is_dma and t.wdma and not t.rd):
                for tok in t.w.values():
                    E.wait(tok)
        if is_dma:
            half = self.NDMA // 2
            kind = "sw" if E.name == "gpsimd" else "hw"
            i = (self.dnext_by[kind] % half) + (half if kind == "sw" else 0)
            self.dnext_by[kind] += 1
            self.dnext += 1
            if self.dcnt[i] > 0:
                E.wait(Tok(self.dsem[i], ("d", i), 16 * self.dcnt[i]))
            self.dcnt[i] += 1
            tok = Tok(self.dsem[i], ("d", i), 16 * self.dcnt[i])
            dsem = tok.sem
            E.rec.append(lambda e: e.dma_start(**kw2).then_inc(dsem, 16))
        else:
            E.cnt += 1
            tok = Tok(E.sem, E.key, E.cnt)
            esem = E.sem
            E.rec.append(lambda e: getattr(e, opname)(**kw2).then_inc(esem, 1))
        for t in ins:
            if t not in outs:
                t.rd[tok.key] = tok
        for t in outs:
            if is_dma and t.wdma and not t.rd and t.w:
                t.w[tok.key] = tok
            else:
                t.w = {tok.key: tok}
                t.wdma = is_dma
            t.rd = {}

    def barrier(self):
        toks = [Tok(E.sem, E.key, E.cnt) for E in self.engs.values() if E.cnt > 0]
        for i in range(self.NDMA):
            if self.dcnt[i] > 0:
                toks.append(Tok(self.dsem[i], ("d", i), 16 * self.dcnt[i]))
        for E in self.engs.values():
            for tok in toks:
                E.wait(tok)
        for t in self.tiles:
            t.w = {}
            t.wdma = True
            t.rd = {}
        self._new_epoch_sems()

    def finish(self):
        self.barrier()
        with self.nc.Block() as block:
            for name, E in self.engs.items():
                def run(e, E=E):
                    for f in E.rec:
                        f(e)
                getattr(block, name)(run)


class Arena:
    def __init__(self, c, t, nwords):
        self.c = c
        self.t = t
        self.n = nwords
        self.off = 0

    def reset(self, to=0):
        self.off = to

    def alloc(self, P, shape, dtype):
        shape = tuple(shape)
        n = int(np.prod(shape))
        words = n if dtype in (F32, I32) else (n + 1) // 2
        words = (words + 1) // 2 * 2
        ap = self.t[0:P, self.off:self.off + words]
        self.off += words
        assert self.off <= self.n, ("arena overflow", self.off, self.n)
        if dtype != F32:
            ap = ap.bitcast(dtype)
        ap = ap[:, 0:n]
        if len(shape) == 2:
            ap = ap.rearrange("p (a b) -> p a b", a=shape[0])
        elif len(shape) == 3:
            ap = ap.rearrange("p (a b c) -> p a b c", a=shape[0], b=shape[1])
        return self.c.tile(ap)


class DGrid:
    def __init__(self, c, ap, row_chunks, colb):
        self.ap = ap
        self.rc = row_chunks
        self.colb = colb
        ncb = ap.shape[1] // colb
        self.tiles = [[c.tile(ap) for _ in range(ncb)] for _ in row_chunks]

    def find(self, r):
        for i, (s, n) in enumerate(self.rc):
            if s <= r < s + n:
                return i
        raise KeyError(r)

    def v(self, r0, nr, cb, c0=None, ncol=None):
        i = self.find(r0)
        assert r0 + nr <= self.rc[i][0] + self.rc[i][1]
        if c0 is None:
            c0, ncol = cb * self.colb, self.colb
        return self.tiles[i][cb].v(self.ap[r0:r0 + nr, c0:c0 + ncol])

    def vb(self, r0, P, cb):
        i = self.find(r0)
        ap = self.ap[r0:r0 + 1, cb * self.colb:(cb + 1) * self.colb].partition_broadcast(P)
        return self.tiles[i][cb].v(ap)


ZCH = ([(i * 128, 128) for i in range(12)] +
       [(1536, 128), (1664, 64), (1728, 128), (1856, 32)] +
       [(1888 + i * 128, 128) for i in range(6)] +
       [(2656, 128), (2784, 128), (2912, 8)])
XCH = [(i * 128, 128) for i in range(8)]

NCOL = 151
C_RGN2, C_GCW2, C_GNORM2, C_ALOG2, C_DTB2 = 120, 122, 146, 147, 149
C_NMIX, C_NXA, C_NMLP, C_NMEM, C_NFIN = 0, 8, 16, 24, 32
C_QN, C_KVN, C_RGN = 40, 42, 43
C_LCW, C_LCB, C_LBA, C_LBX, C_LLAM = 47, 55, 57, 59, 61
C_GCW, C_GNORM, C_ALOG, C_DTB = 63, 111, 112, 116

K_ID, K_ROTA, K_ROTC, K_ONES, K_B64 = 0, 128, 256, 384, 512
K_INVA, K_INVC = 640, 641
K_XI, K_ZETA, K_DMASK = 642, 642 + 256, 642 + 512
K_NMGE, K_NMGT, K_NMLT, K_DIAG, K_OFFL = 1410, 1474, 1538, 1602, 1666
K_G64 = 1730
K_XI2, K_ZETA2, K_DMASK2, K_ID2 = 1734, 1734 + 128, 1734 + 256, 1734 + 384
NCST = 1734 + 448


def host_constants():
    c = np.zeros((128, NCST), np.float32)
    c[:, K_ID:K_ID + 128] = np.eye(128, dtype=np.float32)
    ra = np.zeros((128, 128), np.float32)
    rc = np.zeros((128, 128), np.float32)
    for m in range(128):
        if (m % 64) < 32:
            ra[m + 32, m] = -1.0
        else:
            ra[m - 32, m] = 1.0
        if (m % 32) < 16:
            rc[m + 16, m] = -1.0
        else:
            rc[m - 16, m] = 1.0
    c[:, K_ROTA:K_ROTA + 128] = ra
    c[:, K_ROTC:K_ROTC + 128] = rc
    c[:, K_ONES:K_ONES + 128] = 1.0
    b64 = np.zeros((128, 128), np.float32)
    b64[:64, :64] = 1.0
    b64[64:, 64:] = 1.0
    c[:, K_B64:K_B64 + 128] = b64
    p = np.arange(128)
    c[:, K_INVA] = np.power(np.float32(10000.0), -(p % 32).astype(np.float32) / np.float32(32)).astype(np.float32)
    c[:, K_INVC] = np.power(np.float32(10000.0), -(p % 16).astype(np.float32) / np.float32(16)).astype(np.float32)
    idx = np.arange(64, dtype=np.float64)
    for h in range(4):
        lg = math.log1p(-2.0 ** (-5.0 - h))
        c[:, K_XI + 64 * h:K_XI + 64 * h + 64] = (0.125 * np.exp((idx + 1.0) * lg))[None, :]
        c[:, K_ZETA + 64 * h:K_ZETA + 64 * h + 64] = np.exp((63.0 - idx) * lg)[None, :]
        c[:64, K_DMASK + 64 * h:K_DMASK + 64 * h + 64] = np.exp(np.abs(idx[:, None] - idx[None, :]) * lg)
        c[:, K_G64 + h] = math.exp(64.0 * lg)
    pp = np.arange(64)[:, None]
    ff = np.arange(64)[None, :]
    for r0 in (0, 64):
        c[r0:r0 + 64, K_NMGE:K_NMGE + 64] = np.where(ff >= pp, 0.0, NEG)
        c[r0:r0 + 64, K_NMGT:K_NMGT + 64] = np.where(ff > pp, 0.0, NEG)
        c[r0:r0 + 64, K_NMLT:K_NMLT + 64] = np.where(ff < pp, 0.0, NEG)
        c[r0:r0 + 64, K_DIAG:K_DIAG + 64] = ((pp // 32) == (ff // 32)).astype(np.float32)
        c[r0:r0 + 64, K_OFFL:K_OFFL + 64] = ((pp >= 32) & (ff < 32)).astype(np.float32)
        c[r0:r0 + 64, K_ID2:K_ID2 + 64] = np.eye(64, dtype=np.float32)
    for hp in range(2):
        for hl in range(2):
            h = 2 * hp + hl
            lg = math.log1p(-2.0 ** (-5.0 - h))
            rows = slice(64 * hl, 64 * hl + 64)
            c[rows, K_XI2 + 64 * hp:K_XI2 + 64 * hp + 64] = (0.125 * np.exp((idx + 1.0) * lg))[None, :]
            c[rows, K_ZETA2 + 64 * hp:K_ZETA2 + 64 * hp + 64] = np.exp((63.0 - idx) * lg)[None, :]
            c[rows, K_DMASK2 + 64 * hp:K_DMASK2 + 64 * hp + 64] = np.exp(np.abs(idx[:, None] - idx[None, :]) * lg)
    return c


def host_cols(inp):
    out = np.zeros((DEPTH, 128, NCOL), np.float32)
    for l in range(DEPTH):
        o = out[l]

        def put8(c0, v):
            o[:, c0:c0 + 8] = v.reshape(8, 128).T
        put8(C_NMIX, inp["norm_mix"][l])
        put8(C_NXA, inp["norm_xattn"][l])
        put8(C_NMLP, inp["norm_mlp"][l])
        put8(C_NMEM, inp["norm_mem"][l])
        put8(C_NFIN, inp["final_norm"])
        o[:, C_QN] = inp["mla_q_norm"][l][:128]
        o[:64, C_QN + 1] = inp["mla_q_norm"][l][128:]
        o[:, C_KVN] = inp["mla_kv_norm"][l]
        for h in range(4):
            o[:64, C_RGN + h] = inp["ret_gn"][l][h]
        for ch in range(2):
            sl = slice(ch * 128, ch * 128 + 128)
            for k in range(4):
                o[:, C_LCW + ch * 4 + k] = inp["lru_conv_w"][l][k, sl]
            o[:, C_LCB + ch] = inp["lru_conv_b"][l][sl]
            o[:, C_LBA + ch] = inp["lru_ba"][l][sl]
            o[:, C_LBX + ch] = inp["lru_bx"][l][sl]
            o[:, C_LLAM + ch] = inp["lru_lambda"][l][sl]
        for part in range(3):
            for h in range(4):
                for k in range(4):
                    o[:64, C_GCW + (part * 4 + h) * 4 + k] = inp["gdn_conv_w"][l][k, part * 256 + h * 64: part * 256 + h * 64 + 64]
        o[:64, C_GNORM] = inp["gdn_norm"][l]
        o[:, C_GNORM2] = np.tile(inp["gdn_norm"][l], 2)
        for hp in range(2):
            o[:, C_RGN2 + hp] = inp["ret_gn"][l][2 * hp:2 * hp + 2].reshape(128)
            for part in range(3):
                for k in range(4):
                    o[:, C_GCW2 + (part * 2 + hp) * 4 + k] = inp["gdn_conv_w"][l][k, part * 256 + hp * 128: part * 256 + hp * 128 + 128]
            o[:, C_ALOG2 + hp] = np.repeat(inp["gdn_a_log"][l][2 * hp:2 * hp + 2], 64)
            o[:, C_DTB2 + hp] = np.repeat(inp["gdn_dt_bias"][l][2 * hp:2 * hp + 2], 64)
        for h in range(4):
            o[:, C_ALOG + h] = inp["gdn_a_log"][l][h]
            o[:, C_DTB + h] = inp["gdn_dt_bias"][l][h]
    return out


AW = 50600
TWO_PI_S = 6.2831793


class K:
    def __init__(self, nc, st, debug=()):
        self.nc = nc
        self.debug = set(debug)
        c = self.c = Ctx(nc, st)
        self.ar = Arena(c, st.enter_context(nc.sbuf_tensor("arena", [128, AW], F32)), AW)
        self.psl = [c.tile(st.enter_context(nc.psum_tensor("ps%d" % i, [128, 512], F32))) for i in range(8)]
        self.psi = 0
        self.CST = c.tile(st.enter_context(nc.sbuf_tensor("sb_cst", [128, NCST], F32)))
        self.COLS = c.tile(st.enter_context(nc.sbuf_tensor("sb_cols", [128, DEPTH, NCOL], F32)))
        self.ONESB = c.tile(st.enter_context(nc.sbuf_tensor("onesb", [128, 128], BF16)))
        ein = lambda name, shape, dt=F32: nc.dram_tensor(name, list(shape), dt, kind="ExternalInput").ap()
        self.d_in = {}
        for name, shape, dt in [
            ("xT", (DM, T), F32), ("memT", (DM, TM_), F32), ("pos", (1, T), I32),
            ("cst", (128, NCST), F32), ("cols", (DEPTH, 128, NCOL), F32),
            ("w_in", (DEPTH, DM, INC), F32), ("lru_wa", (DEPTH, 4, 64, 64), F32), ("lru_wx", (DEPTH, 4, 64, 64), F32),
            ("mla_w_uq", (DEPTH, 192, 384), F32), ("mla_w_ukv", (DEPTH, 128, 512), F32),
            ("w_out", (DEPTH, DM, DM), F32), ("xa_wq", (DEPTH, DM, DM), F32), ("xa_wkv", (DEPTH, DM, 2 * DM), F32),
            ("xa_wo", (DEPTH, DM, DM), F32), ("mlp_w1", (DEPTH, DM, 4 * DM), F32), ("mlp_w2", (DEPTH, 4 * DM, DM), F32),
        ]:
            self.d_in[name] = c.tile(ein(name, shape, dt))
        self.outT = DGrid(c, nc.dram_tensor("outT", [DM, T], F32, kind="ExternalOutput").ap(), XCH, 512)

        def scratch(name, shape, dt=F32):
            kind = "ExternalOutput" if name in self.debug else "Internal"
            return nc.dram_tensor(name, list(shape), dt, kind=kind).ap()
        self.XT = DGrid(c, scratch("XS", (DM, T)), XCH, 512)
        self.XIN = DGrid(c, self.d_in["xT"].t, XCH, 512)
        self.ZT = DGrid(c, scratch("ZT", (INC, T)), ZCH, 512)
        self.YT = DGrid(c, scratch("YT", (DM, T), BF16), XCH, 512)
        self.TAB = {n: DGrid(c, scratch(n, (128, T)), [(0, 128)], 512) for n in ("COSA", "SINA", "COSC", "SINC")}

    def ps(self):
        t = self.psl[self.psi % 8]
        self.psi += 1
        return t

    def col(self, l, j, P=128):
        return self.COLS[0:P, l, j:j + 1]

    def phase_setup(self):
        c, ar = self.c, self.ar
        c.sp.dma_start(out=self.CST[:, :], in_=self.d_in["cst"][:, :])
        for l in range(DEPTH):
            c.sp.dma_start(out=self.COLS[:, l, :], in_=self.d_in["cols"][l, :, :])
        c.dve.tensor_copy(out=self.ONESB[:, :], in_=self.CST[:, K_ONES:K_ONES + 128])
        ar.reset()
        POSI = ar.alloc(128, (T,), I32)
        POSF = ar.alloc(128, (T,), F32)
        U = ar.alloc(128, (T,), F32)
        UI = ar.alloc(128, (T,), I32)
        UF = ar.alloc(128, (T,), F32)
        R = ar.alloc(128, (T,), F32)
        posd = self.d_in["pos"]
        c.sp.dma_start(out=POSI[:, :], in_=posd.v(posd.t.partition_broadcast(128)))
        c.dve.tensor_copy(out=POSF[:, :], in_=POSI[:, :])
        for kcol, cn, sn in ((K_INVA, "COSA", "SINA"), (K_INVC, "COSC", "SINC")):
            for name, shift in ((sn, 0.0), (cn, 0.25)):
                c.dve.tensor_scalar(out=U[:, :], in0=POSF[:, :], scalar1=self.CST[:, kcol:kcol + 1],
                                    scalar2=1.0 / (2.0 * math.pi), op0=ALU.mult, op1=ALU.mult)
                if shift:
                    c.dve.tensor_scalar(out=U[:, :], in0=U[:, :], scalar1=shift, scalar2=None, op0=ALU.add)
                c.dve.tensor_copy(out=UI[:, :], in_=U[:, :])
                c.dve.tensor_copy(out=UF[:, :], in_=UI[:, :])
                c.dve.tensor_tensor(out=R[:, :], in0=U[:, :], in1=UF[:, :], op=ALU.subtract)
                c.act.activation(out=R[:, :], in_=R[:, :], func=AF.Sin, scale=TWO_PI_S)
                for tb in range(T // 512):
                    c.pool.dma_start(out=self.TAB[name].v(0, 128, tb), in_=R[:, tb * 512:(tb + 1) * 512])

    def load_w_bf16(self, W, src_tile, src_ap_fn, nk, ncols, stg):
        c = self.c
        for k in range(nk):
            s = stg[k % len(stg)]
            c.sp.dma_start(out=s[:, 0:ncols], in_=src_tile.v(src_ap_fn(k)))
            if k % 2:
                c.act.activation(out=W[:, k, :], in_=s[:, 0:ncols], func=AF.Copy)
            else:
                c.dve.tensor_copy(out=W[:, k, :], in_=s[:, 0:ncols])

    def rms_stats(self, x, nk, n, sq, rs, kparts=None):
        c = self.c
        c.act.activation(out=sq[:, :, :], in_=x[:, :, :], func=AF.Square)
        p = self.ps()
        for k in range(nk):
            c.pe.matmul(out=p[:, 0:n], lhsT=self.ONESB[:, :], rhs=sq[:, k, :], start=(k == 0), stop=(k == nk - 1))
        c.act.activation(out=rs[:, 0:n], in_=p[:, 0:n], func=AF.Ln, scale=1.0 / (128 * nk), bias=EPS)
        c.act.activation(out=rs[:, 0:n], in_=rs[:, 0:n], func=AF.Exp, scale=-0.5)

    def phase_proj(self, l, XSRC):
        c, ar = self.c, self.ar
        c.barrier()
        ar.reset()
        CG = [(0, 1024), (1024, 1888), (1888, INC)]
        WG = [ar.alloc(128, (8, c1 - c0), BF16) for c0, c1 in CG]
        stg = [ar.alloc(128, (1032,), F32) for _ in range(4)]
        win = self.d_in["w_in"]
        X32 = [ar.alloc(128, (8, 512), F32) for _ in range(2)]
        for k in range(8):
            c.sp.dma_start(out=X32[0][:, k, :], in_=XSRC.v(k * 128, 128, 0))
        wi = [0]

        def load_group(gi, act_only):
            c0, c1 = CG[gi]
            for k in range(8):
                s_ = stg[wi[0] % 4]
                c.sp.dma_start(out=s_[:, 0:c1 - c0], in_=win.v(win.t[l, k * 128:(k + 1) * 128, c0:c1]))
                if act_only or wi[0] % 2:
                    c.act.activation(out=WG[gi][:, k, :], in_=s_[:, 0:c1 - c0], func=AF.Copy)
                else:
                    c.dve.tensor_copy(out=WG[gi][:, k, :], in_=s_[:, 0:c1 - c0])
                wi[0] += 1

        load_group(0, False)

        def wsl(k, s0, n):
            for gi, (c0, c1) in enumerate(CG):
                if c0 <= s0 < c1:
                    return WG[gi][:, k, s0 - c0:s0 - c0 + n]
        SQ = ar.alloc(128, (8, 512), BF16)
        XG = [ar.alloc(128, (8, 512), BF16) for _ in range(2)]
        RS = [ar.alloc(128, (512,), F32) for _ in range(2)]
        ZS = [ar.alloc(128, (512,), F32) for _ in range(4)]
        for tb in range(T // 512):
            x = X32[tb % 2]
            if tb > 0:
                for k in range(8):
                    c.sp.dma_start(out=x[:, k, :], in_=XSRC.v(k * 128, 128, tb))
            rs = RS[tb % 2]
            self.rms_stats(x, 8, 512, SQ, rs)
            xg = XG[tb % 2]
            for k in range(8):
                c.dve.tensor_scalar(out=xg[:, k, :], in0=x[:, k, :], scalar1=self.col(l, C_NMIX + k), scalar2=None,
                                    op0=ALU.mult)
            if tb == 0:
                load_group(1, True)
                load_group(2, True)
            for ci, (s0, n) in enumerate(ZCH):
                p = self.ps()
                for k in range(8):
                    c.pe.matmul(out=p[0:n, :], lhsT=wsl(k, s0, n), rhs=xg[:, k, :], start=(k == 0), stop=(k == 7))
                z = ZS[ci % 4]
                c.dve.tensor_tensor(out=z[0:n, :], in0=p[0:n, :], in1=rs[0:n, :], op=ALU.mult)
                c.pool.dma_start(out=self.ZT.v(s0, n, tb), in_=z[0:n, :])

    def load_seq_rows(self, dst, grid, r0, P, b, eng=None):
        eng = eng or self.c.sp
        for j in range(4):
            eng.dma_start(out=dst[0:P, j * 512:(j + 1) * 512], in_=grid.v(r0, P, b * 4 + j))

    def store_seq_rows(self, grid, r0, P, b, src):
        for j in range(4):
            self.c.pool.dma_start(out=grid.v(r0, P, b * 4 + j), in_=src[0:P, j * 512:(j + 1) * 512])

    def conv4(self, out, x, l, colbase, P, bias_col=None):
        c = self.c
        w = lambda k: self.col(l, colbase + k, P)
        if bias_col is None:
            c.dve.tensor_scalar(out=out[0:P, :], in0=x[0:P, :], scalar1=w(3), scalar2=None, op0=ALU.mult)
        else:
            c.dve.tensor_scalar(out=out[0:P, :], in0=x[0:P, :], scalar1=w(3), scalar2=self.col(l, bias_col, P),
                                op0=ALU.mult, op1=ALU.add)
        for sh in (1, 2, 3):
            c.dve.scalar_tensor_tensor(out=out[0:P, sh:S], in0=x[0:P, 0:S - sh], scalar=w(3 - sh), in1=out[0:P, sh:S],
                                       op0=ALU.mult, op1=ALU.add)

    def phase_lru(self, l):
        c, ar = self.c, self.ar
        c.barrier()
        ar.reset()
        WA = ar.alloc(128, (128,), F32)
        WX = ar.alloc(128, (128,), F32)
        SP_ = ar.alloc(128, (8,), F32)
        XB = ar.alloc(128, (S,), F32)
        GB = ar.alloc(128, (S,), F32)
        XC = ar.alloc(128, (S,), F32)
        RG = ar.alloc(128, (S,), F32)
        IG = ar.alloc(128, (S,), F32)
        A_ = ar.alloc(128, (S,), F32)
        U_ = ar.alloc(128, (S,), F32)
        H_ = ar.alloc(128, (S,), F32)
        YB = ar.alloc(128, (S,), BF16)
        for ch in range(2):
            for W, nm in ((WA, "lru_wa"), (WX, "lru_wx")):
                c.pool.memset(ap=W[:, :], constant=0.0)
                src = self.d_in[nm]
                for hb in range(2):
                    c.sp.dma_start(out=W[64 * hb:64 * hb + 64, 64 * hb:64 * hb + 64], in_=src[l, 2 * ch + hb, :, :])
            lam = self.col(l, C_LLAM + ch)
            c.dve.tensor_scalar(out=SP_[:, 0:1], in0=lam, scalar1=-1.0, scalar2=None, op0=ALU.mult)
            c.dve.tensor_tensor(out=SP_[:, 1:2], in0=SP_[:, 0:1], in1=lam, op=ALU.max)
            c.act.activation(out=SP_[:, 2:3], in_=SP_[:, 1:2], func=AF.Exp, scale=-1.0)
            c.act.activation(out=SP_[:, 2:3], in_=SP_[:, 2:3], func=AF.Ln, bias=1.0)
            c.dve.tensor_scalar(out=SP_[:, 3:4], in0=SP_[:, 0:1], scalar1=0.0, scalar2=None, op0=ALU.max)
            c.dve.tensor_tensor(out=SP_[:, 3:4], in0=SP_[:, 3:4], in1=SP_[:, 2:3], op=ALU.add)
            c.dve.tensor_scalar(out=SP_[:, 4:5], in0=SP_[:, 3:4], scalar1=-8.0, scalar2=None, op0=ALU.mult)
            c.dve.tensor_scalar(out=SP_[:, 5:6], in0=SP_[:, 3:4], scalar1=-16.0, scalar2=None, op0=ALU.mult)
            for b in range(BL):
                self.load_seq_rows(XB, self.ZT, 1024 + ch * 128, 128, b)
                self.load_seq_rows(GB, self.ZT, 1280 + ch * 128, 128, b)
                self.conv4(XC, XB, l, C_LCW + ch * 4, 128, bias_col=C_LCB + ch)
                for j in range(4):
                    sl = slice(j * 512, (j + 1) * 512)
                    p = self.ps()
                    c.pe.matmul(out=p[:, :], lhsT=WA[:, :], rhs=XC[:, sl], start=True, stop=True)
                    c.act.activation(out=RG[:, sl], in_=p[:, :], func=AF.Sigmoid, bias=self.col(l, C_LBA + ch))
                    p = self.ps()
                    c.pe.matmul(out=p[:, :], lhsT=WX[:, :], rhs=XC[:, sl], start=True, stop=True)
                    c.act.activation(out=IG[:, sl], in_=p[:, :], func=AF.Sigmoid, bias=self.col(l, C_LBX + ch))
                c.act.activation(out=A_[:, :], in_=RG[:, :], func=AF.Exp, scale=SP_[:, 4:5])
                c.act.activation(out=U_[:, :], in_=RG[:, :], func=AF.Exp, scale=SP_[:, 5:6])
                c.dve.tensor_scalar(out=U_[:, :], in0=U_[:, :], scalar1=-1.0, scalar2=1.0, op0=ALU.mult, op1=ALU.add)
                c.dve.tensor_scalar(out=U_[:, :], in0=U_[:, :], scalar1=0.0, scalar2=None, op0=ALU.max)
                c.act.activation(out=U_[:, :], in_=U_[:, :], func=AF.Sqrt)
                c.dve.tensor_tensor(out=IG[:, :], in0=IG[:, :], in1=XC[:, :], op=ALU.mult)
                c.dve.tensor_tensor(out=U_[:, :], in0=U_[:, :], in1=IG[:, :], op=ALU.mult)
                c.dve.tensor_tensor_scan(out=H_[:, :], data0=A_[:, :], data1=U_[:, :], initial=0.0, op0=ALU.mult, op1=ALU.add)
                c.pool.tensor_tensor(out=RG[:, :], in0=GB[:, :], in1=GB[:, :], op=ALU.mult)
                c.pool.tensor_scalar(out=RG[:, :], in0=RG[:, :], scalar1=0.044715, scalar2=1.0, op0=ALU.mult, op1=ALU.add)
                c.pool.tensor_tensor(out=RG[:, :], in0=RG[:, :], in1=GB[:, :], op=ALU.mult)
                c.act.activation(out=RG[:, :], in_=RG[:, :], func=AF.Tanh, scale=0.7978845608028654)
                c.pool.tensor_scalar(out=RG[:, :], in0=RG[:, :], scalar1=0.5, scalar2=0.5, op0=ALU.mult, op1=ALU.add)
                c.pool.tensor_tensor(out=RG[:, :], in0=RG[:, :], in1=GB[:, :], op=ALU.mult)
                c.dve.tensor_tensor(out=YB[:, :], in0=H_[:, :], in1=RG[:, :], op=ALU.mult)
                self.store_seq_rows(self.YT, 256 + ch * 128, 128, b, YB)

    def lru_gen(self, l):
        c, ar = self.c, self.ar
        WA = ar.alloc(128, (128,), F32)
        WX = ar.alloc(128, (128,), F32)
        SP_ = ar.alloc(128, (8,), F32)
        XB, GB, XC, RG, IG, A_, U_, H_ = [ar.alloc(128, (S,), F32) for _ in range(8)]
        YB = ar.alloc(128, (S,), BF16)
        pl = self.psl[7]
        for ch in range(2):
            for W, nm in ((WA, "lru_wa"), (WX, "lru_wx")):
                c.pool.memset(ap=W[:, :], constant=0.0)
                src = self.d_in[nm]
                for hb in range(2):
                    c.sp.dma_start(out=W[64 * hb:64 * hb + 64, 64 * hb:64 * hb + 64], in_=src[l, 2 * ch + hb, :, :])
            yield
            lam = self.col(l, C_LLAM + ch)
            c.dve.tensor_scalar(out=SP_[:, 0:1], in0=lam, scalar1=-1.0, scalar2=None, op0=ALU.mult)
            c.dve.tensor_tensor(out=SP_[:, 1:2], in0=SP_[:, 0:1], in1=lam, op=ALU.max)
            c.act.activation(out=SP_[:, 2:3], in_=SP_[:, 1:2], func=AF.Exp, scale=-1.0)
            c.act.activation(out=SP_[:, 2:3], in_=SP_[:, 2:3], func=AF.Ln, bias=1.0)
            c.dve.tensor_scalar(out=SP_[:, 3:4], in0=SP_[:, 0:1], scalar1=0.0, scalar2=None, op0=ALU.max)
            c.dve.tensor_tensor(out=SP_[:, 3:4], in0=SP_[:, 3:4], in1=SP_[:, 2:3], op=ALU.add)
            c.dve.tensor_scalar(out=SP_[:, 4:5], in0=SP_[:, 3:4], scalar1=-8.0, scalar2=None, op0=ALU.mult)
            c.dve.tensor_scalar(out=SP_[:, 5:6], in0=SP_[:, 3:4], scalar1=-16.0, scalar2=None, op0=ALU.mult)
            yield
            for b in range(BL):
                self.load_seq_rows(XB, self.ZT, 1024 + ch * 128, 128, b)
                self.load_seq_rows(GB, self.ZT, 1280 + ch * 128, 128, b)
                yield
                w = lambda k: self.col(l, C_LCW + ch * 4 + k)
                c.dve.tensor_scalar(out=XC[:, :], in0=XB[:, :], scalar1=w(3), scalar2=self.col(l, C_LCB + ch), op0=ALU.mult, op1=ALU.add)
                yield
                for sh in (1, 2, 3):
                    c.dve.scalar_tensor_tensor(out=XC[:, sh:S], in0=XB[:, 0:S - sh], scalar=w(3 - sh), in1=XC[:, sh:S], op0=ALU.mult, op1=ALU.add)
                    yield
                for j in range(4):
                    sl = slice(j * 512, (j + 1) * 512)
                    c.pe.matmul(out=pl[:, :], lhsT=WA[:, :], rhs=XC[:, sl], start=True, stop=True)
                    c.act.activation(out=RG[:, sl], in_=pl[:, :], func=AF.Sigmoid, bias=self.col(l, C_LBA + ch))
                    yield
                    c.pe.matmul(out=pl[:, :], lhsT=WX[:, :], rhs=XC[:, sl], start=True, stop=True)
                    c.act.activation(out=IG[:, sl], in_=pl[:, :], func=AF.Sigmoid, bias=self.col(l, C_LBX + ch))
                    yield
                c.act.activation(out=A_[:, :], in_=RG[:, :], func=AF.Exp, scale=SP_[:, 4:5])
                yield
                c.act.activation(out=U_[:, :], in_=RG[:, :], func=AF.Exp, scale=SP_[:, 5:6])
                yield
                c.dve.tensor_scalar(out=U_[:, :], in0=U_[:, :], scalar1=-1.0, scalar2=1.0, op0=ALU.mult, op1=ALU.add)
                yield
                c.dve.tensor_scalar(out=U_[:, :], in0=U_[:, :], scalar1=0.0, scalar2=None, op0=ALU.max)
                yield
                c.act.activation(out=U_[:, :], in_=U_[:, :], func=AF.Sqrt)
                yield
                c.pool.tensor_tensor(out=IG[:, :], in0=IG[:, :], in1=XC[:, :], op=ALU.mult)
                yield
                c.dve.tensor_tensor(out=U_[:, :], in0=U_[:, :], in1=IG[:, :], op=ALU.mult)
                yield
                c.dve.tensor_tensor_scan(out=H_[:, :], data0=A_[:, :], data1=U_[:, :], initial=0.0, op0=ALU.mult, op1=ALU.add)
                yield
                c.pool.tensor_tensor(out=RG[:, :], in0=GB[:, :], in1=GB[:, :], op=ALU.mult)
                yield
                c.pool.tensor_scalar(out=RG[:, :], in0=RG[:, :], scalar1=0.044715, scalar2=1.0, op0=ALU.mult, op1=ALU.add)
                yield
                c.pool.tensor_tensor(out=RG[:, :], in0=RG[:, :], in1=GB[:, :], op=ALU.mult)
                yield
                c.act.activation(out=RG[:, :], in_=RG[:, :], func=AF.Tanh, scale=0.7978845608028654)
                yield
                c.pool.tensor_scalar(out=RG[:, :], in0=RG[:, :], scalar1=0.5, scalar2=0.5, op0=ALU.mult, op1=ALU.add)
                yield
                c.pool.tensor_tensor(out=RG[:, :], in0=RG[:, :], in1=GB[:, :], op=ALU.mult)
                yield
                c.dve.tensor_tensor(out=YB[:, :], in0=H_[:, :], in1=RG[:, :], op=ALU.mult)
                yield
                self.store_seq_rows(self.YT, 256 + ch * 128, 128, b, YB)
                yield

    def rope64(self, dst, src, COS, SIN, T2, rot_k, P):
        c = self.c
        for j in range(4):
            sl = slice(j * 512, (j + 1) * 512)
            p = self.ps()
            c.pe.matmul(out=p[0:P, :], lhsT=self.CST[0:P, rot_k:rot_k + P], rhs=src[0:P, sl], start=True, stop=True)
            c.dve.tensor_tensor(out=T2[0:P, sl], in0=p[0:P, :], in1=SIN[0:P, sl], op=ALU.mult)
        c.pool.tensor_tensor(out=dst[0:P, :], in0=src[0:P, :], in1=COS[0:P, :], op=ALU.mult)
        c.pool.tensor_tensor(out=dst[0:P, :], in0=dst[0:P, :], in1=T2[0:P, :], op=ALU.add)

    def tok_major(self, dst, src, evac_scalar=None):
        c = self.c
        for g in range(4):
            p = self.ps()
            for s_ in range(8):
                n = g * 8 + s_
                c.pe.transpose(out=p[0:64, s_ * 64:s_ * 64 + 64], in_=src[0:64, n * 64:n * 64 + 64],
                               identity=self.CST[0:64, K_ID:K_ID + 64])
            pv = p.v(p.t[0:64, :].rearrange("p (s e) -> p s e", s=8))
            if evac_scalar is None:
                c.act.activation(out=dst[:, g * 8:g * 8 + 8, :], in_=pv, func=AF.Copy)
            else:
                c.dve.tensor_tensor(out=dst[:, g * 8:g * 8 + 8, :], in0=pv, in1=evac_scalar(g), op=ALU.mult)

    def phase_ret(self, l):
        c, ar = self.c, self.ar
        c.barrier()
        ar.reset()
        f32 = lambda: ar.alloc(64, (S,), F32)
        Q, Kt, V, G, COS, SIN, T1, T2, KZ, KV, G64, SS = [f32() for _ in range(12)]
        QH = ar.alloc(64, (S,), BF16)
        KH = ar.alloc(64, (S,), BF16)
        QX = ar.alloc(64, (S,), BF16)
        KZT = ar.alloc(64, (32, 64), BF16)
        VT = ar.alloc(64, (32, 64), BF16)
        SB = ar.alloc(64, (33, 64), BF16)
        SC = [ar.alloc(64, (512,), BF16) for _ in range(2)]
        OS = ar.alloc(64, (512,), F32)
        CEN = ar.alloc(64, (512,), F32)
        SQ_ = ar.alloc(64, (512,), F32)
        RSD = ar.alloc(64, (512,), F32)
        SG = ar.alloc(64, (512,), F32)
        YO = [ar.alloc(64, (512,), BF16) for _ in range(2)]
        ones64 = self.CST[0:64, K_B64:K_B64 + 64]
        v3 = lambda t_: t_.v(t_.t.rearrange("p (n i) -> p n i", i=64))
        for b in range(BL):
            self.load_seq_rows(COS, self.TAB["COSA"], 0, 64, b)
            self.load_seq_rows(SIN, self.TAB["SINA"], 0, 64, b)
            for h in range(4):
                self.load_seq_rows(Q, self.ZT, 0 + h * 64, 64, b)
                self.load_seq_rows(Kt, self.ZT, 256 + h * 64, 64, b)
                self.load_seq_rows(V, self.ZT, 512 + h * 64, 64, b)
                self.load_seq_rows(G, self.ZT, 768 + h * 64, 64, b)
                xi = self.CST.v(self.CST.t[0:64, K_XI + 64 * h:K_XI + 64 * h + 64].unsqueeze(1).to_broadcast([64, 32, 64]))
                ze = self.CST.v(self.CST.t[0:64, K_ZETA + 64 * h:K_ZETA + 64 * h + 64].unsqueeze(1).to_broadcast([64, 32, 64]))
                dm = self.CST.v(self.CST.t[0:64, K_DMASK + 64 * h:K_DMASK + 64 * h + 64].unsqueeze(1).to_broadcast([64, 8, 64]))
                self.rope64(T1, Q, COS, SIN, T2, K_ROTA, 64)
                c.dve.tensor_scalar(out=QH[:, :], in0=T1[:, :], scalar1=0.125, scalar2=None, op0=ALU.mult)
                c.pool.tensor_tensor(out=v3(QX), in0=v3(T1), in1=xi, op=ALU.mult)
                self.rope64(T1, Kt, COS, SIN, T2, K_ROTA, 64)
                c.dve.tensor_copy(out=KH[:, :], in_=T1[:, :])
                c.pool.tensor_tensor(out=v3(KZ), in0=v3(T1), in1=ze, op=ALU.mult)
                self.tok_major(KZT, KZ)
                self.tok_major(VT, V)
                kv3 = KV.t.rearrange("p (e n) -> p e n", e=64)
                for g in range(4):
                    p = self.ps()
                    for s_ in range(8):
                        n = g * 8 + s_
                        c.pe.matmul(out=p[0:64, s_ * 64:s_ * 64 + 64], lhsT=KZT[:, n, :], rhs=VT[:, n, :], start=True, stop=True)
                    c.dve.tensor_copy(out=KV.v(kv3[:, :, g * 8:g * 8 + 8]),
                                      in_=p.v(p.t[0:64, :].rearrange("p (s e) -> p e s", s=8)))
                g64 = math.exp(64.0 * math.log1p(-2.0 ** (-5.0 - h)))
                c.pool.memset(ap=G64[:, :], constant=g64)
                c.pool.memset(ap=G64.v(G64.t.rearrange("p (e n) -> p e n", e=64)[:, :, 0:1]), constant=0.0)
                c.dve.tensor_tensor_scan(out=SS[:, :], data0=G64[:, :], data1=KV[:, :], initial=0.0, op0=ALU.mult, op1=ALU.add)
                c.pool.memset(ap=SB[:, 0, :], constant=0.0)
                c.pool.tensor_copy(out=SB[:, 1:33, :], in_=SS.v(SS.t.rearrange("p (e n) -> p n e", e=64)))
                for g in range(4):
                    sl = slice(g * 512, (g + 1) * 512)
                    pS = self.ps()
                    for s_ in range(8):
                        n = g * 8 + s_
                        cs = slice(n * 64, n * 64 + 64)
                        c.pe.matmul(out=pS[0:64, s_ * 64:s_ * 64 + 64], lhsT=KH[:, cs], rhs=QH[:, cs], start=True, stop=True)
                    sc = SC[g % 2]
                    c.dve.tensor_tensor(out=sc.v(sc.t.rearrange("p (s i) -> p s i", s=8)),
                                        in0=pS.v(pS.t[0:64, :].rearrange("p (s i) -> p s i", s=8)), in1=dm, op=ALU.mult)
                    pO = self.ps()
                    for s_ in range(8):
                        n = g * 8 + s_
                        cs = slice(n * 64, n * 64 + 64)
                        c.pe.matmul(out=pO[0:64, s_ * 64:s_ * 64 + 64], lhsT=VT[:, n, :], rhs=sc[:, s_ * 64:s_ * 64 + 64],
                                    start=True, stop=False)
                        c.pe.matmul(out=pO[0:64, s_ * 64:s_ * 64 + 64], lhsT=SB[:, n, :], rhs=QX[:, cs], start=False, stop=True)
                    c.act.activation(out=OS[:, :], in_=pO[0:64, :], func=AF.Copy)
                    p2 = self.ps()
                    c.pe.matmul(out=p2[0:64, :], lhsT=ones64, rhs=OS[:, :], start=True, stop=True)
                    c.dve.scalar_tensor_tensor(out=CEN[:, :], in0=p2[0:64, :], scalar=-1.0 / 64, in1=OS[:, :], op0=ALU.mult, op1=ALU.add)
                    c.act.activation(out=SQ_[:, :], in_=CEN[:, :], func=AF.Square)
                    p3 = self.ps()
                    c.pe.matmul(out=p3[0:64, :], lhsT=ones64, rhs=SQ_[:, :], start=True, stop=True)
                    c.act.activation(out=RSD[:, :], in_=p3[0:64, :], func=AF.Ln, scale=1.0 / 64, bias=EPS)
                    c.act.activation(out=RSD[:, :], in_=RSD[:, :], func=AF.Exp, scale=-0.5)
                    c.act.activation(out=SG[:, :], in_=G[:, sl], func=AF.Silu)
                    c.dve.tensor_tensor(out=CEN[:, :], in0=CEN[:, :], in1=RSD[:, :], op=ALU.mult)
                    yo = YO[g % 2]
                    c.dve.scalar_tensor_tensor(out=yo[:, :], in0=CEN[:, :], scalar=self.col(l, C_RGN + h, 64), in1=SG[:, :],
                                               op0=ALU.mult, op1=ALU.mult)
                    c.pool.dma_start(out=self.YT.v(h * 64, 64, b * 4 + g), in_=yo[:, :])

    def phase_mla(self, l):
        c, ar = self.c, self.ar
        c.barrier()
        ar.reset()
        SCALE = 96.0 ** -0.5
        WUQ = ar.alloc(128, (2, 384), BF16)
        WUKV = ar.alloc(128, (512,), BF16)
        STG = ar.alloc(128, (512,), F32)
        wq = self.d_in["mla_w_uq"]
        c.sp.dma_start(out=STG[:, 0:384], in_=wq[l, 0:128, :])
        c.dve.tensor_copy(out=WUQ[:, 0, :], in_=STG[:, 0:384])
        c.sp.dma_start(out=STG[0:64, 0:384], in_=wq[l, 128:192, :])
        c.dve.tensor_copy(out=WUQ[0:64, 1, :], in_=STG[0:64, 0:384])
        c.sp.dma_start(out=STG[:, :], in_=self.d_in["mla_w_ukv"][l, :, :])
        c.dve.tensor_copy(out=WUKV[:, :], in_=STG[:, :])
        QA = ar.alloc(96, (4, S), BF16)
        KA = ar.alloc(96, (4, S), BF16)
        VTK = ar.alloc(128, (16, 4 * 65), BF16)
        vtk4 = lambda t0, t1: VTK.v(VTK.t[:, t0:t1, :].rearrange("p t (h x) -> p t h x", h=4))
        CQ = ar.alloc(128, (2, 512), F32)
        CKV = ar.alloc(128, (512,), F32)
        KRR = ar.alloc(96, (512,), F32)
        COS = ar.alloc(96, (512,), F32)
        SIN = ar.alloc(96, (512,), F32)
        SQ = ar.alloc(128, (2, 512), BF16)
        SQK = ar.alloc(128, (512,), BF16)
        RSQ = ar.alloc(128, (512,), F32)
        RSK = ar.alloc(128, (512,), F32)
        RC = ar.alloc(128, (4,), F32)
        CQG = ar.alloc(128, (2, 512), BF16)
        CKVG = ar.alloc(128, (512,), BF16)
        QRF = ar.alloc(96, (512,), F32)
        T1 = ar.alloc(96, (512,), F32)
        T2 = ar.alloc(96, (512,), F32)
        PT = [ar.alloc(128, (512,), BF16) for _ in range(4)]
        RROW = ar.alloc(65, (512,), F32)
        RSM = ar.alloc(64, (512,), F32)
        c.pool.memset(ap=VTK[:, :, :], constant=1.0)
        YC = [ar.alloc(64, (512,), BF16) for _ in range(2)]
        rotc = self.CST[64:96, K_ROTC + 64:K_ROTC + 96]
        ones_f = self.CST[:, K_ONES:K_ONES + 64]
        lg = self.lru_gen(l)
        R = slice(64, 96)

        def rope32(dst_views, src):
            p2 = self.ps()
            c.pe.matmul(out=p2[R, :], lhsT=rotc, rhs=src[R, :], start=True, stop=True)
            c.dve.tensor_tensor(out=T2[R, :], in0=p2[R, :], in1=SIN[R, :], op=ALU.mult)
            c.pool.tensor_tensor(out=T1[R, :], in0=src[R, :], in1=COS[R, :], op=ALU.mult)
            for i, dv in enumerate(dst_views):
                (c.dve if i % 2 == 0 else c.pool).tensor_tensor(out=dv, in0=T1[R, :], in1=T2[R, :], op=ALU.add)

        pti = 0
        for b in range(BL):
            for j in range(4):
                tb = b * 4 + j
                tk = slice(j * 512, (j + 1) * 512)
                c.sp.dma_start(out=CQ[:, 0, :], in_=self.ZT.v(1536, 128, tb))
                c.sp.dma_start(out=CQ[0:64, 1, :], in_=self.ZT.v(1664, 64, tb))
                c.sp.dma_start(out=CKV[:, :], in_=self.ZT.v(1728, 128, tb))
                c.sp.dma_start(out=KRR[R, :], in_=self.ZT.v(1856, 32, tb))
                c.sp.dma_start(out=COS[R, :], in_=self.TAB["COSC"].v(64, 32, tb))
                c.sp.dma_start(out=SIN[R, :], in_=self.TAB["SINC"].v(64, 32, tb))
                c.act.activation(out=SQ[:, 0, :], in_=CQ[:, 0, :], func=AF.Square)
                c.act.activation(out=SQ[0:64, 1, :], in_=CQ[0:64, 1, :], func=AF.Square)
                p = self.ps()
                c.pe.matmul(out=p[:, :], lhsT=self.ONESB[:, :], rhs=SQ[:, 0, :], start=True, stop=False)
                c.pe.matmul(out=p[:, :], lhsT=self.ONESB[0:64, :], rhs=SQ[0:64, 1, :], start=False, stop=True)
                c.act.activation(out=RSQ[:, :], in_=p[:, :], func=AF.Ln, scale=1.0 / 192, bias=EPS)
                c.act.activation(out=RSQ[:, :], in_=RSQ[:, :], func=AF.Exp, scale=-0.5)
                c.dve.tensor_scalar(out=CQG[:, 0, :], in0=CQ[:, 0, :], scalar1=self.col(l, C_QN), scalar2=None, op0=ALU.mult)
                c.dve.tensor_scalar(out=CQG[0:64, 1, :], in0=CQ[0:64, 1, :], scalar1=self.col(l, C_QN + 1, 64), scalar2=None, op0=ALU.mult)
                for h in range(4):
                    p = self.ps()
                    c.pe.matmul(out=p[0:64, :], lhsT=WUQ[:, 0, h * 96:h * 96 + 64], rhs=CQG[:, 0, :], start=True, stop=False)
                    c.pe.matmul(out=p[0:64, :], lhsT=WUQ[0:64, 1, h * 96:h * 96 + 64], rhs=CQG[0:64, 1, :], start=False, stop=True)
                    c.dve.scalar_tensor_tensor(out=QA[0:64, h, tk], in0=p[0:64, :], scalar=SCALE, in1=RSQ[0:64, :], op0=ALU.mult, op1=ALU.mult)
                    p = self.ps()
                    c.pe.matmul(out=p[R, :], lhsT=WUQ[:, 0, h * 96 + 64:h * 96 + 96], rhs=CQG[:, 0, :], start=True, stop=False)
                    c.pe.matmul(out=p[R, :], lhsT=WUQ[0:64, 1, h * 96 + 64:h * 96 + 96], rhs=CQG[0:64, 1, :], start=False, stop=True)
                    c.dve.scalar_tensor_tensor(out=QRF[R, :], in0=p[R, :], scalar=SCALE, in1=RSQ[R, :], op0=ALU.mult, op1=ALU.mult)
                    rope32([QA[R, h, tk]], QRF)
                c.act.activation(out=SQK[:, :], in_=CKV[:, :], func=AF.Square)
                p = self.ps()
                c.pe.matmul(out=p[:, :], lhsT=self.ONESB[:, :], rhs=SQK[:, :], start=True, stop=True)
                c.act.activation(out=RSK[:, :], in_=p[:, :], func=AF.Ln, scale=1.0 / 128, bias=EPS)
                c.act.activation(out=RSK[:, :], in_=RSK[:, :], func=AF.Exp, scale=-0.5)
                c.dve.tensor_scalar(out=CKVG[:, :], in0=CKV[:, :], scalar1=self.col(l, C_KVN), scalar2=None, op0=ALU.mult)
                for h in range(4):
                    p = self.ps()
                    c.pe.matmul(out=p[0:64, :], lhsT=WUKV[:, h * 128:h * 128 + 64], rhs=CKVG[:, :], start=True, stop=True)
                    c.dve.tensor_tensor(out=KA[0:64, h, tk], in0=p[0:64, :], in1=RSK[0:64, :], op=ALU.mult)
                pc = self.ps()
                for sub in range(4):
                    c.pe.matmul(out=pc[:, sub:sub + 1], lhsT=SQK[:, sub * 128:sub * 128 + 128], rhs=self.ONESB[:, 0:1], start=True, stop=True)
                c.act.activation(out=RC[:, :], in_=pc[:, 0:4], func=AF.Ln, scale=1.0 / 128, bias=EPS)
                c.act.activation(out=RC[:, :], in_=RC[:, :], func=AF.Exp, scale=-0.5)
                wv = WUKV.v(WUKV.t.rearrange("p (h x) -> p h x", h=4)[:, :, 64:128])
                for sub in range(4):
                    p = self.ps()
                    c.pe.matmul(out=p.v(p.t[:, 0:256].rearrange("p (h e) -> p h e", h=4)), lhsT=CKVG[:, sub * 128:sub * 128 + 128], rhs=wv, start=True, stop=True)
                    c.dve.tensor_scalar(out=VTK.v(VTK.t[:, j * 4 + sub, :].rearrange("p (h x) -> p h x", h=4)[:, :, 0:64]),
                                        in0=p.v(p.t[:, 0:256].rearrange("p (h e) -> p h e", h=4)), scalar1=RC[:, sub:sub + 1], scalar2=None, op0=ALU.mult)
                rope32([KA[R, h, tk] for h in range(4)], KRR)
            items = [(h, qb, kt) for h in range(4) for qb in range(4) for kt in range(4 * qb + 4)]

            def score(idx):
                nonlocal pti
                h, qb, kt = items[idx]
                jd = kt - 4 * qb
                q0 = 128 * jd if jd > 0 else 0
                n = 512 - q0
                ks = slice(kt * 128, kt * 128 + 128)
                qs = slice(qb * 512 + q0, qb * 512 + 512)
                pT = self.psl[3 + pti % 4]
                pt = PT[pti % 4]
                pti += 1
                c.pe.matmul(out=pT[:, 0:n], lhsT=KA[:, h, ks], rhs=QA[:, h, qs], start=True, stop=True)
                c.act.activation(out=pt[:, 0:n], in_=pT[:, 0:n], func=AF.Exp)
                if jd >= 0:
                    c.pool.memset(ap=pt[64:128, 0:64], constant=0.0)
                return pt, q0, n

            LA = 2
            pend = [score(i) for i in range(LA)]
            it = 0
            for idx, (h, qb, kt) in enumerate(items):
                pt, q0, n = pend.pop(0)
                if idx + LA < len(items):
                    pend.append(score(idx + LA))
                if idx % 3 == 0:
                    next(lg, None)
                nkt = 4 * qb + 4
                pO = self.psl[it % 2]
                c.pe.matmul(out=pO[0:65, q0:512], lhsT=VTK[:, kt, h * 65:h * 65 + 65], rhs=pt[:, 0:n], start=(kt == 0), stop=(kt == nkt - 1))
                if kt == nkt - 1:
                    c.act.activation(out=RROW[64:65, :], in_=pO[64:65, :], func=AF.Ln)
                    c.act.activation(out=RROW[64:65, :], in_=RROW[64:65, :], func=AF.Exp, scale=-1.0)
                    pM = self.psl[2]
                    c.pe.matmul(out=pM[0:64, :], lhsT=self.CST[64:65, K_ONES:K_ONES + 64], rhs=RROW[64:65, :], start=True, stop=True)
                    c.act.activation(out=RSM[:, :], in_=pM[0:64, :], func=AF.Copy)
                    yc = YC[it % 2]
                    c.dve.tensor_tensor(out=yc[:, :], in0=pO[0:64, :], in1=RSM[:, :], op=ALU.mult)
                    c.pool.dma_start(out=self.YT.v(512 + h * 64, 64, b * 4 + qb), in_=yc[:, :])
                    it += 1
        for _ in lg:
            pass

    def phase_gdn(self, l):
        c, ar = self.c, self.ar
        c.barrier()
        ar.reset()
        KBG = [ar.alloc(64, (S,), BF16) for _ in range(4)]
        QD = [ar.alloc(64, (S,), BF16) for _ in range(4)]
        TMt = [ar.alloc(64, (32, 64), BF16) for _ in range(4)]
        ATT = [ar.alloc(64, (32, 64), BF16) for _ in range(4)]
        KD = [ar.alloc(64, (32, 64), BF16) for _ in range(4)]
        VB = ar.alloc(64, (4, 32, 64), BF16)
        EGL = ar.alloc(64, (4, 32), F32)
        X, TT, Kf, Qf, Vf, BETB, GCB = [ar.alloc(64, (S,), F32) for _ in range(7)]
        Kb = ar.alloc(64, (S,), BF16)
        Qb = ar.alloc(64, (S,), BF16)
        CS = ar.alloc(64, (S,), BF16)
        GCc = ar.alloc(64, (32,), F32)
        BTc = ar.alloc(64, (32,), F32)
        SCOL = ar.alloc(64, (4,), F32)
        g3 = lambda: ar.alloc(64, (8, 64), F32)
        D1, DX, EUi, EUs, EL, U_, L_, Pa, Qa, R1, R2, Lof = [g3() for _ in range(12)]
        TG, Pb, Qb2, Rd, Rd2, Ysb = DX, U_, L_, R1, R2, D1
        St = ar.alloc(64, (4, 64), F32)
        Sb = ar.alloc(64, (4, 64), BF16)
        RH = ar.alloc(64, (4, 64), BF16)
        VNb = ar.alloc(64, (4, 64), BF16)
        GT = Sub(X, X.t.rearrange("p (h t) -> p h t", h=4))
        OS, SQ_, RSD, SG = [Sub(TT, TT.t[:, i * 512:(i + 1) * 512]) for i in range(4)]
        YO = [ar.alloc(64, (512,), BF16) for _ in range(2)]
        ones64 = self.CST[0:64, K_B64:K_B64 + 64]
        id64 = self.CST[0:64, K_ID:K_ID + 64]
        cbc = lambda k0: self.CST.v(self.CST.t[0:64, k0:k0 + 64].unsqueeze(1).to_broadcast([64, 8, 64]))
        v3 = lambda t_, sl: t_.v(t_.t[:, sl].rearrange("p (n i) -> p n i", i=64))
        p3 = lambda p: p.v(p.t[0:64, :].rearrange("p (s i) -> p s i", s=8))
        c.pool.memset(ap=CS[:, :], constant=1.0)
        c.pool.memset(ap=CS.v(CS.t.rearrange("p (n i) -> p n i", i=64)[:, :, 0:1]), constant=0.0)

        def mm8(pt, lhs_fn, rhs_fn):
            for s_ in range(8):
                c.pe.matmul(out=pt[0:64, s_ * 64:s_ * 64 + 64], lhsT=lhs_fn(s_), rhs=rhs_fn(s_), start=True, stop=True)

        for b in range(BL):
            for h in range(4):
                for j in range(4):
                    c.sp.dma_start(out=BETB[:, j * 512:(j + 1) * 512], in_=self.ZT.vb(2912 + h, 64, b * 4 + j))
                    c.sp.dma_start(out=GCB[:, j * 512:(j + 1) * 512], in_=self.ZT.vb(2916 + h, 64, b * 4 + j))
                c.act.activation(out=BETB[:, :], in_=BETB[:, :], func=AF.Sigmoid)
                c.dve.tensor_scalar(out=X[:, :], in0=GCB[:, :], scalar1=self.col(l, C_DTB + h, 64), scalar2=None, op0=ALU.add)
                c.dve.tensor_scalar(out=TT[:, :], in0=X[:, :], scalar1=-1.0, scalar2=None, op0=ALU.mult)
                c.dve.tensor_tensor(out=TT[:, :], in0=TT[:, :], in1=X[:, :], op=ALU.max)
                c.act.activation(out=TT[:, :], in_=TT[:, :], func=AF.Exp, scale=-1.0)
                c.act.activation(out=TT[:, :], in_=TT[:, :], func=AF.Ln, bias=1.0)
                c.dve.tensor_scalar(out=X[:, :], in0=X[:, :], scalar1=0.0, scalar2=None, op0=ALU.max)
                c.dve.tensor_tensor(out=X[:, :], in0=X[:, :], in1=TT[:, :], op=ALU.add)
                c.act.activation(out=SCOL[:, 0:1], in_=self.col(l, C_ALOG + h, 64), func=AF.Exp)
                c.dve.tensor_scalar(out=SCOL[:, 1:2], in0=SCOL[:, 0:1], scalar1=-1.0, scalar2=None, op0=ALU.mult)
                c.dve.tensor_scalar(out=X[:, :], in0=X[:, :], scalar1=SCOL[:, 1:2], scalar2=None, op0=ALU.mult)
                c.dve.tensor_tensor_scan(out=GCB[:, :], data0=CS[:, :], data1=X[:, :], initial=0.0, op0=ALU.mult, op1=ALU.add)
                for part, dst in ((0, Qf), (1, Kf), (2, Vf)):
                    self.load_seq_rows(X, self.ZT, 1888 + part * 256 + h * 64, 64, b)
                    self.conv4(TT, X, l, C_GCW + (part * 4 + h) * 4, 64)
                    c.act.activation(out=dst[:, :], in_=TT[:, :], func=AF.Silu)
                    if part < 2:
                        c.act.activation(out=TT[:, :], in_=dst[:, :], func=AF.Square)
                        for j in range(4):
                            sl = slice(j * 512, (j + 1) * 512)
                            p = self.ps()
                            c.pe.matmul(out=p[0:64, :], lhsT=ones64, rhs=TT[:, sl], start=True, stop=True)
                            c.act.activation(out=X[:, sl], in_=p[0:64, :], func=AF.Ln, bias=EPS)
                        c.act.activation(out=X[:, :], in_=X[:, :], func=AF.Exp, scale=-0.5)
                        if part == 0:
                            c.dve.scalar_tensor_tensor(out=dst[:, :], in0=dst[:, :], scalar=0.125, in1=X[:, :], op0=ALU.mult, op1=ALU.mult)
                        else:
                            c.dve.tensor_tensor(out=dst[:, :], in0=dst[:, :], in1=X[:, :], op=ALU.mult)
                c.pool.tensor_copy(out=Kb[:, :], in_=Kf[:, :])
                c.pool.tensor_copy(out=Qb[:, :], in_=Qf[:, :])
                c.act.activation(out=TT[:, :], in_=GCB[:, :], func=AF.Exp)
                c.pool.tensor_tensor(out=X[:, :], in0=BETB[:, :], in1=TT[:, :], op=ALU.mult)
                c.dve.tensor_tensor(out=KBG[h][:, :], in0=Kf[:, :], in1=X[:, :], op=ALU.mult)
                c.dve.tensor_tensor(out=QD[h][:, :], in0=Qf[:, :], in1=TT[:, :], op=ALU.mult)
                c.pool.tensor_copy(out=EGL[:, h, :], in_=TT.v(TT.t.rearrange("p (n i) -> p n i", i=64)[:, :, 63]))
                c.pool.tensor_copy(out=X[0:32, :], in_=GCB[0:32, :])
                c.pool.tensor_copy(out=X[32:64, :], in_=BETB[32:64, :])
                for g in range(4):
                    p = self.ps()
                    for s_ in range(8):
                        n = g * 8 + s_
                        c.pe.transpose(out=p[0:64, s_ * 64:s_ * 64 + 64], in_=X[0:64, n * 64:n * 64 + 64], identity=id64)
                    pv = p.t[0:64, :].rearrange("p (s i) -> p s i", s=8)
                    c.dve.tensor_copy(out=GCc[:, g * 8:g * 8 + 8], in_=p.v(pv[:, :, 0]))
                    c.dve.tensor_copy(out=BTc[:, g * 8:g * 8 + 8], in_=p.v(pv[:, :, 32]))
                for g in range(4):
                    tk = slice(g * 512, (g + 1) * 512)
                    ns = slice(g * 8, g * 8 + 8)
                    cs = lambda s_: slice((g * 8 + s_) * 64, (g * 8 + s_) * 64 + 64)
                    gcc_b = GCc.v(GCc.t[:, ns].unsqueeze(2).to_broadcast([64, 8, 64]))
                    btc_b = BTc.v(BTc.t[:, ns].unsqueeze(2).to_broadcast([64, 8, 64]))
                    pG = self.ps()
                    mm8(pG, lambda s_: Kb[:, cs(s_)], lambda s_: Kb[:, cs(s_)])
                    pQ = self.ps()
                    mm8(pQ, lambda s_: Kb[:, cs(s_)], lambda s_: Qb[:, cs(s_)])
                    c.dve.tensor_tensor(out=D1[:, :, :], in0=v3(GCB, tk), in1=gcc_b, op=ALU.subtract)
                    c.pool.tensor_tensor(out=DX[:, :, :], in0=D1[:, :, :], in1=cbc(K_NMGE), op=ALU.add)
                    c.act.activation(out=EUi[:, :, :], in_=DX[:, :, :], func=AF.Exp)
                    c.pool.tensor_tensor(out=DX[:, :, :], in0=D1[:, :, :], in1=cbc(K_NMGT), op=ALU.add)
                    c.act.activation(out=EUs[:, :, :], in_=DX[:, :, :], func=AF.Exp)
                    c.dve.scalar_tensor_tensor(out=DX[:, :, :], in0=D1[:, :, :], scalar=-1.0, in1=cbc(K_NMLT), op0=ALU.mult, op1=ALU.add)
                    c.act.activation(out=EL[:, :, :], in_=DX[:, :, :], func=AF.Exp)
                    c.dve.tensor_tensor(out=ATT[h][:, ns, :], in0=p3(pQ), in1=EUi[:, :, :], op=ALU.mult)
                    c.pool.tensor_tensor(out=TG[:, :, :], in0=v3(BETB, tk), in1=EUs[:, :, :], op=ALU.mult)
                    c.dve.tensor_tensor(out=U_[:, :, :], in0=p3(pG), in1=TG[:, :, :], op=ALU.mult)
                    c.pool.tensor_tensor(out=TG[:, :, :], in0=EL[:, :, :], in1=btc_b, op=ALU.mult)
                    c.dve.tensor_tensor(out=L_[:, :, :], in0=p3(pG), in1=TG[:, :, :], op=ALU.mult)
                    pK = self.ps()
                    for s_ in range(8):
                        c.pe.transpose(out=pK[0:64, s_ * 64:s_ * 64 + 64], in_=Kf[0:64, cs(s_)], identity=id64)
                    c.dve.tensor_tensor(out=KD[h][:, ns, :], in0=p3(pK),
                                        in1=EUi.v(EUi.t[:, :, 63:64].to_broadcast([64, 8, 64])), op=ALU.mult)
                    pV = self.ps()
                    for s_ in range(8):
                        c.pe.transpose(out=pV[0:64, s_ * 64:s_ * 64 + 64], in_=Vf[0:64, cs(s_)], identity=id64)
                    c.dve.tensor_tensor(out=VB[:, h, ns, :], in0=p3(pV), in1=btc_b, op=ALU.mult)
                    c.dve.scalar_tensor_tensor(out=R1[:, :, :], in0=U_[:, :, :], scalar=-1.0, in1=cbc(K_ID), op0=ALU.mult, op1=ALU.add)
                    c.dve.scalar_tensor_tensor(out=R2[:, :, :], in0=L_[:, :, :], scalar=-1.0, in1=cbc(K_ID), op0=ALU.mult, op1=ALU.add)
                    c.pool.tensor_tensor(out=Lof[:, :, :], in0=L_[:, :, :], in1=cbc(K_OFFL), op=ALU.mult)
                    P_, Q_ = U_, L_
                    for lev in range(4):
                        Pn, Qn = (Pa, Qa) if lev % 2 == 0 else (Pb, Qb2)
                        pP = self.ps()
                        mm8(pP, lambda s_: Q_[:, s_, :], lambda s_: P_[:, s_, :])
                        pQ2 = self.ps()
                        mm8(pQ2, lambda s_: P_[:, s_, :], lambda s_: Q_[:, s_, :])
                        c.act.activation(out=Pn[:, :, :], in_=p3(pP), func=AF.Copy)
                        c.dve.tensor_copy(out=Qn[:, :, :], in_=p3(pQ2))
                        pR = self.ps()
                        mm8(pR, lambda s_: Qn[:, s_, :], lambda s_: R1[:, s_, :])
                        pR2 = self.ps()
                        mm8(pR2, lambda s_: Pn[:, s_, :], lambda s_: R2[:, s_, :])
                        c.dve.tensor_tensor(out=R1[:, :, :], in0=p3(pR), in1=R1[:, :, :], op=ALU.add)
                        c.dve.tensor_tensor(out=R2[:, :, :], in0=p3(pR2), in1=R2[:, :, :], op=ALU.add)
                        P_, Q_ = Pn, Qn
                    c.pool.tensor_tensor(out=Rd[:, :, :], in0=R1[:, :, :], in1=cbc(K_DIAG), op=ALU.mult)
                    c.pool.tensor_tensor(out=Rd2[:, :, :], in0=R2[:, :, :], in1=cbc(K_DIAG), op=ALU.mult)
                    pY = self.ps()
                    mm8(pY, lambda s_: Lof[:, s_, :], lambda s_: Rd[:, s_, :])
                    c.act.activation(out=Ysb[:, :, :], in_=p3(pY), func=AF.Copy)
                    pX = self.ps()
                    mm8(pX, lambda s_: Rd2[:, s_, :], lambda s_: Ysb[:, s_, :])
                    c.dve.tensor_tensor(out=TMt[h][:, ns, :], in0=Rd[:, :, :], in1=p3(pX), op=ALU.subtract)
            c.pool.memset(ap=St[:, :, :], constant=0.0)
            c.pool.memset(ap=Sb[:, :, :], constant=0.0)
            pO = self.psl[0:4]
            pP1, pVN, pSN, p7 = self.psl[4], self.psl[5], self.psl[6], self.psl[7]
            p4 = lambda p: p.v(p.t[0:64, 0:256].rearrange("p (h e) -> p h e", h=4))
            for n in range(32):
                s_, g = n % 8, n // 8
                cs = slice(n * 64, n * 64 + 64)
                if s_ == 0:
                    for h in range(4):
                        c.sp.dma_start(out=GT[:, h, :], in_=self.ZT.v(2656 + h * 64, 64, b * 4 + g))
                for h in range(4):
                    c.pe.matmul(out=pP1[0:64, h * 64:h * 64 + 64], lhsT=KBG[h][:, cs], rhs=Sb[:, h, :], start=True, stop=True)
                c.dve.tensor_tensor(out=RH[:, :, :], in0=VB[:, :, n, :], in1=p4(pP1), op=ALU.subtract)
                for h in range(4):
                    c.pe.matmul(out=pVN[0:64, h * 64:h * 64 + 64], lhsT=TMt[h][:, n, :], rhs=RH[:, h, :], start=True, stop=True)
                c.act.activation(out=VNb[:, :, :], in_=p4(pVN), func=AF.Copy)
                for h in range(4):
                    c.pe.matmul(out=pO[h][0:64, s_ * 64:s_ * 64 + 64], lhsT=Sb[:, h, :], rhs=QD[h][:, cs], start=True, stop=False)
                    c.pe.matmul(out=pO[h][0:64, s_ * 64:s_ * 64 + 64], lhsT=VNb[:, h, :], rhs=ATT[h][:, n, :], start=False, stop=True)
                for h in range(4):
                    c.pe.matmul(out=pSN[0:64, h * 64:h * 64 + 64], lhsT=KD[h][:, n, :], rhs=VNb[:, h, :], start=True, stop=True)
                for h in range(4):
                    c.dve.scalar_tensor_tensor(out=St[:, h, :], in0=St[:, h, :], scalar=EGL[:, h, n:n + 1],
                                               in1=pSN[0:64, h * 64:h * 64 + 64], op0=ALU.mult, op1=ALU.add)
                c.pool.tensor_copy(out=Sb[:, :, :], in_=St[:, :, :])
                if s_ == 7:
                    for h in range(4):
                        c.act.activation(out=OS[:, :], in_=pO[h][0:64, :], func=AF.Copy)
                        c.act.activation(out=SQ_[:, :], in_=OS[:, :], func=AF.Square)
                        c.pe.matmul(out=p7[0:64, :], lhsT=ones64, rhs=SQ_[:, :], start=True, stop=True)
                        c.act.activation(out=RSD[:, :], in_=p7[0:64, :], func=AF.Ln, scale=1.0 / 64, bias=EPS)
                        c.act.activation(out=RSD[:, :], in_=RSD[:, :], func=AF.Exp, scale=-0.5)
                        c.act.activation(out=SG[:, :], in_=GT[:, h, :], func=AF.Silu)
                        c.dve.tensor_tensor(out=OS[:, :], in0=OS[:, :], in1=RSD[:, :], op=ALU.mult)
                        yo = YO[h % 2]
                        c.dve.scalar_tensor_tensor(out=yo[:, :], in0=OS[:, :], scalar=self.col(l, C_GNORM, 64), in1=SG[:, :],
                                                   op0=ALU.mult, op1=ALU.mult)
                        c.pool.dma_start(out=self.YT.v(768 + h * 64, 64, b * 4 + g), in_=yo[:, :])

    def phase_wout(self, l, XSRC):
        c, ar = self.c, self.ar
        c.barrier()
        ar.reset()
        W = ar.alloc(128, (8, DM), BF16)
        stg = [ar.alloc(128, (DM,), F32) for _ in range(2)]
        wo = self.d_in["w_out"]
        self.load_w_bf16(W, wo, lambda k: wo.t[l, k * 128:(k + 1) * 128, :], 8, DM, stg)
        Y = [ar.alloc(128, (8, 512), BF16) for _ in range(2)]
        X32 = [ar.alloc(128, (8, 512), F32) for _ in range(2)]
        XO = [ar.alloc(128, (512,), F32) for _ in range(4)]
        for tb in range(T // 512):
            y, x = Y[tb % 2], X32[tb % 2]
            for k in range(8):
                c.sp.dma_start(out=y[:, k, :], in_=self.YT.v(k * 128, 128, tb))
                c.sp.dma_start(out=x[:, k, :], in_=XSRC.v(k * 128, 128, tb))
            for cc in range(8):
                p = self.ps()
                for k in range(8):
                    c.pe.matmul(out=p[:, :], lhsT=W[:, k, cc * 128:cc * 128 + 128], rhs=y[:, k, :], start=(k == 0), stop=(k == 7))
                xo = XO[cc % 4]
                c.dve.tensor_tensor(out=xo[:, :], in0=p[:, :], in1=x[:, cc, :], op=ALU.add)
                c.pool.dma_start(out=self.XT.v(cc * 128, 128, tb), in_=xo[:, :])

    def phase_xattn(self, l, XSRC):
        c, ar = self.c, self.ar
        c.barrier()
        ar.reset()
        KT = ar.alloc(128, (8, TM_), BF16)
        VT = ar.alloc(128, (4, DM), BF16)
        WM = ar.alloc(128, (8, DM), BF16)
        WQ = ar.alloc(128, (8, DM), BF16)
        WO = ar.alloc(128, (8, DM), BF16)
        stgB = [ar.alloc(128, (DM,), F32) for _ in range(2)]
        mark = ar.off
        WKV = ar.alloc(128, (8, 2 * DM), BF16)
        stg = [ar.alloc(128, (2 * DM,), F32) for _ in range(2)]
        wkv = self.d_in["xa_wkv"]
        M32 = ar.alloc(128, (8, TM_), F32)
        SQ = ar.alloc(128, (8, TM_), BF16)
        MG = ar.alloc(128, (8, TM_), BF16)
        RS = ar.alloc(128, (TM_,), F32)
        RC = ar.alloc(128, (4,), F32)
        mt_ = self.d_in["memT"]
        for k in range(8):
            c.sp.dma_start(out=M32[:, k, :], in_=mt_[k * 128:(k + 1) * 128, :])
        self.load_w_bf16(WKV, wkv, lambda k: wkv.t[l, k * 128:(k + 1) * 128, :], 8, 2 * DM, stg)
        wm, wq, wo = self.d_in["w_out"], self.d_in["xa_wq"], self.d_in["xa_wo"]
        self.load_w_bf16(WM, wm, lambda k: wm.t[l, k * 128:(k + 1) * 128, :], 8, DM, stgB)
        self.load_w_bf16(WQ, wq, lambda k: wq.t[l, k * 128:(k + 1) * 128, :], 8, DM, stgB)
        self.load_w_bf16(WO, wo, lambda k: wo.t[l, k * 128:(k + 1) * 128, :], 8, DM, stgB)
        self.rms_stats(M32, 8, TM_, SQ, RS)
        for k in range(8):
            c.dve.tensor_scalar(out=MG[:, k, :], in0=M32[:, k, :], scalar1=self.col(l, C_NMEM + k), scalar2=None, op0=ALU.mult)
        for cc in range(8):
            p = self.ps()
            for k in range(8):
                c.pe.matmul(out=p[:, :], lhsT=WKV[:, k, cc * 128:cc * 128 + 128], rhs=MG[:, k, :], start=(k == 0), stop=(k == 7))
            c.dve.tensor_tensor(out=KT[:, cc, :], in0=p[:, :], in1=RS[:, :], op=ALU.mult)
        pc = self.ps()
        for mt in range(4):
            for k in range(8):
                c.pe.matmul(out=pc[:, mt:mt + 1], lhsT=SQ[:, k, mt * 128:mt * 128 + 128], rhs=self.ONESB[:, 0:1], start=(k == 0), stop=(k == 7))
        c.act.activation(out=RC[:, :], in_=pc[:, 0:4], func=AF.Ln, scale=1.0 / DM, bias=EPS)
        c.act.activation(out=RC[:, :], in_=RC[:, :], func=AF.Exp, scale=-0.5)
        for mt in range(4):
            for hf in range(2):
                p = self.ps()
                for k in range(8):
                    c.pe.matmul(out=p[:, :], lhsT=MG[:, k, mt * 128:mt * 128 + 128], rhs=WKV[:, k, DM + hf * 512:DM + hf * 512 + 512],
                                start=(k == 0), stop=(k == 7))
                c.dve.tensor_scalar(out=VT[:, mt, hf * 512:hf * 512 + 512], in0=p[:, :], scalar1=RC[:, mt:mt + 1], scalar2=None, op0=ALU.mult)
        c.barrier()
        ar.reset(mark)
        YM = ar.alloc(128, (8, 512), BF16)
        X32 = [ar.alloc(128, (8, 512), F32) for _ in range(2)]
        SQ = ar.alloc(128, (8, 512), BF16)
        XG = ar.alloc(128, (8, 512), BF16)
        RS = ar.alloc(128, (512,), F32)
        QT = ar.alloc(128, (8, 512), BF16)
        PT = [ar.alloc(128, (512,), BF16) for _ in range(4)]
        RSM = ar.alloc(128, (512,), F32)
        OT = ar.alloc(128, (8, 512), BF16)
        XO = [ar.alloc(128, (512,), F32) for _ in range(4)]
        for tb in range(T // 512):
            b = tb // 4
            x = X32[tb % 2]
            for k in range(8):
                c.sp.dma_start(out=YM[:, k, :], in_=self.YT.v(k * 128, 128, tb))
                c.sp.dma_start(out=x[:, k, :], in_=XSRC.v(k * 128, 128, tb))
            for cc in range(8):
                p = self.ps()
                for k in range(8):
                    c.pe.matmul(out=p[:, :], lhsT=WM[:, k, cc * 128:cc * 128 + 128], rhs=YM[:, k, :], start=(k == 0), stop=(k == 7))
                c.dve.tensor_tensor(out=x[:, cc, :], in0=p[:, :], in1=x[:, cc, :], op=ALU.add)
            self.rms_stats(x, 8, 512, SQ, RS)
            for k in range(8):
                c.dve.tensor_scalar(out=XG[:, k, :], in0=x[:, k, :], scalar1=self.col(l, C_NXA + k), scalar2=None, op0=ALU.mult)
            for cc in range(8):
                p = self.ps()
                for k in range(8):
                    c.pe.matmul(out=p[:, :], lhsT=WQ[:, k, cc * 128:cc * 128 + 128], rhs=XG[:, k, :], start=(k == 0), stop=(k == 7))
                c.dve.scalar_tensor_tensor(out=QT[:, cc, :], in0=p[:, :], scalar=1.0 / 16.0, in1=RS[:, :], op0=ALU.mult, op1=ALU.mult)
            for h in range(4):
                pts = []
                for m in range(2):
                    pS = self.ps()
                    for c2 in range(2):
                        c.pe.matmul(out=pS[:, :], lhsT=KT[:, 2 * h + c2, b * NMEM + m * 128:b * NMEM + m * 128 + 128],
                                    rhs=QT[:, 2 * h + c2, :], start=(c2 == 0), stop=(c2 == 1))
                    pt = PT[(2 * h + m) % 4]
                    c.act.activation(out=pt[:, :], in_=pS[:, :], func=AF.Exp)
                    pts.append(pt)
                pM = self.ps()
                for m in range(2):
                    c.pe.matmul(out=pM[:, :], lhsT=self.ONESB[:, :], rhs=pts[m][:, :], start=(m == 0), stop=(m == 1))
                c.act.activation(out=RSM[:, :], in_=pM[:, :], func=AF.Ln)
                c.act.activation(out=RSM[:, :], in_=RSM[:, :], func=AF.Exp, scale=-1.0)
                for c2 in range(2):
                    pO = self.ps()
                    for m in range(2):
                        c.pe.matmul(out=pO[:, :], lhsT=VT[:, b * 2 + m, (2 * h + c2) * 128:(2 * h + c2) * 128 + 128], rhs=pts[m][:, :],
                                    start=(m == 0), stop=(m == 1))
                    c.dve.tensor_tensor(out=OT[:, 2 * h + c2, :], in0=pO[:, :], in1=RSM[:, :], op=ALU.mult)
            for cc in range(8):
                p = self.ps()
                for k in range(8):
                    c.pe.matmul(out=p[:, :], lhsT=WO[:, k, cc * 128:cc * 128 + 128], rhs=OT[:, k, :], start=(k == 0), stop=(k == 7))
                xo = XO[cc % 4]
                c.dve.tensor_tensor(out=xo[:, :], in0=p[:, :], in1=x[:, cc, :], op=ALU.add)
                c.pool.dma_start(out=self.XT.v(cc * 128, 128, tb), in_=xo[:, :])

    def phase_mlp(self, l, final):
        c, ar = self.c, self.ar
        c.barrier()
        ar.reset()
        DF = 4 * DM
        W1 = ar.alloc(128, (8, DF), BF16)
        W2 = ar.alloc(128, (32, DM), BF16)
        stg = [ar.alloc(128, (1024,), F32) for _ in range(3)]
        w1, w2 = self.d_in["mlp_w1"], self.d_in["mlp_w2"]
        i = 0
        for k in range(8):
            for hf in range(4):
                s = stg[i % 3]
                c.sp.dma_start(out=s[:, :], in_=w1[l, k * 128:(k + 1) * 128, hf * 1024:(hf + 1) * 1024])
                if i % 2:
                    c.act.activation(out=W1[:, k, hf * 1024:(hf + 1) * 1024], in_=s[:, :], func=AF.Copy)
                else:
                    c.dve.tensor_copy(out=W1[:, k, hf * 1024:(hf + 1) * 1024], in_=s[:, :])
                i += 1
        NB = 256
        X32 = [ar.alloc(128, (8, NB), F32) for _ in range(2)]
        for k in range(8):
            c.sp.dma_start(out=X32[0][:, k, :], in_=self.XT.v(k * 128, 128, 0, c0=0, ncol=NB))
        SQ = ar.alloc(128, (8, NB), BF16)
        XG = ar.alloc(128, (8, NB), BF16)
        RS = ar.alloc(128, (NB,), F32)
        RS2 = ar.alloc(128, (NB,), F32)
        HR = [ar.alloc(128, (NB,), BF16) for _ in range(2)]
        HID = ar.alloc(128, (32, NB), BF16)
        TO = [ar.alloc(128, (NB,), F32) for _ in range(2)]
        XN = ar.alloc(128, (8, NB), F32)
        OUTS = [ar.alloc(128, (NB,), F32) for _ in range(2)]
        for tb in range(T // NB):
            x = X32[tb % 2]
            dv = lambda grid, k: grid.v(k * 128, 128, tb // 2, c0=tb * NB, ncol=NB)
            if tb > 0:
                for k in range(8):
                    c.sp.dma_start(out=x[:, k, :], in_=dv(self.XT, k))
            self.rms_stats(x, 8, NB, SQ, RS)
            c.pool.tensor_tensor(out=RS2[:, :], in0=RS[:, :], in1=RS[:, :], op=ALU.mult)
            for k in range(8):
                c.dve.tensor_scalar(out=XG[:, k, :], in0=x[:, k, :], scalar1=self.col(l, C_NMLP + k), scalar2=None, op0=ALU.mult)
            if tb == 0:
                for f in range(32):
                    s = stg[i % 3]
                    c.sp.dma_start(out=s[:, 0:DM], in_=w2[l, f * 128:(f + 1) * 128, :])
                    c.dve.tensor_copy(out=W2[:, f, :], in_=s[:, 0:DM])
                    i += 1
            for f in range(32):
                p = self.ps()
                for k in range(8):
                    c.pe.matmul(out=p[:, 0:NB], lhsT=W1[:, k, f * 128:f * 128 + 128], rhs=XG[:, k, :], start=(k == 0), stop=(k == 7))
                hr = HR[f % 2]
                c.act.activation(out=hr[:, :], in_=p[:, 0:NB], func=AF.Relu)
                c.pool.tensor_tensor(out=HID[:, f, :], in0=hr[:, :], in1=hr[:, :], op=ALU.mult)
            for cc in range(8):
                p = self.ps()
                for f in range(32):
                    c.pe.matmul(out=p[:, 0:NB], lhsT=W2[:, f, cc * 128:cc * 128 + 128], rhs=HID[:, f, :], start=(f == 0), stop=(f == 31))
                to = TO[cc % 2]
                c.dve.tensor_tensor(out=to[:, :], in0=p[:, 0:NB], in1=RS2[:, :], op=ALU.mult)
                c.dve.tensor_tensor(out=XN[:, cc, :], in0=to[:, :], in1=x[:, cc, :], op=ALU.add)
                if not final:
                    c.pool.dma_start(out=dv(self.XT, cc), in_=XN[:, cc, :])
            if final:
                self.rms_stats(XN, 8, NB, SQ, RS)
                for k in range(8):
                    o = OUTS[k % 2]
                    c.dve.scalar_tensor_tensor(out=o[:, :], in0=XN[:, k, :], scalar=self.col(l, C_NFIN + k), in1=RS[:, :],
                                               op0=ALU.mult, op1=ALU.mult)
                    c.pool.dma_start(out=dv(self.outT, k), in_=o[:, :])

    def tok_major2(self, dst, src, evac=None):
        c = self.c
        for g in range(4):
            p = self.ps()
            for hl in range(2):
                pb = 64 * hl
                for s_ in range(8):
                    n = g * 8 + s_
                    c.pe.matmul(out=p[pb:pb + 64, s_ * 64:s_ * 64 + 64], lhsT=src[pb:pb + 64, n * 64:n * 64 + 64],
                                rhs=self.CST[pb:pb + 64, K_ID + pb:K_ID + pb + 64], start=True, stop=True)
            pv = p.v(p.t[:, :].rearrange("p (s e) -> p s e", s=8))
            if evac is None:
                c.act.activation(out=dst[:, g * 8:g * 8 + 8, :], in_=pv, func=AF.Copy)
            else:
                c.dve.tensor_tensor(out=dst[:, g * 8:g * 8 + 8, :], in0=pv, in1=evac(g), op=ALU.mult)

    def phase_ret2(self, l):
        c, ar = self.c, self.ar
        c.barrier()
        ar.reset()
        f32 = lambda: ar.alloc(128, (S,), F32)
        Q, Kt, V, G, COS, SIN, T1, T2, KZ, KV, G64, SS = [f32() for _ in range(12)]
        QH = ar.alloc(128, (S,), BF16)
        KH = ar.alloc(128, (S,), BF16)
        QX = ar.alloc(128, (S,), BF16)
        KZT = ar.alloc(128, (32, 64), BF16)
        VT = ar.alloc(128, (32, 64), BF16)
        SB = ar.alloc(128, (33, 64), BF16)
        SC = [ar.alloc(128, (512,), BF16) for _ in range(2)]
        OS2 = [ar.alloc(128, (512,), F32) for _ in range(2)]
        CEN2 = [ar.alloc(128, (512,), F32) for _ in range(2)]
        SQ2 = [ar.alloc(128, (512,), F32) for _ in range(2)]
        RSD = ar.alloc(128, (512,), F32)
        SG = ar.alloc(128, (512,), F32)
        YO = [ar.alloc(128, (512,), BF16) for _ in range(2)]
        b64 = self.CST[:, K_B64:K_B64 + 128]
        v3 = lambda t_: t_.v(t_.t.rearrange("p (n i) -> p n i", i=64))
        bc = lambda k0, n: self.CST.v(self.CST.t[:, k0:k0 + 64].unsqueeze(1).to_broadcast([128, n, 64]))
        for b in range(BL):
            self.load_seq_rows(COS, self.TAB["COSA"], 0, 128, b)
            self.load_seq_rows(SIN, self.TAB["SINA"], 0, 128, b)
            for hp in range(2):
                self.load_seq_rows(Q, self.ZT, 0 + hp * 128, 128, b)
                self.load_seq_rows(Kt, self.ZT, 256 + hp * 128, 128, b)
                self.load_seq_rows(V, self.ZT, 512 + hp * 128, 128, b)
                self.load_seq_rows(G, self.ZT, 768 + hp * 128, 128, b)
                self.rope64(T1, Q, COS, SIN, T2, K_ROTA, 128)
                c.dve.tensor_scalar(out=QH[:, :], in0=T1[:, :], scalar1=0.125, scalar2=None, op0=ALU.mult)
                c.pool.tensor_tensor(out=v3(QX), in0=v3(T1), in1=bc(K_XI2 + 64 * hp, 32), op=ALU.mult)
                self.rope64(T1, Kt, COS, SIN, T2, K_ROTA, 128)
                c.dve.tensor_copy(out=KH[:, :], in_=T1[:, :])
                c.pool.tensor_tensor(out=v3(KZ), in0=v3(T1), in1=bc(K_ZETA2 + 64 * hp, 32), op=ALU.mult)
                self.tok_major2(KZT, KZ)
                self.tok_major2(VT, V)
                kv3 = KV.t.rearrange("p (e n) -> p e n", e=64)
                for g in range(4):
                    p = self.ps()
                    for hl in range(2):
                        pb = 64 * hl
                        for s_ in range(8):
                            n = g * 8 + s_
                            c.pe.matmul(out=p[pb:pb + 64, s_ * 64:s_ * 64 + 64], lhsT=KZT[pb:pb + 64, n, :], rhs=VT[pb:pb + 64, n, :],
                                        start=True, stop=True)
                    c.dve.tensor_copy(out=KV.v(kv3[:, :, g * 8:g * 8 + 8]),
                                      in_=p.v(p.t[:, :].rearrange("p (s e) -> p e s", s=8)))
                for hl in range(2):
                    g64 = math.exp(64.0 * math.log1p(-2.0 ** (-5.0 - (2 * hp + hl))))
                    c.pool.memset(ap=G64[64 * hl:64 * hl + 64, :], constant=g64)
                c.pool.memset(ap=G64.v(G64.t.rearrange("p (e n) -> p e n", e=64)[:, :, 0:1]), constant=0.0)
                c.dve.tensor_tensor_scan(out=SS[:, :], data0=G64[:, :], data1=KV[:, :], initial=0.0, op0=ALU.mult, op1=ALU.add)
                c.pool.memset(ap=SB[:, 0, :], constant=0.0)
                c.pool.tensor_copy(out=SB[:, 1:33, :], in_=SS.v(SS.t.rearrange("p (e n) -> p n e", e=64)))
                st_ = {}

                def stA(g):
                    pS = self.ps()
                    for hl in range(2):
                        pb = 64 * hl
                        for s_ in range(8):
                            cs = slice((g * 8 + s_) * 64, (g * 8 + s_) * 64 + 64)
                            c.pe.matmul(out=pS[pb:pb + 64, s_ * 64:s_ * 64 + 64], lhsT=KH[pb:pb + 64, cs], rhs=QH[pb:pb + 64, cs],
                                        start=True, stop=True)
                    sc = SC[g % 2]
                    c.dve.tensor_tensor(out=sc.v(sc.t.rearrange("p (s i) -> p s i", s=8)),
                                        in0=pS.v(pS.t[:, :].rearrange("p (s i) -> p s i", s=8)), in1=bc(K_DMASK2 + 64 * hp, 8), op=ALU.mult)

                def stB(g):
                    sc = SC[g % 2]
                    pO = self.ps()
                    for hl in range(2):
                        pb = 64 * hl
                        for s_ in range(8):
                            n = g * 8 + s_
                            cs = slice(n * 64, n * 64 + 64)
                            c.pe.matmul(out=pO[pb:pb + 64, s_ * 64:s_ * 64 + 64], lhsT=VT[pb:pb + 64, n, :],
                                        rhs=sc[pb:pb + 64, s_ * 64:s_ * 64 + 64], start=True, stop=False)
                            c.pe.matmul(out=pO[pb:pb + 64, s_ * 64:s_ * 64 + 64], lhsT=SB[pb:pb + 64, n, :], rhs=QX[pb:pb + 64, cs],
                                        start=False, stop=True)
                    c.act.activation(out=OS2[g % 2][:, :], in_=pO[:, :], func=AF.Copy)

                def stC(g):
                    p2 = self.ps()
                    c.pe.matmul(out=p2[:, :], lhsT=b64, rhs=OS2[g % 2][:, :], start=True, stop=True)
                    c.dve.scalar_tensor_tensor(out=CEN2[g % 2][:, :], in0=p2[:, :], scalar=-1.0 / 64, in1=OS2[g % 2][:, :], op0=ALU.mult, op1=ALU.add)
                    c.act.activation(out=SQ2[g % 2][:, :], in_=CEN2[g % 2][:, :], func=AF.Square)

                def stD(g):
                    sl = slice(g * 512, (g + 1) * 512)
                    p3 = self.ps()
                    c.pe.matmul(out=p3[:, :], lhsT=b64, rhs=SQ2[g % 2][:, :], start=True, stop=True)
                    c.act.activation(out=RSD[:, :], in_=p3[:, :], func=AF.Ln, scale=1.0 / 64, bias=EPS)
                    c.act.activation(out=RSD[:, :], in_=RSD[:, :], func=AF.Exp, scale=-0.5)
                    c.act.activation(out=SG[:, :], in_=G[:, sl], func=AF.Silu)
                    c.dve.tensor_tensor(out=CEN2[g % 2][:, :], in0=CEN2[g % 2][:, :], in1=RSD[:, :], op=ALU.mult)
                    yo = YO[g % 2]
                    c.dve.scalar_tensor_tensor(out=yo[:, :], in0=CEN2[g % 2][:, :], scalar=self.col(l, C_RGN2 + hp), in1=SG[:, :],
                                               op0=ALU.mult, op1=ALU.mult)
                    c.pool.dma_start(out=self.YT.v(hp * 128, 128, b * 4 + g), in_=yo[:, :])

                for t in range(6):
                    if t < 4:
                        stA(t)
                        stB(t)
                    if 0 <= t - 1 < 4:
                        stC(t - 1)
                    if 0 <= t - 2 < 4:
                        stD(t - 2)

    def phase_gdn2(self, l):
        c, ar = self.c, self.ar
        c.barrier()
        ar.reset()
        NPR = 2 * BL
        KBG = [ar.alloc(128, (S,), BF16) for _ in range(NPR)]
        QD = [ar.alloc(128, (S,), BF16) for _ in range(NPR)]
        TMt = [ar.alloc(128, (32, 64), BF16) for _ in range(NPR)]
        ATT = [ar.alloc(128, (32, 64), BF16) for _ in range(NPR)]
        KD = [ar.alloc(128, (32, 64), BF16) for _ in range(NPR)]
        VB = ar.alloc(128, (NPR, 32, 64), BF16)
        EGL = ar.alloc(128, (NPR, 32), F32)
        X, TT, Kf, Qf, Vf, BETB, GCB = [ar.alloc(128, (S,), F32) for _ in range(7)]
        Kb = ar.alloc(128, (S,), BF16)
        Qb = ar.alloc(128, (S,), BF16)
        CS = ar.alloc(128, (S,), BF16)
        GCc = ar.alloc(128, (32,), F32)
        BTc = ar.alloc(128, (32,), F32)
        SCOL = ar.alloc(128, (4,), F32)
        g3 = lambda: ar.alloc(128, (8, 64), F32)
        D1, DX, EUi, EUs, EL, U_, L_, Pa, Qa, R1, R2, Lof = [g3() for _ in range(12)]
        TG, Pb, Qb2, Rd, Rd2, Ysb = DX, U_, L_, R1, R2, D1
        St = ar.alloc(128, (NPR, 64), F32)
        Sb = ar.alloc(128, (NPR, 64), BF16)
        RH = ar.alloc(128, (NPR, 64), BF16)
        VNb = ar.alloc(128, (NPR, 64), BF16)
        GT = Sub(X, X.t.rearrange("p (h t) -> p h t", h=NPR))
        OS, SQ_, RSD, SG = [Sub(TT, TT.t[:, i * 512:(i + 1) * 512]) for i in range(4)]
        YO = [ar.alloc(128, (512,), BF16) for _ in range(2)]
        b64 = self.CST[:, K_B64:K_B64 + 128]
        idb = lambda pb: self.CST[pb:pb + 64, K_ID + pb:K_ID + pb + 64]
        cbc = lambda k0: self.CST.v(self.CST.t[:, k0:k0 + 64].unsqueeze(1).to_broadcast([128, 8, 64]))
        v3 = lambda t_, sl: t_.v(t_.t[:, sl].rearrange("p (n i) -> p n i", i=64))
        p3 = lambda p: p.v(p.t[:, :].rearrange("p (s i) -> p s i", s=8))
        c.pool.memset(ap=CS[:, :], constant=1.0)
        c.pool.memset(ap=CS.v(CS.t.rearrange("p (n i) -> p n i", i=64)[:, :, 0:1]), constant=0.0)

        def mm8(pt, lhs_fn, rhs_fn):
            for hl in range(2):
                pb = 64 * hl
                for s_ in range(8):
                    c.pe.matmul(out=pt[pb:pb + 64, s_ * 64:s_ * 64 + 64], lhsT=lhs_fn(pb, s_), rhs=rhs_fn(pb, s_), start=True, stop=True)

        def tr8(pt, src, cs):
            for hl in range(2):
                pb = 64 * hl
                for s_ in range(8):
                    c.pe.matmul(out=pt[pb:pb + 64, s_ * 64:s_ * 64 + 64], lhsT=src[pb:pb + 64, cs(s_)], rhs=idb(pb), start=True, stop=True)

        for b in range(BL):
            for hp in range(2):
                ip = b * 2 + hp
                for hl in range(2):
                    for j in range(4):
                        c.sp.dma_start(out=BETB[64 * hl:64 * hl + 64, j * 512:(j + 1) * 512], in_=self.ZT.vb(2912 + 2 * hp + hl, 64, b * 4 + j))
                        c.sp.dma_start(out=GCB[64 * hl:64 * hl + 64, j * 512:(j + 1) * 512], in_=self.ZT.vb(2916 + 2 * hp + hl, 64, b * 4 + j))
                self.load_seq_rows(X, self.ZT, 1888 + 0 * 256 + hp * 128, 128, b)
                self.load_seq_rows(Vf, self.ZT, 1888 + 1 * 256 + hp * 128, 128, b)
                c.act.activation(out=BETB[:, :], in_=BETB[:, :], func=AF.Sigmoid)
                c.dve.tensor_scalar(out=Qf[:, :], in0=GCB[:, :], scalar1=self.col(l, C_DTB2 + hp), scalar2=None, op0=ALU.add)
                c.dve.tensor_scalar(out=Kf[:, :], in0=Qf[:, :], scalar1=-1.0, scalar2=None, op0=ALU.mult)
                c.dve.tensor_tensor(out=Kf[:, :], in0=Kf[:, :], in1=Qf[:, :], op=ALU.max)
                c.act.activation(out=Kf[:, :], in_=Kf[:, :], func=AF.Exp, scale=-1.0)
                c.act.activation(out=Kf[:, :], in_=Kf[:, :], func=AF.Ln, bias=1.0)
                c.dve.tensor_scalar(out=Qf[:, :], in0=Qf[:, :], scalar1=0.0, scalar2=None, op0=ALU.max)
                c.dve.tensor_tensor(out=Qf[:, :], in0=Qf[:, :], in1=Kf[:, :], op=ALU.add)
                c.act.activation(out=SCOL[:, 0:1], in_=self.col(l, C_ALOG2 + hp), func=AF.Exp)
                c.dve.tensor_scalar(out=SCOL[:, 1:2], in0=SCOL[:, 0:1], scalar1=-1.0, scalar2=None, op0=ALU.mult)
                c.dve.tensor_scalar(out=Qf[:, :], in0=Qf[:, :], scalar1=SCOL[:, 1:2], scalar2=None, op0=ALU.mult)
                c.dve.tensor_tensor_scan(out=GCB[:, :], data0=CS[:, :], data1=Qf[:, :], initial=0.0, op0=ALU.mult, op1=ALU.add)
                for part, dst, raw in ((0, Qf, X), (1, Kf, Vf), (2, Vf, X)):
                    if part == 2:
                        self.load_seq_rows(X, self.ZT, 1888 + 2 * 256 + hp * 128, 128, b)
                    self.conv4(TT, raw, l, C_GCW2 + (part * 2 + hp) * 4, 128)
                    c.act.activation(out=dst[:, :], in_=TT[:, :], func=AF.Silu)
                    if part < 2:
                        c.act.activation(out=TT[:, :], in_=dst[:, :], func=AF.Square)
                        for j in range(4):
                            sl = slice(j * 512, (j + 1) * 512)
                            p = self.ps()
                            c.pe.matmul(out=p[:, :], lhsT=b64, rhs=TT[:, sl], start=True, stop=True)
                            c.act.activation(out=X[:, sl], in_=p[:, :], func=AF.Ln, bias=EPS)
                        c.act.activation(out=X[:, :], in_=X[:, :], func=AF.Exp, scale=-0.5)
                        if part == 0:
                            c.dve.scalar_tensor_tensor(out=dst[:, :], in0=dst[:, :], scalar=0.125, in1=X[:, :], op0=ALU.mult, op1=ALU.mult)
                        else:
                            c.dve.tensor_tensor(out=dst[:, :], in0=dst[:, :], in1=X[:, :], op=ALU.mult)
                c.act.activation(out=Kb[:, :], in_=Kf[:, :], func=AF.Copy)
                c.act.activation(out=Qb[:, :], in_=Qf[:, :], func=AF.Copy)
                c.act.activation(out=TT[:, :], in_=GCB[:, :], func=AF.Exp)
                c.pool.tensor_tensor(out=X[:, :], in0=BETB[:, :], in1=TT[:, :], op=ALU.mult)
                c.dve.tensor_tensor(out=KBG[ip][:, :], in0=Kf[:, :], in1=X[:, :], op=ALU.mult)
                c.dve.tensor_tensor(out=QD[ip][:, :], in0=Qf[:, :], in1=TT[:, :], op=ALU.mult)
                c.pool.tensor_copy(out=EGL[:, ip, :], in_=TT.v(TT.t.rearrange("p (n i) -> p n i", i=64)[:, :, 63]))
                for g in range(4):
                    cs = lambda s_: slice((g * 8 + s_) * 64, (g * 8 + s_) * 64 + 64)
                    p = self.ps()
                    tr8(p, GCB, cs)
                    c.dve.tensor_copy(out=GCc[:, g * 8:g * 8 + 8], in_=p.v(p.t[:, :].rearrange("p (s i) -> p s i", s=8)[:, :, 0]))
                    p = self.ps()
                    tr8(p, BETB, cs)
                    c.dve.tensor_copy(out=BTc[:, g * 8:g * 8 + 8], in_=p.v(p.t[:, :].rearrange("p (s i) -> p s i", s=8)[:, :, 0]))
                for g in range(4):
                    tk = slice(g * 512, (g + 1) * 512)
                    ns = slice(g * 8, g * 8 + 8)
                    cs = lambda s_: slice((g * 8 + s_) * 64, (g * 8 + s_) * 64 + 64)
                    gcc_b = GCc.v(GCc.t[:, ns].unsqueeze(2).to_broadcast([128, 8, 64]))
                    btc_b = BTc.v(BTc.t[:, ns].unsqueeze(2).to_broadcast([128, 8, 64]))
                    pG = self.ps()
                    mm8(pG, lambda pb, s_: Kb[pb:pb + 64, cs(s_)], lambda pb, s_: Kb[pb:pb + 64, cs(s_)])
                    pQ = self.ps()
                    mm8(pQ, lambda pb, s_: Kb[pb:pb + 64, cs(s_)], lambda pb, s_: Qb[pb:pb + 64, cs(s_)])
                    c.dve.tensor_tensor(out=D1[:, :, :], in0=v3(GCB, tk), in1=gcc_b, op=ALU.subtract)
                    c.pool.tensor_tensor(out=DX[:, :, :], in0=D1[:, :, :], in1=cbc(K_NMGE), op=ALU.add)
                    c.act.activation(out=EUi[:, :, :], in_=DX[:, :, :], func=AF.Exp)
                    c.pool.tensor_tensor(out=DX[:, :, :], in0=D1[:, :, :], in1=cbc(K_NMGT), op=ALU.add)
                    c.act.activation(out=EUs[:, :, :], in_=DX[:, :, :], func=AF.Exp)
                    c.dve.scalar_tensor_tensor(out=DX[:, :, :], in0=D1[:, :, :], scalar=-1.0, in1=cbc(K_NMLT), op0=ALU.mult, op1=ALU.add)
                    c.act.activation(out=EL[:, :, :], in_=DX[:, :, :], func=AF.Exp)
                    c.dve.tensor_tensor(out=ATT[ip][:, ns, :], in0=p3(pQ), in1=EUi[:, :, :], op=ALU.mult)
                    c.pool.tensor_tensor(out=TG[:, :, :], in0=v3(BETB, tk), in1=EUs[:, :, :], op=ALU.mult)
                    c.dve.tensor_tensor(out=U_[:, :, :], in0=p3(pG), in1=TG[:, :, :], op=ALU.mult)
                    c.pool.tensor_tensor(out=TG[:, :, :], in0=EL[:, :, :], in1=btc_b, op=ALU.mult)
                    c.dve.tensor_tensor(out=L_[:, :, :], in0=p3(pG), in1=TG[:, :, :], op=ALU.mult)
                    pK = self.ps()
                    tr8(pK, Kf, cs)
                    c.dve.tensor_tensor(out=KD[ip][:, ns, :], in0=p3(pK),
                                        in1=EUi.v(EUi.t[:, :, 63:64].to_broadcast([128, 8, 64])), op=ALU.mult)
                    pV = self.ps()
                    tr8(pV, Vf, cs)
                    c.dve.tensor_tensor(out=VB[:, ip, ns, :], in0=p3(pV), in1=btc_b, op=ALU.mult)
                    c.dve.scalar_tensor_tensor(out=R1[:, :, :], in0=U_[:, :, :], scalar=-1.0, in1=cbc(K_ID2), op0=ALU.mult, op1=ALU.add)
                    c.pool.tensor_tensor(out=Lof[:, :, :], in0=L_[:, :, :], in1=cbc(K_OFFL), op=ALU.mult)
                    P_, Q_ = U_, L_
                    for lev in range(4):
                        Pn, Qn = (Pa, Qa) if lev % 2 == 0 else (Pb, Qb2)
                        pP = self.ps()
                        mm8(pP, lambda pb, s_: Q_[pb:pb + 64, s_, :], lambda pb, s_: P_[pb:pb + 64, s_, :])
                        pQ2 = self.ps()
                        mm8(pQ2, lambda pb, s_: P_[pb:pb + 64, s_, :], lambda pb, s_: Q_[pb:pb + 64, s_, :])
                        c.act.activation(out=Pn[:, :, :], in_=p3(pP), func=AF.Copy)
                        c.dve.tensor_copy(out=Qn[:, :, :], in_=p3(pQ2))
                        pR = self.ps()
                        mm8(pR, lambda pb, s_: Qn[pb:pb + 64, s_, :], lambda pb, s_: R1[pb:pb + 64, s_, :])
                        c.dve.tensor_tensor(out=R1[:, :, :], in0=p3(pR), in1=R1[:, :, :], op=ALU.add)
                        P_, Q_ = Pn, Qn
                    c.pool.tensor_tensor(out=Rd[:, :, :], in0=R1[:, :, :], in1=cbc(K_DIAG), op=ALU.mult)
                    pT2 = self.ps()
                    mm8(pT2, lambda pb, s_: Rd[pb:pb + 64, s_, :], lambda pb, s_: idb(pb))
                    c.act.activation(out=Rd2[:, :, :], in_=p3(pT2), func=AF.Copy)
                    pY = self.ps()
                    mm8(pY, lambda pb, s_: Lof[pb:pb + 64, s_, :], lambda pb, s_: Rd[pb:pb + 64, s_, :])
                    c.act.activation(out=Ysb[:, :, :], in_=p3(pY), func=AF.Copy)
                    pX = self.ps()
                    mm8(pX, lambda pb, s_: Rd2[pb:pb + 64, s_, :], lambda pb, s_: Ysb[pb:pb + 64, s_, :])
                    c.dve.tensor_tensor(out=TMt[ip][:, ns, :], in0=Rd[:, :, :], in1=p3(pX), op=ALU.subtract)
        c.pool.memset(ap=St[:, :, :], constant=0.0)
        c.pool.memset(ap=Sb[:, :, :], constant=0.0)
        pO = self.psl[0:4]
        pP1, pVN, pSN, p7 = self.psl[4], self.psl[5], self.psl[6], self.psl[7]
        p4 = lambda p: p.v(p.t[:, 0:64 * NPR].rearrange("p (h e) -> p h e", h=NPR))
        for n in range(32):
            s_, g = n % 8, n // 8
            cs = slice(n * 64, n * 64 + 64)
            if s_ == 0:
                for ip in range(NPR):
                    c.sp.dma_start(out=GT[:, ip, :], in_=self.ZT.v(2656 + (ip % 2) * 128, 128, (ip // 2) * 4 + g))
            for ip in range(NPR):
                for pb in (0, 64):
                    c.pe.matmul(out=pP1[pb:pb + 64, ip * 64:ip * 64 + 64], lhsT=KBG[ip][pb:pb + 64, cs], rhs=Sb[pb:pb + 64, ip, :], start=True, stop=True)
            c.dve.tensor_tensor(out=RH[:, :, :], in0=VB[:, :, n, :], in1=p4(pP1), op=ALU.subtract)
            for ip in range(NPR):
                for pb in (0, 64):
                    c.pe.matmul(out=pVN[pb:pb + 64, ip * 64:ip * 64 + 64], lhsT=TMt[ip][pb:pb + 64, n, :], rhs=RH[pb:pb + 64, ip, :], start=True, stop=True)
            c.act.activation(out=VNb[:, :, :], in_=p4(pVN), func=AF.Copy)
            for ip in range(NPR):
                for pb in (0, 64):
                    c.pe.matmul(out=pO[ip][pb:pb + 64, s_ * 64:s_ * 64 + 64], lhsT=Sb[pb:pb + 64, ip, :], rhs=QD[ip][pb:pb + 64, cs], start=True, stop=False)
                    c.pe.matmul(out=pO[ip][pb:pb + 64, s_ * 64:s_ * 64 + 64], lhsT=VNb[pb:pb + 64, ip, :], rhs=ATT[ip][pb:pb + 64, n, :], start=False, stop=True)
            for ip in range(NPR):
                for pb in (0, 64):
                    c.pe.matmul(out=pSN[pb:pb + 64, ip * 64:ip * 64 + 64], lhsT=KD[ip][pb:pb + 64, n, :], rhs=VNb[pb:pb + 64, ip, :], start=True, stop=True)
            for ip in range(NPR):
                c.dve.scalar_tensor_tensor(out=St[:, ip, :], in0=St[:, ip, :], scalar=EGL[:, ip, n:n + 1],
                                           in1=pSN[:, ip * 64:ip * 64 + 64], op0=ALU.mult, op1=ALU.add)
            c.pool.tensor_copy(out=Sb[:, :, :], in_=St[:, :, :])
            if s_ == 7:
                for ip in range(NPR):
                    c.act.activation(out=OS[:, :], in_=pO[ip][:, :], func=AF.Copy)
                    c.act.activation(out=SQ_[:, :], in_=OS[:, :], func=AF.Square)
                    c.pe.matmul(out=p7[:, :], lhsT=b64, rhs=SQ_[:, :], start=True, stop=True)
                    c.act.activation(out=RSD[:, :], in_=p7[:, :], func=AF.Ln, scale=1.0 / 64, bias=EPS)
                    c.act.activation(out=RSD[:, :], in_=RSD[:, :], func=AF.Exp, scale=-0.5)
                    c.act.activation(out=SG[:, :], in_=GT[:, ip, :], func=AF.Silu)
                    c.dve.tensor_tensor(out=OS[:, :], in0=OS[:, :], in1=RSD[:, :], op=ALU.mult)
                    yo = YO[ip % 2]
                    c.dve.scalar_tensor_tensor(out=yo[:, :], in0=OS[:, :], scalar=self.col(l, C_GNORM2), in1=SG[:, :],
                                               op0=ALU.mult, op1=ALU.mult)
                    c.pool.dma_start(out=self.YT.v(768 + (ip % 2) * 128, 128, (ip // 2) * 4 + g), in_=yo[:, :])


def build(debug=(), upto=None):
    nc = bass.Bass("TRN2", target_bir_lowering=False)
    with ExitStack() as st:
        k = K(nc, st, debug)
        k.phase_setup()
        for l in range(DEPTH):
            k.phase_proj(l, k.XIN if l == 0 else k.XT)
            if upto == "proj":
                break
            k.phase_ret2(l)
            if upto == "ret":
                break
            k.phase_mla(l)
            if upto == "mla":
                break
            k.phase_gdn2(l)
            if upto == "gdn":
                break
            k.phase_xattn(l, k.XIN if l == 0 else k.XT)
            if upto == "xattn":
                break
            k.phase_mlp(l, final=(l == DEPTH - 1))
            if upto == "mlp":
                break
        k.c.finish()
        print("instr counts:", {n: E.total + E.cnt for n, E in k.c.engs.items()}, "dmas", k.c.dnext, "epochs", k.c.epoch)
    return nc


WEIGHT_KEYS = ["w_in", "lru_wa", "lru_wx", "mla_w_uq", "mla_w_ukv", "w_out", "xa_wq", "xa_wkv", "xa_wo", "mlp_w1", "mlp_w2"]


def prep_inputs(inp, cores=range(NCORES)):
    cst = host_constants()
    cols = host_cols(inp)
    shared = {k: np.ascontiguousarray(inp[k], dtype=np.float32) for k in WEIGHT_KEYS}
    shared["cst"] = cst
    shared["cols"] = cols
    maps = []
    for ci in cores:
        m = dict(shared)
        xs = np.asarray(inp["x"][ci * BL:(ci + 1) * BL], dtype=np.float32).reshape(T, DM)
        m["xT"] = np.ascontiguousarray(xs.T)
        ms = np.asarray(inp["mem"][ci * BL:(ci + 1) * BL], dtype=np.float32).reshape(TM_, DM)
        m["memT"] = np.ascontiguousarray(ms.T)
        m["pos"] = np.ascontiguousarray(np.asarray(inp["positions"][ci * BL:(ci + 1) * BL], dtype=np.int32).reshape(1, T))
        maps.append(m)
    return maps


def kernel(**inputs):
    nc = build()
    maps = prep_inputs(inputs)
    res = run_bass_kernel_spmd(nc, maps, core_ids=list(range(NCORES)))
    outs = [np.asarray(r["outT"]).T.reshape(BL, S, DM) for r in res.results]
    return np.ascontiguousarray(np.concatenate(outs, axis=0).astype(np.float32))
```
